# Optimizing a Trainium2 kernel written in Bass

```python
import jax, jax.numpy as jnp
from jax import lax
import numpy as np

D_MODEL = 1024
BATCH = 2
SEQ = 8192
DEPTH = 4

N_META = 16
CHUNK = 64
EPS = 1e-6

HG_HEADS = 4
HG_DIM = 128
HG_WIDTH = HG_HEADS * HG_DIM
RET_HEADS = 4
RET_DK = 32
RET_DV = 64
RET_QK_WIDTH = RET_HEADS * RET_DK
RET_V_WIDTH = RET_HEADS * RET_DV
ROPE_BASE = 10000.0
LRU_WIDTH = 256
LRU_BLOCKS = 4
LRU_BLOCK = LRU_WIDTH // LRU_BLOCKS
LRU_CONV = 4
LRU_C = 8.0
D_MIX = HG_WIDTH + RET_V_WIDTH + LRU_WIDTH
IN_SIZES = (HG_WIDTH, HG_WIDTH, HG_WIDTH, HG_WIDTH,
            RET_QK_WIDTH, RET_QK_WIDTH, RET_V_WIDTH, RET_V_WIDTH,
            LRU_WIDTH, LRU_WIDTH)
IN_COLS = sum(IN_SIZES)
IN_SPLITS = tuple(int(s) for s in np.cumsum(IN_SIZES)[:-1])
D_FF = 2816
FFN_CONV = 3

kernel_name = "hymba_hgrn2_retention_rglru_convffn"


def rmsnorm(x, g):
    xf = x.astype(jnp.float32)
    y = xf * lax.rsqrt(jnp.mean(xf * xf, axis=-1, keepdims=True) + EPS)
    return (y * g.astype(jnp.float32)).astype(x.dtype)


def head_groupnorm(o, g):
    of = o.astype(jnp.float32)
    mu = jnp.mean(of, axis=-1, keepdims=True)
    var = jnp.mean(jnp.square(of - mu), axis=-1, keepdims=True)
    return ((of - mu) * lax.rsqrt(var + EPS) * g.astype(jnp.float32)).astype(o.dtype)


def causal_dwconv(x, w, b):
    width, ch = w.shape
    y = lax.conv_general_dilated(
        x, w[:, None, :].astype(x.dtype), window_strides=(1,), padding=[(width - 1, 0)],
        dimension_numbers=('NWC', 'WIO', 'NWC'), feature_group_count=ch)
    return y + b.astype(x.dtype)


def split_heads(a, n):
    bsz, t, _ = a.shape
    return a.reshape(bsz, t, n, -1).transpose(0, 2, 1, 3)


def merge_heads(a):
    bsz, h, t, d = a.shape
    return a.transpose(0, 2, 1, 3).reshape(bsz, t, h * d)


def to_chunks(a, pad):
    bsz, h, _, d = a.shape
    a = jnp.pad(a, ((0, 0), (0, 0), (pad, 0), (0, 0)))
    return a.reshape(bsz, h, -1, CHUNK, d).transpose(2, 0, 1, 3, 4)


def from_chunks(o, pad):
    n, bsz, h, c, d = o.shape
    return o.transpose(1, 2, 0, 3, 4).reshape(bsz, h, n * c, d)[:, :, pad:]


def hgrn2_chunked(q, k, v, log_f):
    bsz, h, t, dk = q.shape
    dv = v.shape[-1]
    pad = (-t) % CHUNK
    qc, kc, vc, gc = (to_chunks(a, pad) for a in (q, k, v, log_f))
    causal = jnp.tril(jnp.ones((CHUNK, CHUNK), dtype=bool))[:, :, None]

    def step(state, inp):
        qi, ki, vi, gi = inp
        b = jnp.cumsum(gi.astype(jnp.float32), axis=-2)
        diff = b[:, :, :, None, :] - b[:, :, None, :, :]
        decay = jnp.exp(jnp.where(causal, diff, -jnp.inf))
        scores = jnp.einsum('bhtd,bhtsd,bhsd->bhts', qi, decay, ki)
        o = jnp.einsum('bhts,bhse->bhte', scores, vi) + \
            jnp.einsum('bhtd,bhde->bhte', qi * jnp.exp(b), state)
        b_last = b[:, :, -1:, :]
        state = jnp.exp(b_last[:, :, 0, :])[..., None] * state + \
            jnp.einsum('bhsd,bhse->bhde', ki * jnp.exp(b_last - b), vi)
        return state, o

    s0 = jnp.zeros((bsz, h, dk, dv), jnp.float32)
    _, o = lax.scan(step, s0, (qc, kc, vc, gc))
    return from_chunks(o, pad)


def retention_chunked(q, k, v, log_gamma):
    bsz, h, t, dk = q.shape
    dv = v.shape[-1]
    pad = (-t) % CHUNK
    qc, kc, vc = (to_chunks(a, pad).transpose(1, 2, 0, 3, 4) for a in (q, k, v))
    idx = jnp.arange(CHUNK, dtype=jnp.float32)
    lg = log_gamma[:, None]
    rel = idx[:, None] - idx[None, :]
    dmat = jnp.where(rel >= 0, jnp.exp(lg[:, :, None] * jnp.maximum(rel, 0.0)), 0.0)
    scores = jnp.einsum('bhntd,bhnsd->bhnts', qc, kc) * dmat[None, :, None]
    intra = jnp.einsum('bhnts,bhnse->bhnte', scores, vc)
    w_state = jnp.exp(lg * (CHUNK - 1 - idx))
    kv = jnp.einsum('bhnsd,bhnse->nbhde', kc * w_state[None, :, None, :, None], vc)
    chunk_decay = jnp.exp(log_gamma * CHUNK)[None, :, None, None]

    def step(state, kv_n):
        return chunk_decay * state + kv_n, state

    _, r_prev = lax.scan(step, jnp.zeros((bsz, h, dk, dv), jnp.float32), kv)
    w_q = jnp.exp(lg * (idx + 1.0))
    inter = jnp.einsum('bhntd,nbhde->bhnte', qc * w_q[None, :, None, :, None], r_prev)
    o = (intra + inter).transpose(2, 0, 1, 3, 4)
    return from_chunks(o, pad)


def rope(x, cos, sin):
    x1, x2 = jnp.split(x, 2, axis=-1)
    return jnp.concatenate([x1 * cos - x2 * sin, x1 * sin + x2 * cos], axis=-1)


def rg_lru(x, gate_a_w, gate_a_b, gate_i_w, gate_i_b, lam):
    bsz, t, w = x.shape
    xb = x.reshape(bsz, t, LRU_BLOCKS, LRU_BLOCK)
    r = jax.nn.sigmoid(jnp.einsum('btnd,nde->btne', xb, gate_a_w).reshape(bsz, t, w) + gate_a_b)
    i = jax.nn.sigmoid(jnp.einsum('btnd,nde->btne', xb, gate_i_w).reshape(bsz, t, w) + gate_i_b)
    log_a = -LRU_C * r.astype(jnp.float32) * jax.nn.softplus(-lam.astype(jnp.float32))
    a = jnp.exp(log_a)
    u = jnp.sqrt(-jnp.expm1(2.0 * log_a)) * (i * x)

    def combine(c1, c2):
        a1, b1 = c1
        a2, b2 = c2
        return a1 * a2, a2 * b1 + b2

    _, hseq = lax.associative_scan(combine, (a, u), axis=1)
    return hseq.astype(x.dtype)


def hybrid_mixer(h, w_in, lb, hg_norm_g, ret_norm_g, lru_conv_w, lru_conv_b,
                 gate_a_w, gate_a_b, gate_i_w, gate_i_b, lru_lambda, w_out, cos, sin):
    proj = jnp.einsum('btd,dc->btc', h, w_in)
    hq, hf, hi, hg, rq, rk, rv, rg, lx, ly = jnp.split(proj, IN_SPLITS, axis=-1)

    z = hf.astype(jnp.float32)
    log_f = jnp.logaddexp(jnp.log(lb), jnp.log1p(-lb) + jax.nn.log_sigmoid(z))
    k_hg = (1.0 - lb) * jax.nn.sigmoid(-z)
    q_hg = jax.nn.silu(hq) * HG_DIM ** -0.5
    o_hg = hgrn2_chunked(split_heads(q_hg, HG_HEADS), split_heads(k_hg, HG_HEADS),
                         split_heads(hi, HG_HEADS), split_heads(log_f, HG_HEADS))
    o_hg = merge_heads(rmsnorm(o_hg, hg_norm_g)) * jax.nn.silu(hg)

    log_gamma = jnp.log1p(-jnp.exp2(-5.0 - jnp.arange(RET_HEADS, dtype=jnp.float32)))
    q_r = rope(split_heads(rq, RET_HEADS), cos, sin)
    k_r = rope(split_heads(rk, RET_HEADS), cos, sin) * RET_DK ** -0.5
    o_ret = retention_chunked(q_r, k_r, split_heads(rv, RET_HEADS), log_gamma)
    o_ret = merge_heads(head_groupnorm(o_ret, ret_norm_g)) * jax.nn.silu(rg)

    xc = causal_dwconv(lx, lru_conv_w, lru_conv_b)
    o_lru = rg_lru(xc, gate_a_w, gate_a_b, gate_i_w, gate_i_b, lru_lambda) * jax.nn.gelu(ly)

    merged = jnp.concatenate([o_hg, o_ret, o_lru], axis=-1)
    return jnp.einsum('btc,cd->btd', merged, w_out)


def conv_glu_ffn(h, w_up, conv_w, conv_b, w_down):
    u = causal_dwconv(jnp.einsum('btd,df->btf', h, w_up), conv_w, conv_b)
    gate, val = jnp.split(u, 2, axis=-1)
    return jnp.einsum('btf,fd->btd', jax.nn.silu(gate) * val, w_down)


def setup_inputs(seed: int = 0) -> dict:
    key = jax.random.key(seed)
    ks = jax.random.split(key, 24)
    f32 = jnp.float32

    def nrm(k, shape, scale):
        return scale * jax.random.normal(k, shape, f32)

    a_c = jax.random.uniform(ks[10], (DEPTH, LRU_WIDTH), f32, 0.9, 0.999)
    s = a_c ** (1.0 / LRU_C)
    return {
        "x": nrm(ks[0], (BATCH, SEQ, D_MODEL), 1.0),
        "meta_tokens": nrm(ks[1], (N_META, D_MODEL), 1.0),
        "norm_mix_g": 1.0 + nrm(ks[2], (DEPTH, D_MODEL), 0.02),
        "w_in": nrm(ks[3], (DEPTH, D_MODEL, IN_COLS), D_MODEL ** -0.5),
        "hg_lb_logits": nrm(ks[4], (DEPTH, HG_WIDTH), 0.1),
        "hg_norm_g": 1.0 + nrm(ks[5], (DEPTH, HG_DIM), 0.02),
        "ret_norm_g": 1.0 + nrm(ks[6], (DEPTH, RET_DV), 0.02),
        "lru_conv_w": nrm(ks[7], (DEPTH, LRU_CONV, LRU_WIDTH), LRU_CONV ** -0.5),
        "lru_conv_b": nrm(ks[8], (DEPTH, LRU_WIDTH), 0.01),
        "lru_gate_a_w": nrm(ks[9], (DEPTH, LRU_BLOCKS, LRU_BLOCK, LRU_BLOCK), LRU_BLOCK ** -0.5),
        "lru_gate_a_b": nrm(ks[11], (DEPTH, LRU_WIDTH), 0.01),
        "lru_gate_i_w": nrm(ks[12], (DEPTH, LRU_BLOCKS, LRU_BLOCK, LRU_BLOCK), LRU_BLOCK ** -0.5),
        "lru_gate_i_b": nrm(ks[13], (DEPTH, LRU_WIDTH), 0.01),
        "lru_lambda": jnp.log(s) - jnp.log1p(-s),
        "w_out": nrm(ks[14], (DEPTH, D_MIX, D_MODEL), D_MIX ** -0.5),
        "norm_ffn_g": 1.0 + nrm(ks[15], (DEPTH, D_MODEL), 0.02),
        "ffn_w_up": nrm(ks[16], (DEPTH, D_MODEL, 2 * D_FF), D_MODEL ** -0.5),
        "ffn_conv_w": nrm(ks[17], (DEPTH, FFN_CONV, 2 * D_FF), FFN_CONV ** -0.5),
        "ffn_conv_b": nrm(ks[18], (DEPTH, 2 * D_FF), 0.01),
        "ffn_w_down": nrm(ks[19], (DEPTH, D_FF, D_MODEL), D_FF ** -0.5),
        "final_norm_g": 1.0 + nrm(ks[20], (D_MODEL,), 0.02),
    }


def reference(x, meta_tokens, norm_mix_g, w_in, hg_lb_logits, hg_norm_g, ret_norm_g,
              lru_conv_w, lru_conv_b, lru_gate_a_w, lru_gate_a_b, lru_gate_i_w, lru_gate_i_b,
              lru_lambda, w_out, norm_ffn_g, ffn_w_up, ffn_conv_w, ffn_conv_b, ffn_w_down,
              final_norm_g):
    bsz = x.shape[0]
    meta = jnp.broadcast_to(meta_tokens[None].astype(x.dtype), (bsz, N_META, x.shape[-1]))
    h = jnp.concatenate([meta, x], axis=1)
    t = h.shape[1]

    pos = jnp.arange(t, dtype=jnp.float32)
    theta = ROPE_BASE ** (-jnp.linspace(0.0, 1.0, RET_DK // 2, dtype=jnp.float32))
    ang = pos[:, None] * theta[None, :]
    cos, sin = jnp.cos(ang), jnp.sin(ang)

    lb_all = jnp.cumsum(jax.nn.softmax(hg_lb_logits.astype(jnp.float32), axis=0), axis=0)
    lb_all = lb_all - lb_all[0:1]

    for l in range(DEPTH):
        h = h + hybrid_mixer(rmsnorm(h, norm_mix_g[l]), w_in[l], lb_all[l], hg_norm_g[l],
                             ret_norm_g[l], lru_conv_w[l], lru_conv_b[l], lru_gate_a_w[l],
                             lru_gate_a_b[l], lru_gate_i_w[l], lru_gate_i_b[l], lru_lambda[l],
                             w_out[l], cos, sin).astype(h.dtype)
        h = h + conv_glu_ffn(rmsnorm(h, norm_ffn_g[l]), ffn_w_up[l], ffn_conv_w[l],
                             ffn_conv_b[l], ffn_w_down[l]).astype(h.dtype)

    return rmsnorm(h, final_norm_g)[:, N_META:]
```

```python
import numpy as np
import concourse.bass as bass
import concourse.mybir as mybir
from concourse.bass_utils import run_bass_kernel_spmd

F32 = mybir.dt.float32
BF16 = mybir.dt.bfloat16
AF = mybir.ActivationFunctionType
ALU = mybir.AluOpType

D = 1024
NL = 4
PRE = 16
NR = 2048
LT = PRE + NR
NSEG = 4
DFF = 2816
NJ = DFF // 128
EPS = 1e-6
WINX = 3584
PL = 214
NPP = NL * PL + 8
TILES = [(0, 16)] + [(16 + 512 * i, 512) for i in range(4)]
SUPERS = [[0, 1, 2], [3, 4]]
SUP_COLS = [(0, 1040), (1040, 1024)]
FFN_GROUPS = [(0, 8), (8, 16), (16, 22)]
SEM_MAX = 60000
_CACHE = {}


def chunks_of(W):
    if W <= 64:
        return [(0, W)]
    return [(i * 64, 64) for i in range(W // 64)]


class Prog:
    def __init__(self, nc):
        self.nc = nc
        self.ops = []
        self.label = ""

    def add(self, eng, fn, reads=(), writes=(), dma=False, cc=False):
        self.ops.append({"eng": eng, "fn": fn, "r": tuple(reads), "w": tuple(writes), "dma": dma or cc,
                         "inc": False, "deps": (), "cc": cc, "label": self.label})

    def emit(self):
        nc = self.nc
        ops = self.ops
        engobj = {"pe": nc.tensor, "act": nc.scalar, "dve": nc.vector, "pool": nc.gpsimd, "sp": nc.sync}
        lastw = {}
        readers = {}
        for i, op in enumerate(ops):
            deps = set()
            raw = set()
            for r in op["r"]:
                if r in lastw:
                    deps.add(lastw[r])
                    raw.add(lastw[r])
            for w in op["w"]:
                if w in lastw:
                    deps.add(lastw[w])
                deps.update(readers.get(w, ()))
            need = []
            for j in deps:
                oj = ops[j]
                if (not oj["dma"]) and (not op["dma"]) and oj["eng"] == op["eng"]:
                    if not (j in raw and op["eng"] != "pe"):
                        continue
                need.append(j)
            op["deps"] = need
            for j in need:
                ops[j]["inc"] = True
            for r in op["r"]:
                readers.setdefault(r, []).append(i)
            for w in op["w"]:
                lastw[w] = i
                readers[w] = []
        cnt = {}
        esems = {}
        NDS = 10
        dsems = {}
        dslot_n = {}
        dslot_val = {}
        for i, op in enumerate(ops):
            if not op["inc"]:
                continue
            e = op["eng"]
            if op["cc"]:
                op["prev"] = (None, 0)
                op["sem"] = (nc.alloc_semaphore(f"cc_{i}"), 1)
            elif op["dma"]:
                if e not in dsems:
                    dsems[e] = [nc.alloc_semaphore(f"dq_{e}_{k}") for k in range(NDS)]
                    dslot_n[e] = 0
                    dslot_val[e] = [0] * NDS
                slot = dslot_n[e] % NDS
                dslot_n[e] += 1
                op["prev"] = (dsems[e][slot], dslot_val[e][slot])
                dslot_val[e][slot] += 16
                assert dslot_val[e][slot] < 65000
                op["sem"] = (dsems[e][slot], dslot_val[e][slot])
            else:
                c = cnt.get(e, 0)
                ep = c // SEM_MAX
                lst = esems.setdefault(e, [])
                while len(lst) <= ep:
                    lst.append(nc.alloc_semaphore(f"es_{e}_{len(lst)}"))
                op["sem"] = (lst[ep], c % SEM_MAX + 1)
                cnt[e] = c + 1
        waited = {}
        nwait = 0
        for i, op in enumerate(ops):
            e = op["eng"]
            E = engobj[e]
            best = {}
            for j in op["deps"]:
                s, v = ops[j]["sem"]
                k = id(s)
                if k not in best or best[k][1] < v:
                    best[k] = (s, v)
            if op["dma"] and op["inc"] and op["prev"][1] > 0:
                s, v = op["prev"]
                k = id(s)
                if k not in best or best[k][1] < v:
                    best[k] = (s, v)
            for k, (s, v) in best.items():
                if waited.get((e, k), 0) >= v:
                    continue
                E.wait_ge(s, v)
                waited[(e, k)] = v
                nwait += 1
            if op["fn"] is None:
                continue
            ins = op["fn"](E)
            if op["inc"]:
                s, v = op["sem"]
                ins.then_inc(s, 16 if (op["dma"] and not op["cc"]) else 1)
        return len(ops), nwait


def _gammas():
    return (1.0 - np.exp2(-5.0 - np.arange(4, dtype=np.float64)))


def make_consts():
    g = _gammas()
    cf = {}
    cf["ident"] = np.eye(128, dtype=np.float32)
    cm = np.ones((128, 512), np.float32)
    cm[:, ::64] = 0.0
    cf["cmask"] = cm
    t = np.arange(64)
    wq = np.zeros((128, 512), np.float64)
    for p in range(128):
        wq[p] = np.tile(g[p // 32] ** (t + 1.0), 8)
    cf["wq"] = wq.astype(np.float32)
    gg = np.zeros((128, 2), np.float64)
    for p in range(128):
        gg[p, 0] = g[p // 32] ** 64
        gg[p, 1] = g[p // 32] ** 16
    cf["g6416"] = gg.astype(np.float32)
    g2 = np.zeros((128, 1), np.float64)
    for p in range(128):
        g2[p, 0] = g[p // 32] ** 2048
    cf["g2048"] = g2.astype(np.float32)
    cb = {}
    cb["identb"] = np.eye(128, dtype=np.float32)
    cb["ones1024"] = np.full((128, 128), 1.0 / 1024, np.float32)
    cb["ones128"] = np.full((128, 128), 1.0 / 128, np.float32)
    bo = np.zeros((128, 128), np.float32)
    bo[:64, :64] = 1.0 / 64
    bo[64:, 64:] = 1.0 / 64
    cb["bones64"] = bo
    m = np.zeros((128, 512), np.float32)
    s = np.arange(64)[:, None]
    tt = np.arange(64)[None, :]
    m[:64] = np.tile((s <= tt).astype(np.float32), (1, 8))
    cb["maskT"] = m
    dm = np.zeros((128, 4 * 512), np.float64)
    for hd in range(4):
        blk = np.where(tt >= s, g[hd] ** np.maximum(tt - s, 0), 0.0) * 32 ** -0.5
        dm[:64, hd * 512:(hd + 1) * 512] = np.tile(blk, (1, 8))
    cb["dmatT"] = dm.astype(np.float32)
    ws = np.zeros((128, 1024), np.float64)
    ws16 = np.zeros((128, 128), np.float64)
    for hd in range(4):
        col = (g[hd] ** (63.0 - np.arange(64))) * 32 ** -0.5
        col16 = (g[hd] ** (15.0 - np.arange(16))) * 32 ** -0.5
        for c in range(8):
            ws[:64, c * 128 + hd * 32: c * 128 + hd * 32 + 32] = col[:, None]
        ws16[:16, hd * 32: hd * 32 + 32] = col16[:, None]
    cb["wst"] = ws.astype(np.float32)
    cb["wst16"] = ws16.astype(np.float32)
    return cf, cb


CF_ORDER = [("ident", 128), ("cmask", 512), ("wq", 512), ("g6416", 2), ("g2048", 1)]
CB_ORDER = [("identb", 128), ("ones1024", 128), ("ones128", 128), ("bones64", 128), ("maskT", 512),
            ("dmatT", 2048), ("wst", 1024), ("wst16", 128)]


def _offsets(order):
    o = {}
    c = 0
    for n, w in order:
        o[n] = (c, w)
        c += w
    return o, c


CF_OFF, NCF = _offsets(CF_ORDER)
CB_OFF, NCB = _offsets(CB_ORDER)


def rope_tables(pos):
    theta = 10000.0 ** (-np.linspace(0.0, 1.0, 16, dtype=np.float32))
    ang = (pos.astype(np.float32)[:, None] * theta[None, :]).astype(np.float32)
    cos = np.cos(ang).astype(np.float32)
    sin = np.sin(ang).astype(np.float32)
    out = np.zeros((2, 128, pos.shape[0]), np.float32)
    for hd in range(4):
        for dd in range(32):
            p = hd * 32 + dd
            out[0, p] = cos[:, dd % 16]
            out[1, p] = (-sin[:, dd] if dd < 16 else sin[:, dd - 16])
    return out


def make_wqseg():
    g = _gammas()
    out = np.zeros((128, LT), np.float64)
    t = np.arange(NR, dtype=np.float64)
    for p in range(128):
        out[p, PRE:] = g[p // 32] ** (t + 1.0)
    return out.astype(np.float32)


_WQSEG = make_wqseg()


def winx_perm():
    hq, hf, hi, hg, rq, rk, rv, rg, lx, ly = 0, 512, 1024, 1536, 2048, 2176, 2304, 2560, 2816, 3072
    cols = []
    for hh in range(4):
        r = np.arange(hh * 128, (hh + 1) * 128)
        cols += [hq + r, hg + r, hf + r, hi + r]
    sw = np.concatenate([np.concatenate([np.arange(h * 32 + 16, h * 32 + 32), np.arange(h * 32, h * 32 + 16)])
                         for h in range(4)])
    cols += [rq + np.arange(128), rq + sw, rg + np.arange(256), rk + np.arange(128), rk + sw, rv + np.arange(256)]
    cols += [lx + np.arange(256), ly + np.arange(256)]
    p = np.concatenate(cols)
    assert p.shape[0] == WINX
    return p


def pack_params(inp):
    pp = np.zeros((128, NPP), np.float32)
    for l in range(NL):
        b = l * PL
        pp[:, b + 0:b + 8] = inp["norm_mix_g"][l].reshape(8, 128).T
        pp[:, b + 8:b + 16] = inp["norm_ffn_g"][l].reshape(8, 128).T
        pp[:, b + 16:b + 20] = inp["hg_lb_logits"][l].reshape(4, 128).T
        pp[:, b + 20] = inp["hg_norm_g"][l]
        pp[:, b + 21] = np.tile(inp["ret_norm_g"][l], 2)
        cw = inp["lru_conv_w"][l]
        for j in range(2):
            for tap in range(4):
                pp[:, b + 22 + j * 4 + tap] = cw[tap, j * 128:(j + 1) * 128]
        pp[:, b + 30:b + 32] = inp["lru_conv_b"][l].reshape(2, 128).T
        pp[:, b + 32:b + 34] = inp["lru_gate_a_b"][l].reshape(2, 128).T
        pp[:, b + 34:b + 36] = inp["lru_gate_i_b"][l].reshape(2, 128).T
        pp[:, b + 36:b + 38] = inp["lru_lambda"][l].reshape(2, 128).T
        fw = inp["ffn_conv_w"][l]
        for c in range(44):
            for tap in range(3):
                pp[:, b + 38 + c * 3 + tap] = fw[tap, c * 128:(c + 1) * 128]
        pp[:, b + 170:b + 214] = inp["ffn_conv_b"][l].reshape(44, 128).T
    pp[:, NL * PL:NL * PL + 8] = inp["final_norm_g"].reshape(8, 128).T
    return pp


def pack_lrug(inp):
    out = np.zeros((NL, 2, 2, 128, 128), np.float32)
    for l in range(NL):
        for gi, nm in enumerate(("lru_gate_a_w", "lru_gate_i_w")):
            w = inp[nm][l]
            for j in range(2):
                for bb in range(2):
                    out[l, gi, j, bb * 64:(bb + 1) * 64, bb * 64:(bb + 1) * 64] = w[j * 2 + bb]
    return out


def build_program(nlayers=NL, fused=False, dbg=None):
    nc = bass.Bass("TRN2", target_bir_lowering=False)
    P = Prog(nc)

    def din(name, shape):
        return nc.dram_tensor(name, shape, F32, kind="ExternalInput").ap()

    def dout(name, shape):
        return nc.dram_tensor(name, shape, F32, kind="ExternalOutput").ap()

    xs = din("xs", [LT, D])
    ppd = din("pp", [128, NPP])
    cstf = din("cstf", [128, NCF])
    cstb = din("cstb", [128, NCB])
    roped = din("rope", [2, 128, LT])
    wqsegd = din("wqseg", [128, LT])
    mskd = din("msk", [128, 9])
    winx = din("w_in_x", [NL, D, WINX])
    woutd = din("w_out", [NL, D, D])
    wupd = din("w_up", [NL, D, 2 * DFF])
    wdnd = din("w_down", [NL, DFF, D])
    lrugd = din("lrug", [NL * 4, 128, 128])
    outd = dout("out", [NR, D])
    NST = 584
    if not fused:
        st_in = din("st_in", [NL, 128, NST])
        halo_in = din("halo_in", [NL * 2, 128, 24])
        st_out = dout("st_out", [NL, 128, NST])
        halo_out = dout("halo_out", [NL * 2, 128, 24])
    else:
        xh_src = nc.dram_tensor("xh_src", [128, 24], F32, kind="Internal").ap()
        xh_dst = nc.dram_tensor("xh_dst", [4 * 128, 24], F32, kind="Internal").ap()
        st_src = nc.dram_tensor("st_src", [128, NST], F32, kind="Internal").ap()
        st_dst = nc.dram_tensor("st_dst", [4 * 128, NST], F32, kind="Internal").ap()
        lscr = nc.dram_tensor("lscr", [6, 128, LT], F32, kind="Internal").ap()
        hscr_o = nc.dram_tensor("hscr_o", [4, 128, LT], F32, kind="Internal").ap()
        hscr_q = nc.dram_tensor("hscr_q", [4, 128, LT], BF16, kind="Internal").ap()
        hscr_g = nc.dram_tensor("hscr_g", [4, 128, LT], BF16, kind="Internal").ap()
        rscr_o = nc.dram_tensor("rscr_o", [2, 128, LT], F32, kind="Internal").ap()
        rscr_q = nc.dram_tensor("rscr_q", [128, LT], BF16, kind="Internal").ap()
        rscr_g = nc.dram_tensor("rscr_g", [2, 128, LT], BF16, kind="Internal").ap()
    dbgd = None
    if dbg is not None:
        dbgd = dout("dbg", [128, 8, LT])

    def sb(name, shape, dt=F32):
        return nc.alloc_sbuf_tensor(name, shape, dt)

    h = sb("h", [128, 8, LT])
    hn = sb("hn", [128, 8, 1040], BF16)
    mg = sb("mg", [128, 8, 1040], BF16)
    pp = sb("pp_sb", [128, NPP])
    der = sb("der", [128, 4, 4, 3])
    cng = sb("cng", [128, 4, 2, 2])
    cf = sb("cf", [128, NCF])
    cbt = sb("cbt", [128, NCB], BF16)
    msk = sb("msk_sb", [128, 9])
    ropec = sb("ropec", [128, 512])
    ropes = sb("ropes", [128, 512])
    lrug = sb("lrug_sb", [128, 4, 128], BF16)
    NWU = 4
    wb = [sb(f"wb{i}", [128, 4096], BF16) for i in range(NWU)]
    NFS = 8
    Fs = [sb(f"F{i}", [128, 512]) for i in range(NFS)]
    NBS = 8
    Bs = [sb(f"B{i}", [128, 512], BF16) for i in range(NBS)]
    TM1 = sb("TM1", [64, 2048], BF16)
    TM2 = sb("TM2", [64, 1024], BF16)
    SBk = sb("SBk", [128, 1024], BF16)
    sq = sb("sq", [128, 4, 512], BF16)
    STT = sb("STT", [128, 584])
    Shg = [STT[:, i * 128:(i + 1) * 128] for i in range(4)]
    Dhg = STT[:, 512:516]
    Rst = STT[:, 516:580]
    hl = STT[:, 580:582]
    alog = STT[:, 582:584]
    rsum = sb("rsum", [128, 4])
    STIN = sb("STIN", [128, 584])
    lxs = [sb(f"lxs{j}", [128, 515]) for j in range(2)]
    ubg = sb("ubg", [128, 514])
    ubv = sb("ubv", [128, 514])
    uhalo = sb("uhalo", [128, 44, 2])
    stin_hg = STIN[:, 0:512]
    stin_ret = STIN[:, 516:580]
    stin_lru = STIN[:, 580:582]
    hgat = sb("hgat", [128, 4, 24])
    halo_sb = sb("halo_sb", [128, 8, 3])
    halo_t = sb("halo_t", [128, 8, 3])
    xin = sb("xin", [128, 1024])
    osb = xin
    tiny = sb("tiny", [128, 64])
    gsc = sb("gsc", [128, 32])
    ret_qw = sb("ret_qw", [128, 512], BF16)
    ret_gs = [sb(f"ret_gs{j}", [128, 512], BF16) for j in range(2)]

    NPS = 6
    PS = [nc.alloc_psum_tensor(f"ps{i}", [128, 512], F32) for i in range(NPS)]
    PT = [nc.alloc_psum_tensor(f"pt{i}", [128, 1024], BF16) for i in range(2)]
    ctr = {"ps": 0, "pt": 0, "F": 0, "B": 0, "wu": 0}

    def ps_next():
        i = ctr["ps"] % NPS
        ctr["ps"] += 1
        return PS[i], ("ps", i)

    def pt_next():
        i = ctr["pt"] % 2
        ctr["pt"] += 1
        return PT[i], ("pt", i)

    def f_next():
        i = ctr["F"] % NFS
        ctr["F"] += 1
        return Fs[i], ("F", i)

    def b_next():
        i = ctr["B"] % NBS
        ctr["B"] += 1
        return Bs[i], ("B", i)

    def wu_next(n=1):
        res = []
        for _ in range(n):
            i = ctr["wu"] % NWU
            ctr["wu"] += 1
            res.append((wb[i], ("wu", i)))
        return res

    def cfa(name):
        o, w = CF_OFF[name]
        return cf[:, o:o + w]

    def cba(name):
        o, w = CB_OFF[name]
        return cbt[:, o:o + w]

    def ppc(l, c, n=1):
        return pp[:, l * PL + c: l * PL + c + n]

    def act(out, in_, func, reads, writes, bias=None, scale=None):
        kw = {}
        if bias is not None:
            kw["bias"] = bias
        if scale is not None:
            kw["scale"] = scale
        P.add("act", lambda e: e.activation(out=out, in_=in_, func=func, **kw), reads, writes)

    def tt(out, in0, in1, op, reads, writes, eng="dve"):
        P.add(eng, lambda e: e.tensor_tensor(out=out, in0=in0, in1=in1, op=op), reads, writes)

    def ts(out, in0, s1, s2, op0, op1, reads, writes, eng="dve"):
        if op1 is None:
            P.add(eng, lambda e: e.tensor_scalar(out=out, in0=in0, scalar1=s1, scalar2=None, op0=op0), reads, writes)
        else:
            P.add(eng, lambda e: e.tensor_scalar(out=out, in0=in0, scalar1=s1, scalar2=s2, op0=op0, op1=op1),
                  reads, writes)

    def stt(out, in0, scalar, in1, op0, op1, reads, writes, eng="dve"):
        P.add(eng, lambda e: e.scalar_tensor_tensor(out=out, in0=in0, scalar=scalar, in1=in1, op0=op0, op1=op1),
              reads, writes)

    def cp(out, in_, reads, writes, eng="dve"):
        if eng == "act":
            P.add("act", lambda e: e.activation(out=out, in_=in_, func=AF.Copy), reads, writes)
        else:
            P.add(eng, lambda e: e.tensor_copy(out=out, in_=in_), reads, writes)

    def mm(out, lhsT, rhs, start, stop, reads, writes, tp=None):
        if tp is None:
            P.add("pe", lambda e: e.matmul(out, lhsT, rhs, start=start, stop=stop), reads, writes)
        else:
            P.add("pe", lambda e: e.matmul(out, lhsT, rhs, start=start, stop=stop, tile_position=tp), reads, writes)

    def tr(out, in_, ident, reads, writes):
        P.add("pe", lambda e: e.transpose(out, in_, ident), reads, writes)

    def dma(q, out, in_, reads, writes):
        P.add(q, lambda e: e.dma_start(out=out, in_=in_), reads, writes, dma=True)

    CONST = ("const",)

    dma("sp", pp[:], ppd[:], [], [CONST])
    dma("sp", cf[:], cstf[:], [], [CONST])
    dma("pool", cbt[:], cstb[:], [], [CONST])
    dma("sp", msk[:], mskd[:], [], [CONST])
    P.add("dve", lambda e: e.memset(uhalo[:], 0.0), [], [("uhalo",)])
    P.add("dve", lambda e: e.memset(gsc[:], 0.0), [], [("gsc",)])
    for j in range(2):
        P.add("dve", lambda e, j=j: e.memset(lxs[j][:], 0.0), [], [("lxs", j)])

    def lg(l):
        return pp[:, l * PL + 16: l * PL + 20]
    mx = tiny[:, 0:4]
    ex = tiny[:, 4:20].rearrange("p (l h) -> p l h", l=4)
    ssum = tiny[:, 20:24]
    TK = ("tiny",)
    tt(mx, lg(0), lg(1), ALU.max, [CONST], [TK])
    tt(mx, mx, lg(2), ALU.max, [CONST, TK], [TK])
    tt(mx, mx, lg(3), ALU.max, [CONST, TK], [TK])
    for l in range(4):
        tt(ex[:, l, :], lg(l), mx, ALU.subtract, [CONST, TK], [TK])
    act(tiny[:, 4:20], tiny[:, 4:20], AF.Exp, [TK], [TK])
    tt(ssum, ex[:, 0, :], ex[:, 1, :], ALU.add, [TK], [TK])
    tt(ssum, ssum, ex[:, 2, :], ALU.add, [TK], [TK])
    tt(ssum, ssum, ex[:, 3, :], ALU.add, [TK], [TK])
    P.add("dve", lambda e: e.reciprocal(out=ssum, in_=ssum), [TK], [TK])
    for l in range(4):
        tt(ex[:, l, :], ex[:, l, :], ssum, ALU.mult, [TK], [TK])
    DK = ("der",)
    P.add("dve", lambda e: e.memset(der[:, 0, :, 0], 0.0), [], [DK])
    for l in range(1, 4):
        tt(der[:, l, :, 0], der[:, l - 1, :, 0], ex[:, l, :], ALU.add, [TK, DK], [DK])
    for l in range(4):
        ts(der[:, l, :, 1], der[:, l, :, 0], -1.0, 1.0, ALU.mult, ALU.add, [DK], [DK])
        ts(der[:, l, :, 2], der[:, l, :, 0], 1.0, None, ALU.subtract, None, [DK], [DK])
    CK = ("cng",)
    for l in range(4):
        lam = pp[:, l * PL + 36: l * PL + 38]
        act(tiny[:, 32:34], lam, AF.Exp, [CONST, TK], [TK], scale=-1.0)
        act(tiny[:, 34:36], tiny[:, 32:34], AF.Ln, [TK], [TK], bias=1.0)
        ts(cng[:, l, :, 0], tiny[:, 34:36], -8.0, None, ALU.mult, None, [TK], [CK])
        ts(cng[:, l, :, 1], tiny[:, 34:36], -16.0, None, ALU.mult, None, [TK], [CK])

    ident = cfa("ident")
    blocks = [(0, 16)] + [(16 + 128 * i, 128) for i in range(16)]

    def hkey(k, ti):
        return ("h", k, ti)

    def tile_of(t0):
        for ti, (a, w) in enumerate(TILES):
            if a <= t0 < a + w:
                return ti
        raise ValueError

    for (t0, nt) in blocks:
        ti = tile_of(t0)
        dma("sp", xin[0:nt, :], xs[t0:t0 + nt, :], [], [("xo",)])
        for kg in range(2):
            pst, pk = ps_next()
            for kk in range(4):
                k = kg * 4 + kk
                tr(pst[:, kk * 128: kk * 128 + nt], xin[0:nt, k * 128:(k + 1) * 128], ident[0:nt, 0:nt],
                   [("xo",), CONST], [pk])
            cp(h[:, kg * 4:(kg + 1) * 4, t0:t0 + nt],
               pst[:, :].rearrange("p (k t) -> p k t", k=4)[:, :, 0:nt],
               [pk], [hkey(k, ti) for k in range(kg * 4, kg * 4 + 4)], eng="act" if kg else "dve")

    if dbg == ("x", 0):
        dma("sp", dbgd[:], h[:], [hkey(k, ti) for k in range(8) for ti in range(5)], [("dbg",)])
    def halo_inject(l, sub):
        idx = l * 2 + sub
        dma("sp", halo_sb[:].rearrange("p k t -> p (k t)"), halo_in[idx], [], [("halo_sb",)])
        ts(halo_t[:], halo_sb[:], msk[:, 1:2], None, ALU.mult, None, [("halo_sb",), CONST], [("halo_t",)])
        stt(h[:, :, 13:16], h[:, :, 13:16], msk[:, 0:1], halo_t[:], ALU.mult, ALU.add,
            [("halo_t",), CONST] + [hkey(k, 0) for k in range(8)], [hkey(k, 0) for k in range(8)])

    def halo_dump(l, sub):
        idx = l * 2 + sub
        dma("sp", halo_out[idx].rearrange("p (k t) -> p k t", k=8), h[:, :, LT - 3:LT],
            [hkey(k, 4) for k in range(8)], [("halo_out", idx)])

    def norm_phase(gcol_fn, su, tiles=None):
        P.label = "norm"
        c_base = SUP_COLS[su][0]
        for ti in (SUPERS[su] if tiles is None else tiles):
            t0, W = TILES[ti]
            c0 = t0 - c_base
            pst, pk = ps_next()
            for k in range(8):
                act(sq[:, k % 4, 0:W], h[:, k, t0:t0 + W], AF.Square, [hkey(k, ti)], [("sq", k % 4)])
                mm(pst[:, 0:W], cba("ones1024"), sq[:, k % 4, 0:W], k == 0, k == 7, [("sq", k % 4), CONST], [pk])
            rstd, rk = f_next()
            act(rstd[:, 0:W], pst[:, 0:W], AF.Ln, [pk], [rk], bias=EPS)
            act(rstd[:, 0:W], rstd[:, 0:W], AF.Exp, [rk], [rk], scale=-0.5)
            for k in range(8):
                stt(hn[:, k, c0:c0 + W], h[:, k, t0:t0 + W], gcol_fn(k), rstd[:, 0:W], ALU.mult, ALU.mult,
                    [hkey(k, ti), rk, CONST], [("hn", k, ti)])

    def load_w(dram2d, c0, ncols, units, row0=0, nrows=D):
        nk = nrows // 128
        per_unit = 4096 // ncols if ncols <= 4096 else 0
        views = []
        src = dram2d[row0:row0 + nrows, c0:c0 + ncols].rearrange("(k p) c -> p k c", p=128)
        if nk * ncols <= 4096:
            wt, wk = units[0]
            dst = wt[:, 0:nk * ncols].rearrange("p (k c) -> p k c", k=nk)
            dma("pool", dst, src, [], [wk])
            return [(dst[:, k, :], wk) for k in range(nk)]
        kpu = 4096 // ncols
        res = []
        for ui in range((nk + kpu - 1) // kpu):
            wt, wk = units[ui]
            k0 = ui * kpu
            k1 = min(nk, k0 + kpu)
            dst = wt[:, 0:(k1 - k0) * ncols].rearrange("p (k c) -> p k c", k=k1 - k0)
            dma("pool", dst, src[:, k0:k1, :], [], [wk])
            for k in range(k0, k1):
                res.append((dst[:, k - k0, :], wk))
        return res

    def hn_cols(ti, su):
        t0, W = TILES[ti]
        c0 = t0 - SUP_COLS[su][0]
        return c0, W

    def proj_fm(Wk, col0, ti, su, ncols=128):
        c0, W = hn_cols(ti, su)
        pst, pk = ps_next()
        for k in range(8):
            wap, wk = Wk[k]
            mm(pst[0:ncols, 0:W], wap[:, col0:col0 + ncols], hn[:, k, c0:c0 + W], k == 0, k == 7,
               [wk, ("hn", k, ti)], [pk])
        return pst, pk

    def state_merge(state_ap, skey, in_ap, in_key, so=False):
        if so:
            ts(state_ap, state_ap, msk[:, 0:1], None, ALU.mult, None, [skey, CONST], [skey])
        else:
            stt(state_ap, state_ap, msk[:, 0:1], in_ap, ALU.mult, ALU.add, [skey, in_key, CONST], [skey])

    def hg_group(l, hh, su, Wk, so=False):
        P.label = "pre_hg" if so else "hg"
        lbc = der[:, l, hh, 0:1]
        omlc = der[:, l, hh, 1:2]
        nomlc = der[:, l, hh, 2:3]
        gn = ppc(l, 20)
        S = Shg[hh]
        SK = ("Shg", hh)
        for ti in SUPERS[su]:
            c0, W = hn_cols(ti, su)
            chs = chunks_of(W)
            nch = len(chs)
            full = (not so) or fused
            save = so and fused
            t0_ = TILES[ti][0]
            zf, zfk = proj_fm(Wk, 256, ti, su)
            vps = []
            for ci, (s0, cw) in enumerate(chs):
                if ci % 4 == 0:
                    vp, vk = ps_next()
                    vps.append((vp, vk))
                for k in range(8):
                    wap, wk = Wk[k]
                    mm(vp[0:cw, (ci % 4) * 128:(ci % 4 + 1) * 128], hn[:, k, c0 + s0:c0 + s0 + cw], wap[:, 384:512],
                       k == 0, k == 7, [wk, ("hn", k, ti)], [vk])
            sg, sgk = f_next()
            act(sg[:, 0:W], zf[:, 0:W], AF.Sigmoid, [zfk], [sgk])
            logf, lfk = f_next()
            act(logf[:, 0:W], sg[:, 0:W], AF.Ln, [sgk, ("der",)], [lfk], bias=lbc, scale=omlc)
            bT, btk = f_next()
            P.add("dve", lambda e, bT=bT, logf=logf, W=W: e.tensor_tensor_scan(
                out=bT[:, 0:W], data0=cfa("cmask")[:, 0:W], data1=logf[:, 0:W], initial=0.0,
                op0=ALU.mult, op1=ALU.add), [lfk, CONST], [btk])
            kT, ktk = f_next()
            ts(kT[:, 0:W], sg[:, 0:W], nomlc, omlc, ALU.mult, ALU.add, [sgk, ("der",)], [ktk])
            Ep, epk = f_next()
            act(Ep[:, 0:W], bT[:, 0:W], AF.Exp, [btk], [epk])
            nb, nbk = f_next()
            ts(nb[:, 0:W], bT[:, 0:W], -1.0, 80.0, ALU.mult, ALU.min, [btk], [nbk])
            En, enk = f_next()
            act(En[:, 0:W], nb[:, 0:W], AF.Exp, [nbk], [enk])
            kt, kttk = b_next()
            tt(kt[:, 0:W], kT[:, 0:W], En[:, 0:W], ALU.mult, [ktk, enk], [kttk])
            khT, khk = b_next()
            if W == 512:
                ep3 = Ep[:, 0:512].rearrange("p (c t) -> p c t", c=8)
                tt(khT[:, 0:512].rearrange("p (c t) -> p c t", c=8), kt[:, 0:512].rearrange("p (c t) -> p c t", c=8),
                   ep3[:, :, 63:64].broadcast_to([128, 8, 64]), ALU.mult, [kttk, epk], [khk])
            else:
                for (s0, cw) in chs:
                    ts(khT[:, s0:s0 + cw], kt[:, s0:s0 + cw], Ep[:, s0 + cw - 1:s0 + cw], None, ALU.mult, None,
                       [kttk, epk], [khk])
            for bi, (vp, vk) in enumerate(vps):
                n = min(4, nch - bi * 4)
                cwm = chs[0][1]
                cp(TM1[0:cwm, bi * 512: bi * 512 + n * 128], vp[0:cwm, 0:n * 128], [vk], [("TM1",)], eng="act")
            ptt, ptk = pt_next()
            for ci, (s0, cw) in enumerate(chs):
                tr(ptt[0:cw, ci * 128:(ci + 1) * 128], khT[:, s0:s0 + cw], cba("identb"), [khk, CONST], [ptk])
            cwm = chs[0][1]
            cp(TM2[0:cwm, 0:nch * 128], ptt[0:cwm, 0:nch * 128], [ptk], [("TM2",)], eng="dve")
            if full:
                zq, zqk = proj_fm(Wk, 0, ti, su)
                zg, zgk = proj_fm(Wk, 128, ti, su)
                qs, qsk = f_next()
                act(qs[:, 0:W], zq[:, 0:W], AF.Silu, [zqk], [qsk])
                gs, gsk = b_next()
                act(gs[:, 0:W], zg[:, 0:W], AF.Silu, [zgk], [gsk])
                qt, qtk = b_next()
                stt(qt[:, 0:W], qs[:, 0:W], 128.0 ** -0.5, Ep[:, 0:W], ALU.mult, ALU.mult, [qsk, epk], [qtk])
            if full:
                sc, sck = ps_next()
                for ci, (s0, cw) in enumerate(chs):
                    mm(sc[0:cw, ci * 64: ci * 64 + cw], kt[:, s0:s0 + cw], qt[:, s0:s0 + cw], True, True,
                       [kttk, qtk], [sck])
                AT, atk = b_next()
                wtot = (nch - 1) * 64 + chs[-1][1]
                stt(AT[0:cwm, 0:wtot], sc[0:cwm, 0:wtot], 1e30, cba("maskT")[0:cwm, 0:wtot], ALU.min, ALU.mult,
                    [sck, CONST], [atk])
            kvs = []
            for ci, (s0, cw) in enumerate(chs):
                if ci % 4 == 0:
                    kp, kk = ps_next()
                    kvs.append((kp, kk))
                mm(kp[:, (ci % 4) * 128:(ci % 4 + 1) * 128], TM2[0:cw, ci * 128:(ci + 1) * 128],
                   TM1[0:cw, ci * 128:(ci + 1) * 128], True, True, [("TM2",), ("TM1",)], [kk])
            XK = ("xo",)
            if full:
                cp(xin[:, 0:128], S[:], [SK], [XK], eng="dve")
            if save:
                qg, qgk = b_next()
                if ti == 0:
                    P.add("dve", lambda e, qg=qg, W=W: e.memset(qg[:, 0:W], 0.0), [], [qgk])
            for ci, (s0, cw) in enumerate(chs):
                kp, kk = kvs[ci // 4]
                dcol = Ep[:, s0 + cw - 1:s0 + cw]
                kv_ap = kp[:, (ci % 4) * 128:(ci % 4 + 1) * 128]
                if not full:
                    stt(S[:], S[:], dcol, kv_ap, ALU.mult, ALU.add, [SK, epk, kk], [SK])
                elif ci == nch - 1:
                    stt(S[:], xin[:, ci * 128:(ci + 1) * 128], dcol, kv_ap, ALU.mult, ALU.add, [XK, epk, kk], [SK])
                else:
                    stt(xin[:, (ci + 1) * 128:(ci + 2) * 128], xin[:, ci * 128:(ci + 1) * 128], dcol, kv_ap,
                        ALU.mult, ALU.add, [XK, epk, kk], [XK])
                if so and ti != 0 and not save:
                    ts(Dhg[:, hh:hh + 1], Dhg[:, hh:hh + 1], dcol, None, ALU.mult, None,
                       [("Dhg",), epk], [("Dhg",)])
            if save and ti != 0:
                GS = ("gsc",)
                cp(gsc[:, 0:1], Dhg[:, hh:hh + 1], [("Dhg",)], [GS], eng="dve")
                ep3 = Ep[:, 0:512].rearrange("p (c t) -> p c t", c=8)
                P.add("dve", lambda e, ep3=ep3, hh=hh: e.tensor_tensor_scan(
                    out=gsc[:, 1:9], data0=ep3[:, :, 63], data1=gsc[:, 16:24], initial=Dhg[:, hh:hh + 1],
                    op0=ALU.mult, op1=ALU.add), [epk, ("Dhg",), GS], [GS])
                cp(Dhg[:, hh:hh + 1], gsc[:, 8:9], [GS], [("Dhg",)], eng="dve")
                tt(qg[:, 0:512].rearrange("p (c t) -> p c t", c=8), qt[:, 0:512].rearrange("p (c t) -> p c t", c=8),
                   gsc[:, 0:8].rearrange("p (c o) -> p c o", o=1).broadcast_to([128, 8, 64]), ALU.mult,
                   [qtk, GS], [qgk])
            if full:
                cp(SBk[:, 0:nch * 128], xin[:, 0:nch * 128], [XK], [("SBk",)], eng="act")
            if ti == 0:
                state_merge(S[:], SK, stin_hg[:, hh * 128:(hh + 1) * 128], ("stin",), so)
            if not full:
                continue
            op_, ok = ps_next()
            for ci, (s0, cw) in enumerate(chs):
                mm(op_[:, s0:s0 + cw], TM1[0:cw, ci * 128:(ci + 1) * 128], AT[0:cw, ci * 64: ci * 64 + cw],
                   True, False, [("TM1",), atk], [ok])
                mm(op_[:, s0:s0 + cw], SBk[:, ci * 128:(ci + 1) * 128], qt[:, s0:s0 + cw], False, True,
                   [("SBk",), qtk], [ok])
            if save:
                ol, olk = f_next()
                cp(ol[:, 0:W], op_[:, 0:W], [ok], [olk], eng="act")
                dma("sp", hscr_o[hh][:, t0_:t0_ + W], ol[:, 0:W], [olk], [("hscr", 0, hh, ti)])
                dma("sp", hscr_q[hh][:, t0_:t0_ + W], qg[:, 0:W], [qgk], [("hscr", 1, hh, ti)])
                dma("sp", hscr_g[hh][:, t0_:t0_ + W], gs[:, 0:W], [gsk], [("hscr", 2, hh, ti)])
                continue
            osq, osk = b_next()
            act(osq[:, 0:W], op_[:, 0:W], AF.Square, [ok], [osk])
            msp, msk_ = ps_next()
            mm(msp[:, 0:W], cba("ones128"), osq[:, 0:W], True, True, [osk, CONST], [msk_])
            rstd, rk = f_next()
            act(rstd[:, 0:W], msp[:, 0:W], AF.Ln, [msk_], [rk], bias=EPS)
            act(rstd[:, 0:W], rstd[:, 0:W], AF.Exp, [rk], [rk], scale=-0.5)
            t1, t1k = f_next()
            stt(t1[:, 0:W], op_[:, 0:W], gn, rstd[:, 0:W], ALU.mult, ALU.mult, [ok, rk, CONST], [t1k])
            tt(mg[:, hh, c0:c0 + W], t1[:, 0:W], gs[:, 0:W], ALU.mult, [t1k, gsk], [("mg", hh, ti)])

    def hg_main(l, su):
        P.label = "hg"
        gn = ppc(l, 20)
        units = [(hh, ti) for hh in range(4) for ti in SUPERS[su]]

        def stage_a(hh, ti):
            c0, W = hn_cols(ti, su)
            t0_ = TILES[ti][0]
            d = dict(hh=hh, ti=ti, c0=c0, W=W)
            d["ol"], d["olk"] = f_next()
            dma("sp", d["ol"][:, 0:W], hscr_o[hh][:, t0_:t0_ + W], [("hscr", 0, hh, ti)], [d["olk"]])
            qg, qgk = b_next()
            dma("sp", qg[:, 0:W], hscr_q[hh][:, t0_:t0_ + W], [("hscr", 1, hh, ti)], [qgk])
            d["gs"], d["gsk"] = b_next()
            dma("sp", d["gs"][:, 0:W], hscr_g[hh][:, t0_:t0_ + W], [("hscr", 2, hh, ti)], [d["gsk"]])
            cps, ck = ps_next()
            mm(cps[:, 0:W], SBk[:, hh * 128:(hh + 1) * 128], qg[:, 0:W], True, True, [("SBk",), qgk], [ck])
            tt(d["ol"][:, 0:W], d["ol"][:, 0:W], cps[:, 0:W], ALU.add, [d["olk"], ck], [d["olk"]])
            d["osq"], d["osk"] = b_next()
            act(d["osq"][:, 0:W], d["ol"][:, 0:W], AF.Square, [d["olk"]], [d["osk"]])
            return d

        def stage_b(d):
            W, c0, hh, ti = d["W"], d["c0"], d["hh"], d["ti"]
            msp, mk = ps_next()
            mm(msp[:, 0:W], cba("ones128"), d["osq"][:, 0:W], True, True, [d["osk"], CONST], [mk])
            rstd, rk = f_next()
            act(rstd[:, 0:W], msp[:, 0:W], AF.Ln, [mk], [rk], bias=EPS)
            act(rstd[:, 0:W], rstd[:, 0:W], AF.Exp, [rk], [rk], scale=-0.5)
            stt(d["ol"][:, 0:W], d["ol"][:, 0:W], gn, rstd[:, 0:W], ALU.mult, ALU.mult, [d["olk"], rk, CONST],
                [d["olk"]])
            tt(mg[:, hh, c0:c0 + W], d["ol"][:, 0:W], d["gs"][:, 0:W], ALU.mult, [d["olk"], d["gsk"]],
               [("mg", hh, ti)])

        prev = None
        for (hh, ti) in units:
            cur = stage_a(hh, ti)
            if prev is not None:
                stage_b(prev)
            prev = cur
        stage_b(prev)

    def ret_group(l, su, Wk, so=False):
        P.label = "pre_ret" if so else "ret"
        gn = ppc(l, 21)
        RK = ("Rst",)
        for ti in SUPERS[su]:
            t0, W = TILES[ti]
            c0, _ = hn_cols(ti, su)
            chs = chunks_of(W)
            nch = len(chs)
            cwm = chs[0][1]
            full = (not so) or fused
            save = so and fused
            dma("sp", ropec[:, 0:W], roped[0][:, t0:t0 + W], [], [("ropec",)])
            dma("sp", ropes[:, 0:W], roped[1][:, t0:t0 + W], [], [("ropes",)])

            def rope_fm(colx, cols_):
                zx, zxk = proj_fm(Wk, colx, ti, su)
                zs, zsk = proj_fm(Wk, cols_, ti, su)
                a1, a1k = f_next()
                tt(a1[:, 0:W], zx[:, 0:W], ropec[:, 0:W], ALU.mult, [zxk, ("ropec",)], [a1k])
                a2, a2k = f_next()
                tt(a2[:, 0:W], zs[:, 0:W], ropes[:, 0:W], ALU.mult, [zsk, ("ropes",)], [a2k])
                r, rk_ = f_next()
                tt(r[:, 0:W], a1[:, 0:W], a2[:, 0:W], ALU.add, [a1k, a2k], [rk_])
                return r, rk_
            if full:
                qr, qrk = rope_fm(0, 128)
                qrb, qrbk = b_next()
                cp(qrb[:, 0:W], qr[:, 0:W], [qrk], [qrbk], eng="act")
                qw, qwk = ret_qw, ("ret_qw",)
                tt(qw[:, 0:W], qr[:, 0:W], cfa("wq")[:, 0:W], ALU.mult, [qrk, CONST], [qwk])
                if save:
                    wqt, wqtk = f_next()
                    dma("sp", wqt[:, 0:W], wqsegd[:, t0:t0 + W], [], [wqtk])
                    qwg, qwgk = b_next()
                    tt(qwg[:, 0:W], qr[:, 0:W], wqt[:, 0:W], ALU.mult, [qrk, wqtk], [qwgk])
                    dma("sp", rscr_q[:, t0:t0 + W], qwg[:, 0:W], [qwgk], [("rscr", "q", ti)])
            kr, krk = rope_fm(512, 640)
            krb, krbk = b_next()
            cp(krb[:, 0:W], kr[:, 0:W], [krk], [krbk], eng="act")
            gsl = []
            for j in range(2 if full else 0):
                zg, zgk = proj_fm(Wk, 256 + j * 128, ti, su)
                gs, gsk = ret_gs[j], ("ret_gs", j)
                act(gs[:, 0:W], zg[:, 0:W], AF.Silu, [zgk], [gsk])
                gsl.append((gs, gsk))
            for ci, (s0, cw) in enumerate(chs):
                if ci % 2 == 0:
                    vp, vk = ps_next()
                for k in range(8):
                    wap, wk = Wk[k]
                    mm(vp[0:cw, (ci % 2) * 256:(ci % 2 + 1) * 256], hn[:, k, c0 + s0:c0 + s0 + cw], wap[:, 768:1024],
                       k == 0, k == 7, [wk, ("hn", k, ti)], [vk])
                if ci % 2 == 1 or ci == nch - 1:
                    n = (ci % 2) + 1
                    cb0 = (ci - (ci % 2)) * 256
                    cp(TM1[0:cw, cb0: cb0 + n * 256], vp[0:cw, 0:n * 256], [vk], [("TM1",)], eng="act")
            ptt, ptk = pt_next()
            for ci, (s0, cw) in enumerate(chs):
                tr(ptt[0:cw, ci * 128:(ci + 1) * 128], krb[:, s0:s0 + cw], cba("identb"), [krbk, CONST], [ptk])
            wst_ap = cba("wst")[0:cwm, 0:nch * 128] if W == 512 else cba("wst16")[0:cwm, 0:128]
            tt(TM2[0:cwm, 0:nch * 128], ptt[0:cwm, 0:nch * 128], wst_ap, ALU.mult, [ptk, CONST], [("TM2",)])
            ATs = []
            wtot = (nch - 1) * 64 + chs[-1][1]
            for hd in range(4 if full else 0):
                sc, sck = ps_next()
                for ci, (s0, cw) in enumerate(chs):
                    mm(sc[0:cw, ci * 64: ci * 64 + cw], krb[32 * hd:32 * hd + 32, s0:s0 + cw],
                       qrb[32 * hd:32 * hd + 32, s0:s0 + cw], True, True, [krbk, qrbk], [sck], tp=(32 * hd, 0))
                AT, atk = b_next()
                tt(AT[0:cwm, 0:wtot], sc[0:cwm, 0:wtot], cba("dmatT")[0:cwm, hd * 512: hd * 512 + wtot], ALU.mult,
                   [sck, CONST], [atk])
                ATs.append((AT, atk))
            kvp, kvk = ps_next()
            for ci, (s0, cw) in enumerate(chs):
                for hd in range(4):
                    mm(kvp[32 * hd:32 * hd + 32, ci * 64:(ci + 1) * 64],
                       TM2[0:cw, ci * 128 + 32 * hd: ci * 128 + 32 * hd + 32],
                       TM1[0:cw, ci * 256 + 64 * hd: ci * 256 + 64 * hd + 64], True, True,
                       [("TM2",), ("TM1",)], [kvk], tp=(0, 32 * hd))
            gdec = cfa("g6416")[:, 0:1] if W == 512 else cfa("g6416")[:, 1:2]
            XK = ("xo",)
            if full:
                cp(xin[:, 0:64], Rst[:], [RK], [XK], eng="dve")
            for ci, (s0, cw) in enumerate(chs):
                kv_ap = kvp[:, ci * 64:(ci + 1) * 64]
                if not full:
                    stt(Rst[:], Rst[:], gdec, kv_ap, ALU.mult, ALU.add, [RK, kvk, CONST], [RK])
                elif ci == nch - 1:
                    stt(Rst[:], xin[:, ci * 64:(ci + 1) * 64], gdec, kv_ap, ALU.mult, ALU.add, [XK, kvk, CONST], [RK])
                else:
                    stt(xin[:, (ci + 1) * 64:(ci + 2) * 64], xin[:, ci * 64:(ci + 1) * 64], gdec, kv_ap,
                        ALU.mult, ALU.add, [XK, kvk, CONST], [XK])
            if full:
                cp(SBk[:, 0:nch * 64], xin[:, 0:nch * 64], [XK], [("SBk",)], eng="act")
            if ti == 0:
                state_merge(Rst[:], RK, stin_ret[:], ("stin",), so)
            if not full:
                continue
            for j in range(2):
                op_, ok = ps_next()
                for hh2 in range(2):
                    hd = j * 2 + hh2
                    po = 64 * hh2
                    AT, atk = ATs[hd]
                    for ci, (s0, cw) in enumerate(chs):
                        mm(op_[po:po + 64, s0:s0 + cw], TM1[0:cw, ci * 256 + 64 * hd: ci * 256 + 64 * hd + 64],
                           AT[0:cw, ci * 64: ci * 64 + cw], True, False, [("TM1",), atk], [ok], tp=(0, po))
                        mm(op_[po:po + 64, s0:s0 + cw], SBk[32 * hd:32 * hd + 32, ci * 64:(ci + 1) * 64],
                           qw[32 * hd:32 * hd + 32, s0:s0 + cw], False, True, [("SBk",), qwk], [ok], tp=(32 * hd, po))
                if save:
                    ol, olk = f_next()
                    cp(ol[:, 0:W], op_[:, 0:W], [ok], [olk], eng="act")
                    dma("sp", rscr_o[j][:, t0:t0 + W], ol[:, 0:W], [olk], [("rscr", "o", j, ti)])
                    gs, gsk = gsl[j]
                    dma("sp", rscr_g[j][:, t0:t0 + W], gs[:, 0:W], [gsk], [("rscr", "g", j, ti)])
                    continue
                ob, obk = b_next()
                cp(ob[:, 0:W], op_[:, 0:W], [ok], [obk], eng="act")
                of, ofk = f_next()
                cp(of[:, 0:W], op_[:, 0:W], [ok], [ofk], eng="act")
                mup, muk = ps_next()
                mm(mup[:, 0:W], cba("bones64"), ob[:, 0:W], True, True, [obk, CONST], [muk])
                dd, ddk = f_next()
                tt(dd[:, 0:W], of[:, 0:W], mup[:, 0:W], ALU.subtract, [ofk, muk], [ddk])
                dsq, dsk = b_next()
                act(dsq[:, 0:W], dd[:, 0:W], AF.Square, [ddk], [dsk])
                vp2, vk2 = ps_next()
                mm(vp2[:, 0:W], cba("bones64"), dsq[:, 0:W], True, True, [dsk, CONST], [vk2])
                rstd, rk = f_next()
                act(rstd[:, 0:W], vp2[:, 0:W], AF.Ln, [vk2], [rk], bias=EPS)
                act(rstd[:, 0:W], rstd[:, 0:W], AF.Exp, [rk], [rk], scale=-0.5)
                t1, t1k = f_next()
                stt(t1[:, 0:W], dd[:, 0:W], gn, rstd[:, 0:W], ALU.mult, ALU.mult, [ddk, rk, CONST], [t1k])
                gs, gsk = gsl[j]
                tt(mg[:, 4 + j, c0:c0 + W], t1[:, 0:W], gs[:, 0:W], ALU.mult, [t1k, gsk], [("mg", 4 + j, ti)])

    def ret_main(l, su):
        P.label = "ret"
        gn = ppc(l, 21)
        for ti in SUPERS[su]:
            t0, W = TILES[ti]
            c0, _ = hn_cols(ti, su)
            qwg, qwgk = b_next()
            dma("sp", qwg[:, 0:W], rscr_q[:, t0:t0 + W], [("rscr", "q", ti)], [qwgk])
            st = []
            for j in range(2):
                of, ofk = f_next()
                dma("sp", of[:, 0:W], rscr_o[j][:, t0:t0 + W], [("rscr", "o", j, ti)], [ofk])
                gs, gsk = b_next()
                dma("sp", gs[:, 0:W], rscr_g[j][:, t0:t0 + W], [("rscr", "g", j, ti)], [gsk])
                st.append(dict(j=j, of=of, ofk=ofk, gs=gs, gsk=gsk))
            for d in st:
                d["cps"], d["ck"] = ps_next()
                for hh2 in range(2):
                    hd = d["j"] * 2 + hh2
                    po = 64 * hh2
                    mm(d["cps"][po:po + 64, 0:W], SBk[32 * hd:32 * hd + 32, 512:576], qwg[32 * hd:32 * hd + 32, 0:W],
                       True, True, [("SBk",), qwgk], [d["ck"]], tp=(32 * hd, po))
            for d in st:
                tt(d["of"][:, 0:W], d["of"][:, 0:W], d["cps"][:, 0:W], ALU.add, [d["ofk"], d["ck"]], [d["ofk"]])
            for d in st:
                d["ob"], d["obk"] = b_next()
                cp(d["ob"][:, 0:W], d["of"][:, 0:W], [d["ofk"]], [d["obk"]], eng="act")
            for d in st:
                d["mup"], d["muk"] = ps_next()
                mm(d["mup"][:, 0:W], cba("bones64"), d["ob"][:, 0:W], True, True, [d["obk"], CONST], [d["muk"]])
            for d in st:
                d["dd"], d["ddk"] = f_next()
                tt(d["dd"][:, 0:W], d["of"][:, 0:W], d["mup"][:, 0:W], ALU.subtract, [d["ofk"], d["muk"]], [d["ddk"]])
            for d in st:
                d["dsq"], d["dsk"] = b_next()
                act(d["dsq"][:, 0:W], d["dd"][:, 0:W], AF.Square, [d["ddk"]], [d["dsk"]])
            for d in st:
                d["vp2"], d["vk2"] = ps_next()
                mm(d["vp2"][:, 0:W], cba("bones64"), d["dsq"][:, 0:W], True, True, [d["dsk"], CONST], [d["vk2"]])
            for d in st:
                d["rstd"], d["rk"] = f_next()
                act(d["rstd"][:, 0:W], d["vp2"][:, 0:W], AF.Ln, [d["vk2"]], [d["rk"]], bias=EPS)
            for d in st:
                act(d["rstd"][:, 0:W], d["rstd"][:, 0:W], AF.Exp, [d["rk"]], [d["rk"]], scale=-0.5)
            for d in st:
                stt(d["dd"][:, 0:W], d["dd"][:, 0:W], gn, d["rstd"][:, 0:W], ALU.mult, ALU.mult,
                    [d["ddk"], d["rk"], CONST], [d["ddk"]])
            for d in st:
                tt(mg[:, 4 + d["j"], c0:c0 + W], d["dd"][:, 0:W], d["gs"][:, 0:W], ALU.mult, [d["ddk"], d["gsk"]],
                   [("mg", 4 + d["j"], ti)])

    def lru_group(l, su, Wk, so=False):
        P.label = "pre_lru" if so else "lru"
        for ti in SUPERS[su]:
            c0, W = hn_cols(ti, su)
            for j in range(2):
                LK = ("lxs", j)
                t0_ = TILES[ti][0]
                zx, zxk = proj_fm(Wk, j * 128, ti, su)
                if (not so) or fused:
                    zy, zyk = proj_fm(Wk, 256 + j * 128, ti, su)
                cp(lxs[j][:, 3:3 + W], zx[:, 0:W], [zxk], [LK], eng="act")
                if so and fused:
                    ge, gek = f_next()
                    act(ge[:, 0:W], zy[:, 0:W], AF.Gelu_apprx_tanh, [zyk], [gek])
                xc, xck = f_next()
                ts(xc[:, 0:W], lxs[j][:, 0:W], ppc(l, 22 + j * 4 + 0), ppc(l, 30 + j), ALU.mult, ALU.add,
                   [LK, CONST], [xck])
                for tap in range(1, 4):
                    stt(xc[:, 0:W], lxs[j][:, tap:tap + W], ppc(l, 22 + j * 4 + tap), xc[:, 0:W], ALU.mult, ALU.add,
                        [LK, xck, CONST], [xck])
                cp(lxs[j][:, 0:3], lxs[j][:, W:W + 3], [LK], [LK], eng="dve")
                xcb, xcbk = b_next()
                cp(xcb[:, 0:W], xc[:, 0:W], [xck], [xcbk], eng="act")
                gap, gak = ps_next()
                mm(gap[:, 0:W], lrug[:, 0 * 2 + j, :], xcb[:, 0:W], True, True, [xcbk, ("lrug",)], [gak])
                gip, gik = ps_next()
                mm(gip[:, 0:W], lrug[:, 1 * 2 + j, :], xcb[:, 0:W], True, True, [xcbk, ("lrug",)], [gik])
                r, rk_ = f_next()
                act(r[:, 0:W], gap[:, 0:W], AF.Sigmoid, [gak, CONST], [rk_], bias=ppc(l, 32 + j))
                ii, iik = f_next()
                act(ii[:, 0:W], gip[:, 0:W], AF.Sigmoid, [gik, CONST], [iik], bias=ppc(l, 34 + j))
                a, ak = f_next()
                act(a[:, 0:W], r[:, 0:W], AF.Exp, [rk_, ("cng",)], [ak], scale=cng[:, l, j, 0:1])
                a2, a2k = f_next()
                act(a2[:, 0:W], r[:, 0:W], AF.Exp, [rk_, ("cng",)], [a2k], scale=cng[:, l, j, 1:2])
                th, thk = f_next()
                act(th[:, 0:W], r[:, 0:W], AF.Tanh, [rk_, ("cng",)], [thk], scale=cng[:, l, j, 0:1])
                stt(a2[:, 0:W], a2[:, 0:W], 1.0, th[:, 0:W], ALU.add, ALU.mult, [a2k, thk], [a2k])
                act(a2[:, 0:W], a2[:, 0:W], AF.Sqrt, [a2k], [a2k], scale=-1.0)
                tt(ii[:, 0:W], ii[:, 0:W], xc[:, 0:W], ALU.mult, [iik, xck], [iik])
                tt(ii[:, 0:W], ii[:, 0:W], a2[:, 0:W], ALU.mult, [iik, a2k], [iik])
                hs, hsk = f_next()
                P.add("dve", lambda e, hs=hs, a=a, ii=ii, W=W, j=j: e.tensor_tensor_scan(
                    out=hs[:, 0:W], data0=a[:, 0:W], data1=ii[:, 0:W], initial=hl[:, j:j + 1],
                    op0=ALU.mult, op1=ALU.add), [ak, iik, ("hl",)], [hsk])
                cp(hl[:, j:j + 1], hs[:, W - 1:W], [hsk], [("hl",)], eng="dve")
                if ti == 0:
                    state_merge(hl[:, j:j + 1], ("hl",), stin_lru[:, j:j + 1], ("stin",), so)
                if so:
                    if ti != 0:
                        P.add("dve", lambda e, r=r, W=W, j=j: e.reduce_sum(
                            out=rsum[:, 2 + j:3 + j], in_=r[:, 0:W], axis=mybir.AxisListType.X), [rk_], [("rsum",)])
                        tt(rsum[:, j:j + 1], rsum[:, j:j + 1], rsum[:, 2 + j:3 + j], ALU.add, [("rsum",)], [("rsum",)])
                    if fused:
                        dma("sp", lscr[0 + j][:, t0_:t0_ + W], a[:, 0:W], [ak], [("lscr", 0, j, ti)])
                        dma("sp", lscr[2 + j][:, t0_:t0_ + W], ii[:, 0:W], [iik], [("lscr", 1, j, ti)])
                        dma("sp", lscr[4 + j][:, t0_:t0_ + W], ge[:, 0:W], [gek], [("lscr", 2, j, ti)])
                    continue
                ge, gek = f_next()
                act(ge[:, 0:W], zy[:, 0:W], AF.Gelu_apprx_tanh, [zyk], [gek])
                tt(mg[:, 6 + j, c0:c0 + W], ge[:, 0:W], hs[:, 0:W], ALU.mult, [gek, hsk], [("mg", 6 + j, ti)])

    def lru_main(l, su):
        P.label = "lru"
        for ti in SUPERS[su]:
            c0, W = hn_cols(ti, su)
            t0_ = TILES[ti][0]
            for j in range(2):
                la, lak = f_next()
                dma("sp", la[:, 0:W], lscr[0 + j][:, t0_:t0_ + W], [("lscr", 0, j, ti)], [lak])
                lu, luk = f_next()
                dma("sp", lu[:, 0:W], lscr[2 + j][:, t0_:t0_ + W], [("lscr", 1, j, ti)], [luk])
                lg_, lgk = f_next()
                dma("sp", lg_[:, 0:W], lscr[4 + j][:, t0_:t0_ + W], [("lscr", 2, j, ti)], [lgk])
                hs, hsk = f_next()
                P.add("dve", lambda e, hs=hs, la=la, lu=lu, W=W, j=j: e.tensor_tensor_scan(
                    out=hs[:, 0:W], data0=la[:, 0:W], data1=lu[:, 0:W], initial=hl[:, j:j + 1],
                    op0=ALU.mult, op1=ALU.add), [lak, luk, ("hl",)], [hsk])
                cp(hl[:, j:j + 1], hs[:, W - 1:W], [hsk], [("hl",)], eng="dve")
                if ti == 0:
                    state_merge(hl[:, j:j + 1], ("hl",), stin_lru[:, j:j + 1], ("stin",), False)
                tt(mg[:, 6 + j, c0:c0 + W], lg_[:, 0:W], hs[:, 0:W], ALU.mult, [lgk, hsk], [("mg", 6 + j, ti)])

    STAGES = []

    def stage(load, comp):
        STAGES.append((load, comp))

    def run_stages():
        n = len(STAGES)
        loaded = {}
        nxt = 0

        def issue_next_load(after):
            nonlocal nxt
            nxt = max(nxt, after)
            while nxt < n and STAGES[nxt][0] is None:
                nxt += 1
            if nxt < n:
                loaded[nxt] = STAGES[nxt][0]()
                nxt += 1
        issue_next_load(0)
        for i in range(n):
            ld, comp = STAGES[i]
            if ld is None:
                comp(None)
                continue
            if i not in loaded:
                loaded[i] = ld()
                nxt = max(nxt, i + 1)
            issue_next_load(i + 1)
            comp(loaded.pop(i))

    def wout_phase(l, su):
        stage(lambda: load_w(woutd[l], 0, 1024, wu_next(2)), lambda Wk: wout_compute(l, su, Wk))

    def wout_compute(l, su, Wk):
        P.label = "wout"
        for ti in (SUPERS[su] if su == 0 else SUPERS[su][::-1]):
            t0, W = TILES[ti]
            c0, _ = hn_cols(ti, su)
            for m in range(8):
                pst, pk = ps_next()
                for k in range(8):
                    wap, wk = Wk[k]
                    mm(pst[:, 0:W], wap[:, m * 128:(m + 1) * 128], mg[:, k, c0:c0 + W], k == 0, k == 7,
                       [wk, ("mg", k, ti)], [pk])
                tt(h[:, m, t0:t0 + W], h[:, m, t0:t0 + W], pst[:, 0:W], ALU.add, [hkey(m, ti), pk], [hkey(m, ti)])

    def ffn_phase(l, su):
        for (j0, j1) in FFN_GROUPS:
            for jb in range(j0, j1, 4):
                nb_ = min(4, j1 - jb)

                def ld(jb=jb, nb_=nb_):
                    Wg = load_w(wupd[l], jb * 128, nb_ * 128, wu_next(1))
                    Wv = load_w(wupd[l], DFF + jb * 128, nb_ * 128, wu_next(1))
                    return (Wg, Wv)
                stage(ld, lambda W2, jb=jb, nb_=nb_, j0=j0: ffn_up_block(l, su, j0, jb, nb_, W2))
            stage(lambda j0=j0, j1=j1: load_w(wdnd[l], 0, 1024, wu_next(2), row0=j0 * 128, nrows=(j1 - j0) * 128),
                  lambda Wd, j0=j0, j1=j1: ffn_down(l, su, j0, j1, Wd))

    def ffn_up_block(l, su, j0, jb, nb_, W2):
        P.label = "ffn"
        Wg, Wv = W2
        for jq in range(nb_):
            j = jb + jq
            jj = j - j0
            cg, cv = j, 22 + j
            cp(ubg[:, 0:2], uhalo[:, cg, :], [("uhalo",)], [("ubg",)], eng="dve")
            cp(ubv[:, 0:2], uhalo[:, cv, :], [("uhalo",)], [("ubv",)], eng="dve")
            for ti in SUPERS[su]:
                c0, W = hn_cols(ti, su)
                ugp, ugk = proj_fm(Wg, jq * 128, ti, su)
                uvp, uvk = proj_fm(Wv, jq * 128, ti, su)
                cp(ubg[:, 2:2 + W], ugp[:, 0:W], [ugk], [("ubg",)], eng="act")
                cp(ubv[:, 2:2 + W], uvp[:, 0:W], [uvk], [("ubv",)], eng="act")
                cgt, cgk = f_next()
                cvt, cvk = f_next()
                pairs = ((ubg, ("ubg",), cg, cgt, cgk), (ubv, ("ubv",), cv, cvt, cvk))
                for (ub, ubk, cc, c_, ck) in pairs:
                    ts(c_[:, 0:W], ub[:, 0:W], ppc(l, 38 + cc * 3 + 0), ppc(l, 170 + cc), ALU.mult, ALU.add,
                       [ubk, CONST], [ck])
                for tap in (1, 2):
                    for (ub, ubk, cc, c_, ck) in pairs:
                        stt(c_[:, 0:W], ub[:, tap:tap + W], ppc(l, 38 + cc * 3 + tap), c_[:, 0:W], ALU.mult, ALU.add,
                            [ubk, ck, CONST], [ck])
                for (ub, ubk, cc, c_, ck) in pairs:
                    cp(ub[:, 0:2], ub[:, W:W + 2], [ubk], [ubk], eng="dve")
                act(cgt[:, 0:W], cgt[:, 0:W], AF.Silu, [cgk], [cgk])
                tt(mg[:, jj, c0:c0 + W], cgt[:, 0:W], cvt[:, 0:W], ALU.mult, [cgk, cvk], [("mg", jj, ti)])
            cp(uhalo[:, cg, :], ubg[:, 0:2], [("ubg",)], [("uhalo",)], eng="dve")
            cp(uhalo[:, cv, :], ubv[:, 0:2], [("ubv",)], [("uhalo",)], eng="dve")

    def ffn_down(l, su, j0, j1, Wd):
        P.label = "ffn_down"
        nj = j1 - j0
        last = (su == 1 and j1 == NJ)
        for ti in (SUPERS[su][::-1] if last else SUPERS[su]):
            t0, W = TILES[ti]
            c0, _ = hn_cols(ti, su)
            for m in range(8):
                pst, pk = ps_next()
                for jj in range(nj):
                    wap, wk = Wd[jj]
                    mm(pst[:, 0:W], wap[:, m * 128:(m + 1) * 128], mg[:, jj, c0:c0 + W], jj == 0, jj == nj - 1,
                       [wk, ("mg", jj, ti)], [pk])
                tt(h[:, m, t0:t0 + W], h[:, m, t0:t0 + W], pst[:, 0:W], ALU.add, [hkey(m, ti), pk],
                   [hkey(m, ti)])

    def dump_dbg(src3, reads):
        dma("sp", dbgd[:], src3, reads, [("dbg",)])

    ALLST = [("Shg", i) for i in range(4)] + [("Dhg",), ("Rst",), ("hl",), ("alog",)]
    RG = [[0, 1, 2, 3], [4, 5, 6, 7]]

    def reset_states():
        P.add("dve", lambda e: e.memset(STT[:], 0.0), [], ALLST)
        P.add("dve", lambda e: e.memset(Dhg[:], 1.0), [], [("Dhg",)])
        P.add("dve", lambda e: e.memset(rsum[:], 0.0), [], [("rsum",)])

    def halo_exchange(l, sub, apply_now=True):
        P.label = "halo_x"
        dma("sp", xh_src.rearrange("p (k t) -> p k t", k=8), h[:, :, LT - 3:LT],
            [hkey(k, 4) for k in range(8)], [("xh_src",)])
        P.add("pool", lambda e: e.collective_compute("AllGather", ALU.bypass, replica_groups=RG,
                                                     ins=[xh_src[:]], outs=[xh_dst[:]]),
              [("xh_src",)], [("xh_dst",)], cc=True)
        dma("sp", hgat[:], xh_dst.rearrange("(j p) c -> p j c", p=128), [("xh_dst",)], [("hgat",)])
        if not apply_now:
            return
        halo_apply()

    def halo_apply():
        P.label = "halo_x"
        ht2 = halo_t[:].rearrange("p k t -> p (k t)")
        ts(ht2, hgat[:, 0, :], msk[:, 2:3], None, ALU.mult, None, [("hgat",), CONST], [("halo_t",)])
        for j in range(1, 4):
            stt(ht2, hgat[:, j, :], msk[:, 2 + j:3 + j], ht2, ALU.mult, ALU.add, [("hgat",), ("halo_t",), CONST],
                [("halo_t",)])
        stt(h[:, :, 13:16], h[:, :, 13:16], msk[:, 0:1], halo_t[:], ALU.mult, ALU.add,
            [("halo_t",), CONST] + [hkey(k, 0) for k in range(8)], [hkey(k, 0) for k in range(8)])

    def state_exchange(l):
        P.label = "state_x"
        tt(alog[:], rsum[:, 0:2], cng[:, l, :, 0], ALU.mult, [("rsum",), ("cng",)], [("alog",)])
        dma("sp", st_src[:], STT[:], ALLST, [("st_src",)])
        P.add("pool", lambda e: e.collective_compute("AllGather", ALU.bypass, replica_groups=RG,
                                                     ins=[st_src[:]], outs=[st_dst[:]]),
              [("st_src",)], [("st_dst",)], cc=True)
        SI = ("stin",)
        P.add("dve", lambda e: e.memset(STIN[:], 0.0), [], [SI])
        G = xin[:, 0:NST]
        GK = ("xo",)
        for j in range(3):
            dma("sp", G, st_dst[j * 128:(j + 1) * 128, :], [("st_dst",)], [GK])
            fs = msk[:, 6 + j:7 + j]
            for hh in range(4):
                cs = slice(hh * 128, (hh + 1) * 128)
                stt(G[:, cs], STIN[:, cs], G[:, 512 + hh:513 + hh], G[:, cs], ALU.mult, ALU.add, [SI, GK], [GK])
            stt(G[:, 516:580], STIN[:, 516:580], cfa("g2048"), G[:, 516:580], ALU.mult, ALU.add, [SI, GK, CONST], [GK])
            act(G[:, 582:584], G[:, 582:584], AF.Exp, [GK], [GK])
            tt(G[:, 582:584], G[:, 582:584], STIN[:, 580:582], ALU.mult, [GK, SI], [GK])
            tt(G[:, 580:582], G[:, 580:582], G[:, 582:584], ALU.add, [GK], [GK])
            tt(G[:, 0:582], G[:, 0:582], STIN[:, 0:582], ALU.subtract, [GK, SI], [GK])
            stt(STIN[:, 0:582], G[:, 0:582], fs, STIN[:, 0:582], ALU.mult, ALU.add, [GK, SI, CONST], [SI])
        cp(SBk[:, 0:512], STIN[:, 0:512], [SI], [("SBk",)], eng="act")
        cp(SBk[:, 512:576], STIN[:, 516:580], [SI], [("SBk",)], eng="act")

    def mixer_pass(l, so):
        def prologue(_):
            for j in range(2):
                P.add("dve", lambda e, j=j: e.memset(lxs[j][:, 0:3], 0.0), [], [("lxs", j)])
        stage(None, prologue)
        for su in range(2):
            if fused and not so:
                stage(None, lambda _, su=su: hg_main(l, su))
                stage(None, lambda _, su=su: ret_main(l, su))
                stage(None, lambda _, su=su: lru_main(l, su))
            else:
                stage(None, lambda _, su=su: norm_phase(lambda k: ppc(l, k), su))
                for hh in range(4):
                    stage(lambda hh=hh: load_w(winx[l], hh * 512, 512, wu_next(1)),
                          lambda Wk, hh=hh, su=su: hg_group(l, hh, su, Wk, so))
                stage(lambda: load_w(winx[l], 2048, 1024, wu_next(2)), lambda Wk, su=su: ret_group(l, su, Wk, so))
                if fused and so and su == 0:
                    stage(None, lambda _: (halo_apply(), norm_phase(lambda k: ppc(l, k), 0, tiles=[0])))
                stage(lambda: load_w(winx[l], 3072, 512, wu_next(1)), lambda Wk, su=su: lru_group(l, su, Wk, so))
            if so:
                continue
            if dbg == ("mg", l) and su == 0:
                stage(None, lambda _: dma("pool", dbgd[:, :, 0:1040], mg[:],
                                          [("mg", k, ti) for k in range(8) for ti in range(3)], [("dbg",)]))
            wout_phase(l, su)

    stage(None, lambda _: reset_states())
    for l in range(nlayers):
        stage(None, lambda _, l=l: dma("pool", lrug[:], lrugd[l * 4:(l + 1) * 4].rearrange("g p c -> p g c"), [],
                                       [("lrug",)]))
        if fused:
            stage(None, lambda _, l=l: halo_exchange(l, 0, apply_now=False))
            mixer_pass(l, True)
            stage(None, lambda _, l=l: (state_exchange(l), reset_states()))
        else:
            def pre(_, l=l):
                dma("sp", STIN[:], st_in[l], [], [("stin",)])
                halo_dump(l, 0)
                halo_inject(l, 0)
            stage(None, pre)
        mixer_pass(l, False)

        def mid(_, l=l):
            if dbg == ("hmid", l):
                dump_dbg(h[:], [hkey(k, ti) for k in range(8) for ti in range(5)])
            if not fused:
                dma("sp", st_out[l], STT[:], ALLST, [("st_out", l)])
            reset_states()
            if fused:
                halo_exchange(l, 1)
            else:
                halo_dump(l, 1)
                halo_inject(l, 1)
            P.add("dve", lambda e: e.memset(uhalo[:], 0.0), [], [("uhalo",)])
        stage(None, mid)
        for su in range(2):
            stage(None, lambda _, l=l, su=su: norm_phase(lambda k: ppc(l, 8 + k), su))
            ffn_phase(l, su)
        if dbg == ("h", l):
            stage(None, lambda _: dump_dbg(h[:], [hkey(k, ti) for k in range(8) for ti in range(5)]))
    run_stages()

    fg = lambda k: pp[:, NL * PL + k: NL * PL + k + 1]
    for ti in range(1, 5):
        t0, W = TILES[ti]
        pst, pk = ps_next()
        for k in range(8):
            act(sq[:, k % 4, 0:W], h[:, k, t0:t0 + W], AF.Square, [hkey(k, ti)], [("sq", k % 4)])
            mm(pst[:, 0:W], cba("ones1024"), sq[:, k % 4, 0:W], k == 0, k == 7, [("sq", k % 4), CONST], [pk])
        rstd, rk = f_next()
        act(rstd[:, 0:W], pst[:, 0:W], AF.Ln, [pk], [rk], bias=EPS)
        act(rstd[:, 0:W], rstd[:, 0:W], AF.Exp, [rk], [rk], scale=-0.5)
        for k in range(8):
            stt(h[:, k, t0:t0 + W], h[:, k, t0:t0 + W], fg(k), rstd[:, 0:W], ALU.mult, ALU.mult,
                [hkey(k, ti), rk, CONST], [hkey(k, ti)])
        for tb in range(4):
            tt0 = t0 + tb * 128
            for kg in range(2):
                pst2, pk2 = ps_next()
                for kk in range(4):
                    k = kg * 4 + kk
                    tr(pst2[:, kk * 128:(kk + 1) * 128], h[:, k, tt0:tt0 + 128], ident, [hkey(k, ti), CONST], [pk2])
                cp(osb[:, kg * 512:(kg + 1) * 512], pst2[:, :], [pk2], [("xo",)], eng="act" if kg else "dve")
            dma("sp", outd[tt0 - PRE: tt0 - PRE + 128, :], osb[:], [("xo",)], [("out", ti, tb)])
    fin_reads = [("out", ti, tb) for ti in range(1, 5) for tb in range(4)]
    if not fused:
        fin_reads += [("halo_out", i) for i in range(2 * nlayers)]
        fin_reads += [("st_out", l) for l in range(nlayers)]
    if dbg is not None:
        fin_reads.append(("dbg",))
    P.add("sp", None, fin_reads, [])
    nops, nwait = P.emit()
    _CACHE["last_ops"] = [(o["eng"], o["label"]) for o in P.ops if o["fn"] is not None]
    return nc, nops, nwait


FUSED = True


def _get_program(fused):
    key = ("nc", fused)
    if key not in _CACHE:
        _CACHE[key] = build_program(fused=fused)[0]
    return _CACHE[key]


def prepare_static(inp):
    cfd, cbd = make_consts()
    cstf = np.concatenate([cfd[n] for n, _ in CF_ORDER], axis=1).astype(np.float32)
    cstb = np.concatenate([cbd[n] for n, _ in CB_ORDER], axis=1).astype(np.float32)
    perm = winx_perm()
    st = {
        "pp": pack_params(inp),
        "cstf": np.ascontiguousarray(cstf),
        "cstb": np.ascontiguousarray(cstb),
        "w_in_x": np.ascontiguousarray(inp["w_in"][:, :, perm]),
        "w_out": np.ascontiguousarray(inp["w_out"]),
        "w_up": np.ascontiguousarray(inp["ffn_w_up"]),
        "w_down": np.ascontiguousarray(inp["ffn_w_down"]),
        "lrug": np.ascontiguousarray(pack_lrug(inp).reshape(NL * 4, 128, 128)),
    }
    return st


def seg_inputs(inp, st, b, s, states=None):
    x = inp["x"]
    xs = np.zeros((LT, D), np.float32)
    if s == 0:
        xs[:PRE] = inp["meta_tokens"]
    xs[PRE:] = x[b, s * NR:(s + 1) * NR]
    pos = np.arange(LT, dtype=np.float32) + s * NR
    m = np.zeros((128, 9), np.float32)
    m[:, 0] = 1.0 if s == 0 else 0.0
    m[:, 1] = 0.0 if s == 0 else 1.0
    if s > 0:
        m[:, 2 + (s - 1)] = 1.0
    for j in range(3):
        m[:, 6 + j] = 1.0 if j < s else 0.0
    d = dict(st)
    d["xs"] = xs
    d["rope"] = rope_tables(pos)
    d["wqseg"] = _WQSEG
    d["msk"] = m
    if states is not None:
        d.update(states)
    return d


def zero_states():
    return {
        "st_in": np.zeros((NL, 128, 584), np.float32),
        "halo_in": np.zeros((NL * 2, 128, 24), np.float32),
    }


def kernel(**inputs):
    inp = {k: np.asarray(v) for k, v in inputs.items()}
    st = prepare_static(inp)
    out = np.zeros((2, NSEG * NR, D), np.float32)
    if FUSED:
        nc = _get_program(True)
        in_maps = [seg_inputs(inp, st, r // NSEG, r % NSEG) for r in range(8)]
        res = run_bass_kernel_spmd(nc, in_maps, core_ids=list(range(8)))
        for r in range(8):
            b, s = r // NSEG, r % NSEG
            out[b, s * NR:(s + 1) * NR] = res.results[r]["out"]
        return out
    nc = _get_program(False)
    states = [zero_states(), zero_states()]
    for s in range(NSEG):
        in_maps = [seg_inputs(inp, st, b, s, states[b]) for b in range(2)]
        res = run_bass_kernel_spmd(nc, in_maps, core_ids=[0, 1])
        for b in range(2):
            r = res.results[b]
            out[b, s * NR:(s + 1) * NR] = r["out"]
            states[b] = {"st_in": r["st_out"], "halo_in": r["halo_out"]}
    return out
```

```python
import numpy as np
import concourse.bass as bass
import concourse.mybir as mybir
from concourse.bass_utils import run_bass_kernel_spmd

F32 = mybir.dt.float32
BF16 = mybir.dt.bfloat16
AF = mybir.ActivationFunctionType
ALU = mybir.AluOpType

D = 1024
NL = 4
PRE = 16
NR = 2048
LT = PRE + NR
NSEG = 4
DFF = 2816
NJ = DFF // 128
EPS = 1e-6
WINX = 3584
PL = 214
NPP = NL * PL + 8
TILES = [(0, 16)] + [(16 + 512 * i, 512) for i in range(4)]
SUPERS = [[0, 1, 2], [3, 4]]
SUP_COLS = [(0, 1040), (1040, 1024)]
FFN_GROUPS = [(0, 8), (8, 16), (16, 22)]
SEM_MAX = 60000
_CACHE = {}


def chunks_of(W):
    if W <= 64:
        return [(0, W)]
    return [(i * 64, 64) for i in range(W // 64)]


class Prog:
    def __init__(self, nc):
        self.nc = nc
        self.ops = []
        self.label = ""

    def add(self, eng, fn, reads=(), writes=(), dma=False, cc=False):
        self.ops.append({"eng": eng, "fn": fn, "r": tuple(reads), "w": tuple(writes), "dma": dma or cc,
                         "inc": False, "deps": (), "cc": cc, "label": self.label})

    def emit(self):
        nc = self.nc
        ops = self.ops
        engobj = {"pe": nc.tensor, "act": nc.scalar, "dve": nc.vector, "pool": nc.gpsimd, "sp": nc.sync}
        lastw = {}
        readers = {}
        for i, op in enumerate(ops):
            deps = set()
            raw = set()
            for r in op["r"]:
                if r in lastw:
                    deps.add(lastw[r])
                    raw.add(lastw[r])
            for w in op["w"]:
                if w in lastw:
                    deps.add(lastw[w])
                deps.update(readers.get(w, ()))
            need = []
            for j in deps:
                oj = ops[j]
                if (not oj["dma"]) and (not op["dma"]) and oj["eng"] == op["eng"]:
                    if not (j in raw and op["eng"] != "pe"):
                        continue
                need.append(j)
            op["deps"] = need
            for j in need:
                ops[j]["inc"] = True
            for r in op["r"]:
                readers.setdefault(r, []).append(i)
            for w in op["w"]:
                lastw[w] = i
                readers[w] = []
        cnt = {}
        esems = {}
        NDS = 6
        dsems = {}
        dslot_n = {}
        dslot_val = {}
        for i, op in enumerate(ops):
            if not op["inc"]:
                continue
            e = op["eng"]
            if op["cc"]:
                op["prev"] = (None, 0)
                op["sem"] = (nc.alloc_semaphore(f"cc_{i}"), 1)
            elif op["dma"]:
                if e not in dsems:
                    dsems[e] = [nc.alloc_semaphore(f"dq_{e}_{k}") for k in range(NDS)]
                    dslot_n[e] = 0
                    dslot_val[e] = [0] * NDS
                slot = dslot_n[e] % NDS
                dslot_n[e] += 1
                op["prev"] = (dsems[e][slot], dslot_val[e][slot])
                dslot_val[e][slot] += 16
                assert dslot_val[e][slot] < 65000
                op["sem"] = (dsems[e][slot], dslot_val[e][slot])
            else:
                c = cnt.get(e, 0)
                ep = c // SEM_MAX
                lst = esems.setdefault(e, [])
                while len(lst) <= ep:
                    lst.append(nc.alloc_semaphore(f"es_{e}_{len(lst)}"))
                op["sem"] = (lst[ep], c % SEM_MAX + 1)
                cnt[e] = c + 1
        waited = {}
        nwait = 0
        for i, op in enumerate(ops):
            e = op["eng"]
            E = engobj[e]
            best = {}
            for j in op["deps"]:
                s, v = ops[j]["sem"]
                k = id(s)
                if k not in best or best[k][1] < v:
                    best[k] = (s, v)
            if op["dma"] and op["inc"] and op["prev"][1] > 0:
                s, v = op["prev"]
                k = id(s)
                if k not in best or best[k][1] < v:
                    best[k] = (s, v)
            for k, (s, v) in best.items():
                if waited.get((e, k), 0) >= v:
                    continue
                E.wait_ge(s, v)
                waited[(e, k)] = v
                nwait += 1
            if op["fn"] is None:
                continue
            ins = op["fn"](E)
            if op["inc"]:
                s, v = op["sem"]
                ins.then_inc(s, 16 if (op["dma"] and not op["cc"]) else 1)
        return len(ops), nwait


def _gammas():
    return (1.0 - np.exp2(-5.0 - np.arange(4, dtype=np.float64)))


def make_consts():
    g = _gammas()
    cf = {}
    cf["ident"] = np.eye(128, dtype=np.float32)
    cm = np.ones((128, 512), np.float32)
    cm[:, ::64] = 0.0
    cf["cmask"] = cm
    t = np.arange(64)
    wq = np.zeros((128, 512), np.float64)
    for p in range(128):
        wq[p] = np.tile(g[p // 32] ** (t + 1.0), 8)
    cf["wq"] = wq.astype(np.float32)
    gg = np.zeros((128, 2), np.float64)
    for p in range(128):
        gg[p, 0] = g[p // 32] ** 64
        gg[p, 1] = g[p // 32] ** 16
    cf["g6416"] = gg.astype(np.float32)
    g2 = np.zeros((128, 1), np.float64)
    for p in range(128):
        g2[p, 0] = g[p // 32] ** 2048
    cf["g2048"] = g2.astype(np.float32)
    cb = {}
    cb["identb"] = np.eye(128, dtype=np.float32)
    cb["ones1024"] = np.full((128, 128), 1.0 / 1024, np.float32)
    cb["ones128"] = np.full((128, 128), 1.0 / 128, np.float32)
    bo = np.zeros((128, 128), np.float32)
    bo[:64, :64] = 1.0 / 64
    bo[64:, 64:] = 1.0 / 64
    cb["bones64"] = bo
    m = np.zeros((128, 512), np.float32)
    s = np.arange(64)[:, None]
    tt = np.arange(64)[None, :]
    m[:64] = np.tile((s <= tt).astype(np.float32), (1, 8))
    cb["maskT"] = m
    dm = np.zeros((128, 4 * 512), np.float64)
    for hd in range(4):
        blk = np.where(tt >= s, g[hd] ** np.maximum(tt - s, 0), 0.0) * 32 ** -0.5
        dm[:64, hd * 512:(hd + 1) * 512] = np.tile(blk, (1, 8))
    cb["dmatT"] = dm.astype(np.float32)
    ws = np.zeros((128, 1024), np.float64)
    ws16 = np.zeros((128, 128), np.float64)
    for hd in range(4):
        col = (g[hd] ** (63.0 - np.arange(64))) * 32 ** -0.5
        col16 = (g[hd] ** (15.0 - np.arange(16))) * 32 ** -0.5
        for c in range(8):
            ws[:64, c * 128 + hd * 32: c * 128 + hd * 32 + 32] = col[:, None]
        ws16[:16, hd * 32: hd * 32 + 32] = col16[:, None]
    cb["wst"] = ws.astype(np.float32)
    cb["wst16"] = ws16.astype(np.float32)
    return cf, cb


CF_ORDER = [("ident", 128), ("cmask", 512), ("wq", 512), ("g6416", 2), ("g2048", 1)]
CB_ORDER = [("identb", 128), ("ones1024", 128), ("ones128", 128), ("bones64", 128), ("maskT", 512),
            ("dmatT", 2048), ("wst", 1024), ("wst16", 128)]


def _offsets(order):
    o = {}
    c = 0
    for n, w in order:
        o[n] = (c, w)
        c += w
    return o, c


CF_OFF, NCF = _offsets(CF_ORDER)
CB_OFF, NCB = _offsets(CB_ORDER)


def rope_tables(pos):
    theta = 10000.0 ** (-np.linspace(0.0, 1.0, 16, dtype=np.float32))
    ang = (pos.astype(np.float32)[:, None] * theta[None, :]).astype(np.float32)
    cos = np.cos(ang).astype(np.float32)
    sin = np.sin(ang).astype(np.float32)
    out = np.zeros((2, 128, pos.shape[0]), np.float32)
    for hd in range(4):
        for dd in range(32):
            p = hd * 32 + dd
            out[0, p] = cos[:, dd % 16]
            out[1, p] = (-sin[:, dd] if dd < 16 else sin[:, dd - 16])
    return out


def make_wqseg():
    g = _gammas()
    out = np.zeros((128, LT), np.float64)
    t = np.arange(NR, dtype=np.float64)
    for p in range(128):
        out[p, PRE:] = g[p // 32] ** (t + 1.0)
    return out.astype(np.float32)


_WQSEG = make_wqseg()


def winx_perm():
    hq, hf, hi, hg, rq, rk, rv, rg, lx, ly = 0, 512, 1024, 1536, 2048, 2176, 2304, 2560, 2816, 3072
    cols = []
    for hh in range(4):
        r = np.arange(hh * 128, (hh + 1) * 128)
        cols += [hq + r, hg + r, hf + r, hi + r]
    sw = np.concatenate([np.concatenate([np.arange(h * 32 + 16, h * 32 + 32), np.arange(h * 32, h * 32 + 16)])
                         for h in range(4)])
    cols += [rq + np.arange(128), rq + sw, rg + np.arange(256), rk + np.arange(128), rk + sw, rv + np.arange(256)]
    cols += [lx + np.arange(256), ly + np.arange(256)]
    p = np.concatenate(cols)
    assert p.shape[0] == WINX
    return p


def pack_params(inp):
    pp = np.zeros((128, NPP), np.float32)
    for l in range(NL):
        b = l * PL
        pp[:, b + 0:b + 8] = inp["norm_mix_g"][l].reshape(8, 128).T
        pp[:, b + 8:b + 16] = inp["norm_ffn_g"][l].reshape(8, 128).T
        pp[:, b + 16:b + 20] = inp["hg_lb_logits"][l].reshape(4, 128).T
        pp[:, b + 20] = inp["hg_norm_g"][l]
        pp[:, b + 21] = np.tile(inp["ret_norm_g"][l], 2)
        cw = inp["lru_conv_w"][l]
        for j in range(2):
            for tap in range(4):
                pp[:, b + 22 + j * 4 + tap] = cw[tap, j * 128:(j + 1) * 128]
        pp[:, b + 30:b + 32] = inp["lru_conv_b"][l].reshape(2, 128).T
        pp[:, b + 32:b + 34] = inp["lru_gate_a_b"][l].reshape(2, 128).T
        pp[:, b + 34:b + 36] = inp["lru_gate_i_b"][l].reshape(2, 128).T
        pp[:, b + 36:b + 38] = inp["lru_lambda"][l].reshape(2, 128).T
        fw = inp["ffn_conv_w"][l]
        for c in range(44):
            for tap in range(3):
                pp[:, b + 38 + c * 3 + tap] = fw[tap, c * 128:(c + 1) * 128]
        pp[:, b + 170:b + 214] = inp["ffn_conv_b"][l].reshape(44, 128).T
    pp[:, NL * PL:NL * PL + 8] = inp["final_norm_g"].reshape(8, 128).T
    return pp


def pack_lrug(inp):
    out = np.zeros((NL, 2, 2, 128, 128), np.float32)
    for l in range(NL):
        for gi, nm in enumerate(("lru_gate_a_w", "lru_gate_i_w")):
            w = inp[nm][l]
            for j in range(2):
                for bb in range(2):
                    out[l, gi, j, bb * 64:(bb + 1) * 64, bb * 64:(bb + 1) * 64] = w[j * 2 + bb]
    return out


def build_program(nlayers=NL, fused=False, dbg=None):
    nc = bass.Bass("TRN2", target_bir_lowering=False)
    P = Prog(nc)

    def din(name, shape):
        return nc.dram_tensor(name, shape, F32, kind="ExternalInput").ap()

    def dout(name, shape):
        return nc.dram_tensor(name, shape, F32, kind="ExternalOutput").ap()

    xs = din("xs", [LT, D])
    ppd = din("pp", [128, NPP])
    cstf = din("cstf", [128, NCF])
    cstb = din("cstb", [128, NCB])
    roped = din("rope", [2, 128, LT])
    wqsegd = din("wqseg", [128, LT])
    mskd = din("msk", [128, 9])
    winx = din("w_in_x", [NL, D, WINX])
    woutd = din("w_out", [NL, D, D])
    wupd = din("w_up", [NL, D, 2 * DFF])
    wdnd = din("w_down", [NL, DFF, D])
    lrugd = din("lrug", [NL * 4, 128, 128])
    outd = dout("out", [NR, D])
    NST = 584
    if not fused:
        st_in = din("st_in", [NL, 128, NST])
        halo_in = din("halo_in", [NL * 2, 128, 24])
        st_out = dout("st_out", [NL, 128, NST])
        halo_out = dout("halo_out", [NL * 2, 128, 24])
    else:
        xh_src = nc.dram_tensor("xh_src", [128, 24], F32, kind="Internal").ap()
        xh_dst = nc.dram_tensor("xh_dst", [4 * 128, 24], F32, kind="Internal").ap()
        st_src = nc.dram_tensor("st_src", [128, NST], F32, kind="Internal").ap()
        st_dst = nc.dram_tensor("st_dst", [4 * 128, NST], F32, kind="Internal").ap()
        lscr = nc.dram_tensor("lscr", [6, 128, LT], F32, kind="Internal").ap()
        hscr_o = nc.dram_tensor("hscr_o", [4, 128, LT], F32, kind="Internal").ap()
        hscr_q = nc.dram_tensor("hscr_q", [4, 128, LT], BF16, kind="Internal").ap()
        hscr_g = nc.dram_tensor("hscr_g", [4, 128, LT], BF16, kind="Internal").ap()
        rscr_o = nc.dram_tensor("rscr_o", [2, 128, LT], F32, kind="Internal").ap()
        rscr_q = nc.dram_tensor("rscr_q", [128, LT], BF16, kind="Internal").ap()
        rscr_g = nc.dram_tensor("rscr_g", [2, 128, LT], BF16, kind="Internal").ap()
    dbgd = None
    if dbg is not None:
        dbgd = dout("dbg", [128, 8, LT])

    def sb(name, shape, dt=F32):
        return nc.alloc_sbuf_tensor(name, shape, dt)

    h = sb("h", [128, 8, LT])
    hn = sb("hn", [128, 8, 1040], BF16)
    mg = sb("mg", [128, 8, 1040], BF16)
    pp = sb("pp_sb", [128, NPP])
    der = sb("der", [128, 4, 4, 3])
    cng = sb("cng", [128, 4, 2, 2])
    cf = sb("cf", [128, NCF])
    cbt = sb("cbt", [128, NCB], BF16)
    msk = sb("msk_sb", [128, 9])
    ropec = sb("ropec", [128, 512])
    ropes = sb("ropes", [128, 512])
    lrug = sb("lrug_sb", [128, 4, 128], BF16)
    NWU = 4
    wb = [sb(f"wb{i}", [128, 4096], BF16) for i in range(NWU)]
    NFS = 8
    Fs = [sb(f"F{i}", [128, 512]) for i in range(NFS)]
    NBS = 8
    Bs = [sb(f"B{i}", [128, 512], BF16) for i in range(NBS)]
    TM1 = sb("TM1", [64, 2048], BF16)
    TM2 = sb("TM2", [64, 1024], BF16)
    SBk = sb("SBk", [128, 1024], BF16)
    sq = sb("sq", [128, 4, 512], BF16)
    STT = sb("STT", [128, 584])
    Shg = [STT[:, i * 128:(i + 1) * 128] for i in range(4)]
    Dhg = STT[:, 512:516]
    Rst = STT[:, 516:580]
    hl = STT[:, 580:582]
    alog = STT[:, 582:584]
    rsum = sb("rsum", [128, 4])
    STIN = sb("STIN", [128, 584])
    lxs = [sb(f"lxs{j}", [128, 515]) for j in range(2)]
    ubg = sb("ubg", [128, 514])
    ubv = sb("ubv", [128, 514])
    uhalo = sb("uhalo", [128, 44, 2])
    stin_hg = STIN[:, 0:512]
    stin_ret = STIN[:, 516:580]
    stin_lru = STIN[:, 580:582]
    hgat = sb("hgat", [128, 4, 24])
    halo_sb = sb("halo_sb", [128, 8, 3])
    halo_t = sb("halo_t", [128, 8, 3])
    xin = sb("xin", [128, 1024])
    osb = xin
    tiny = sb("tiny", [128, 64])
    gsc = sb("gsc", [128, 32])
    ret_qw = sb("ret_qw", [128, 512], BF16)
    ret_gs = [sb(f"ret_gs{j}", [128, 512], BF16) for j in range(2)]

    NPS = 6
    PS = [nc.alloc_psum_tensor(f"ps{i}", [128, 512], F32) for i in range(NPS)]
    PT = [nc.alloc_psum_tensor(f"pt{i}", [128, 1024], BF16) for i in range(2)]
    ctr = {"ps": 0, "pt": 0, "F": 0, "B": 0, "wu": 0}

    def ps_next():
        i = ctr["ps"] % NPS
        ctr["ps"] += 1
        return PS[i], ("ps", i)

    def pt_next():
        i = ctr["pt"] % 2
        ctr["pt"] += 1
        return PT[i], ("pt", i)

    def f_next():
        i = ctr["F"] % NFS
        ctr["F"] += 1
        return Fs[i], ("F", i)

    def b_next():
        i = ctr["B"] % NBS
        ctr["B"] += 1
        return Bs[i], ("B", i)

    def wu_next(n=1):
        res = []
        for _ in range(n):
            i = ctr["wu"] % NWU
            ctr["wu"] += 1
            res.append((wb[i], ("wu", i)))
        return res

    def cfa(name):
        o, w = CF_OFF[name]
        return cf[:, o:o + w]

    def cba(name):
        o, w = CB_OFF[name]
        return cbt[:, o:o + w]

    def ppc(l, c, n=1):
        return pp[:, l * PL + c: l * PL + c + n]

    def act(out, in_, func, reads, writes, bias=None, scale=None):
        kw = {}
        if bias is not None:
            kw["bias"] = bias
        if scale is not None:
            kw["scale"] = scale
        P.add("act", lambda e: e.activation(out=out, in_=in_, func=func, **kw), reads, writes)

    def tt(out, in0, in1, op, reads, writes, eng="dve"):
        P.add(eng, lambda e: e.tensor_tensor(out=out, in0=in0, in1=in1, op=op), reads, writes)

    def ts(out, in0, s1, s2, op0, op1, reads, writes, eng="dve"):
        if op1 is None:
            P.add(eng, lambda e: e.tensor_scalar(out=out, in0=in0, scalar1=s1, scalar2=None, op0=op0), reads, writes)
        else:
            P.add(eng, lambda e: e.tensor_scalar(out=out, in0=in0, scalar1=s1, scalar2=s2, op0=op0, op1=op1),
                  reads, writes)

    def stt(out, in0, scalar, in1, op0, op1, reads, writes, eng="dve"):
        P.add(eng, lambda e: e.scalar_tensor_tensor(out=out, in0=in0, scalar=scalar, in1=in1, op0=op0, op1=op1),
              reads, writes)

    def cp(out, in_, reads, writes, eng="dve"):
        if eng == "act":
            P.add("act", lambda e: e.activation(out=out, in_=in_, func=AF.Copy), reads, writes)
        else:
            P.add(eng, lambda e: e.tensor_copy(out=out, in_=in_), reads, writes)

    def mm(out, lhsT, rhs, start, stop, reads, writes, tp=None):
        if tp is None:
            P.add("pe", lambda e: e.matmul(out, lhsT, rhs, start=start, stop=stop), reads, writes)
        else:
            P.add("pe", lambda e: e.matmul(out, lhsT, rhs, start=start, stop=stop, tile_position=tp), reads, writes)

    def tr(out, in_, ident, reads, writes):
        P.add("pe", lambda e: e.transpose(out, in_, ident), reads, writes)

    def dma(q, out, in_, reads, writes):
        P.add(q, lambda e: e.dma_start(out=out, in_=in_), reads, writes, dma=True)

    CONST = ("const",)

    dma("sp", pp[:], ppd[:], [], [CONST])
    dma("sp", cf[:], cstf[:], [], [CONST])
    dma("pool", cbt[:], cstb[:], [], [CONST])
    dma("sp", msk[:], mskd[:], [], [CONST])
    P.add("dve", lambda e: e.memset(uhalo[:], 0.0), [], [("uhalo",)])
    P.add("dve", lambda e: e.memset(gsc[:], 0.0), [], [("gsc",)])
    for j in range(2):
        P.add("dve", lambda e, j=j: e.memset(lxs[j][:], 0.0), [], [("lxs", j)])

    def lg(l):
        return pp[:, l * PL + 16: l * PL + 20]
    mx = tiny[:, 0:4]
    ex = tiny[:, 4:20].rearrange("p (l h) -> p l h", l=4)
    ssum = tiny[:, 20:24]
    TK = ("tiny",)
    tt(mx, lg(0), lg(1), ALU.max, [CONST], [TK])
    tt(mx, mx, lg(2), ALU.max, [CONST, TK], [TK])
    tt(mx, mx, lg(3), ALU.max, [CONST, TK], [TK])
    for l in range(4):
        tt(ex[:, l, :], lg(l), mx, ALU.subtract, [CONST, TK], [TK])
    act(tiny[:, 4:20], tiny[:, 4:20], AF.Exp, [TK], [TK])
    tt(ssum, ex[:, 0, :], ex[:, 1, :], ALU.add, [TK], [TK])
    tt(ssum, ssum, ex[:, 2, :], ALU.add, [TK], [TK])
    tt(ssum, ssum, ex[:, 3, :], ALU.add, [TK], [TK])
    P.add("dve", lambda e: e.reciprocal(out=ssum, in_=ssum), [TK], [TK])
    for l in range(4):
        tt(ex[:, l, :], ex[:, l, :], ssum, ALU.mult, [TK], [TK])
    DK = ("der",)
    P.add("dve", lambda e: e.memset(der[:, 0, :, 0], 0.0), [], [DK])
    for l in range(1, 4):
        tt(der[:, l, :, 0], der[:, l - 1, :, 0], ex[:, l, :], ALU.add, [TK, DK], [DK])
    for l in range(4):
        ts(der[:, l, :, 1], der[:, l, :, 0], -1.0, 1.0, ALU.mult, ALU.add, [DK], [DK])
        ts(der[:, l, :, 2], der[:, l, :, 0], 1.0, None, ALU.subtract, None, [DK], [DK])
    CK = ("cng",)
    for l in range(4):
        lam = pp[:, l * PL + 36: l * PL + 38]
        act(tiny[:, 32:34], lam, AF.Exp, [CONST, TK], [TK], scale=-1.0)
        act(tiny[:, 34:36], tiny[:, 32:34], AF.Ln, [TK], [TK], bias=1.0)
        ts(cng[:, l, :, 0], tiny[:, 34:36], -8.0, None, ALU.mult, None, [TK], [CK])
        ts(cng[:, l, :, 1], tiny[:, 34:36], -16.0, None, ALU.mult, None, [TK], [CK])

    ident = cfa("ident")
    blocks = [(0, 16)] + [(16 + 128 * i, 128) for i in range(16)]

    def hkey(k, ti):
        return ("h", k, ti)

    def tile_of(t0):
        for ti, (a, w) in enumerate(TILES):
            if a <= t0 < a + w:
                return ti
        raise ValueError

    for (t0, nt) in blocks:
        ti = tile_of(t0)
        dma("sp", xin[0:nt, :], xs[t0:t0 + nt, :], [], [("xo",)])
        for kg in range(2):
            pst, pk = ps_next()
            for kk in range(4):
                k = kg * 4 + kk
                tr(pst[:, kk * 128: kk * 128 + nt], xin[0:nt, k * 128:(k + 1) * 128], ident[0:nt, 0:nt],
                   [("xo",), CONST], [pk])
            cp(h[:, kg * 4:(kg + 1) * 4, t0:t0 + nt],
               pst[:, :].rearrange("p (k t) -> p k t", k=4)[:, :, 0:nt],
               [pk], [hkey(k, ti) for k in range(kg * 4, kg * 4 + 4)], eng="act" if kg else "dve")

    if dbg == ("x", 0):
        dma("sp", dbgd[:], h[:], [hkey(k, ti) for k in range(8) for ti in range(5)], [("dbg",)])
    def halo_inject(l, sub):
        idx = l * 2 + sub
        dma("sp", halo_sb[:].rearrange("p k t -> p (k t)"), halo_in[idx], [], [("halo_sb",)])
        ts(halo_t[:], halo_sb[:], msk[:, 1:2], None, ALU.mult, None, [("halo_sb",), CONST], [("halo_t",)])
        stt(h[:, :, 13:16], h[:, :, 13:16], msk[:, 0:1], halo_t[:], ALU.mult, ALU.add,
            [("halo_t",), CONST] + [hkey(k, 0) for k in range(8)], [hkey(k, 0) for k in range(8)])

    def halo_dump(l, sub):
        idx = l * 2 + sub
        dma("sp", halo_out[idx].rearrange("p (k t) -> p k t", k=8), h[:, :, LT - 3:LT],
            [hkey(k, 4) for k in range(8)], [("halo_out", idx)])

    def norm_phase(gcol_fn, su, tiles=None):
        P.label = "norm"
        c_base = SUP_COLS[su][0]
        for ti in (SUPERS[su] if tiles is None else tiles):
            t0, W = TILES[ti]
            c0 = t0 - c_base
            pst, pk = ps_next()
            for k in range(8):
                act(sq[:, k % 4, 0:W], h[:, k, t0:t0 + W], AF.Square, [hkey(k, ti)], [("sq", k % 4)])
                mm(pst[:, 0:W], cba("ones1024"), sq[:, k % 4, 0:W], k == 0, k == 7, [("sq", k % 4), CONST], [pk])
            rstd, rk = f_next()
            act(rstd[:, 0:W], pst[:, 0:W], AF.Ln, [pk], [rk], bias=EPS)
            act(rstd[:, 0:W], rstd[:, 0:W], AF.Exp, [rk], [rk], scale=-0.5)
            for k in range(8):
                stt(hn[:, k, c0:c0 + W], h[:, k, t0:t0 + W], gcol_fn(k), rstd[:, 0:W], ALU.mult, ALU.mult,
                    [hkey(k, ti), rk, CONST], [("hn", k, ti)])

    def load_w(dram2d, c0, ncols, units, row0=0, nrows=D):
        nk = nrows // 128
        per_unit = 4096 // ncols if ncols <= 4096 else 0
        views = []
        src = dram2d[row0:row0 + nrows, c0:c0 + ncols].rearrange("(k p) c -> p k c", p=128)
        if nk * ncols <= 4096:
            wt, wk = units[0]
            dst = wt[:, 0:nk * ncols].rearrange("p (k c) -> p k c", k=nk)
            dma("pool", dst, src, [], [wk])
            return [(dst[:, k, :], wk) for k in range(nk)]
        kpu = 4096 // ncols
        res = []
        for ui in range((nk + kpu - 1) // kpu):
            wt, wk = units[ui]
            k0 = ui * kpu
            k1 = min(nk, k0 + kpu)
            dst = wt[:, 0:(k1 - k0) * ncols].rearrange("p (k c) -> p k c", k=k1 - k0)
            dma("pool", dst, src[:, k0:k1, :], [], [wk])
            for k in range(k0, k1):
                res.append((dst[:, k - k0, :], wk))
        return res

    def hn_cols(ti, su):
        t0, W = TILES[ti]
        c0 = t0 - SUP_COLS[su][0]
        return c0, W

    def proj_fm(Wk, col0, ti, su, ncols=128):
        c0, W = hn_cols(ti, su)
        pst, pk = ps_next()
        for k in range(8):
            wap, wk = Wk[k]
            mm(pst[0:ncols, 0:W], wap[:, col0:col0 + ncols], hn[:, k, c0:c0 + W], k == 0, k == 7,
               [wk, ("hn", k, ti)], [pk])
        return pst, pk

    def state_merge(state_ap, skey, in_ap, in_key, so=False):
        if so:
            ts(state_ap, state_ap, msk[:, 0:1], None, ALU.mult, None, [skey, CONST], [skey])
        else:
            stt(state_ap, state_ap, msk[:, 0:1], in_ap, ALU.mult, ALU.add, [skey, in_key, CONST], [skey])

    def hg_group(l, hh, su, Wk, so=False):
        P.label = "pre_hg" if so else "hg"
        lbc = der[:, l, hh, 0:1]
        omlc = der[:, l, hh, 1:2]
        nomlc = der[:, l, hh, 2:3]
        gn = ppc(l, 20)
        S = Shg[hh]
        SK = ("Shg", hh)
        for ti in SUPERS[su]:
            c0, W = hn_cols(ti, su)
            chs = chunks_of(W)
            nch = len(chs)
            full = (not so) or fused
            save = so and fused
            t0_ = TILES[ti][0]
            zf, zfk = proj_fm(Wk, 256, ti, su)
            vps = []
            for ci, (s0, cw) in enumerate(chs):
                if ci % 4 == 0:
                    vp, vk = ps_next()
                    vps.append((vp, vk))
                for k in range(8):
                    wap, wk = Wk[k]
                    mm(vp[0:cw, (ci % 4) * 128:(ci % 4 + 1) * 128], hn[:, k, c0 + s0:c0 + s0 + cw], wap[:, 384:512],
                       k == 0, k == 7, [wk, ("hn", k, ti)], [vk])
            sg, sgk = f_next()
            act(sg[:, 0:W], zf[:, 0:W], AF.Sigmoid, [zfk], [sgk])
            logf, lfk = f_next()
            act(logf[:, 0:W], sg[:, 0:W], AF.Ln, [sgk, ("der",)], [lfk], bias=lbc, scale=omlc)
            bT, btk = f_next()
            P.add("dve", lambda e, bT=bT, logf=logf, W=W: e.tensor_tensor_scan(
                out=bT[:, 0:W], data0=cfa("cmask")[:, 0:W], data1=logf[:, 0:W], initial=0.0,
                op0=ALU.mult, op1=ALU.add), [lfk, CONST], [btk])
            kT, ktk = f_next()
            ts(kT[:, 0:W], sg[:, 0:W], nomlc, omlc, ALU.mult, ALU.add, [sgk, ("der",)], [ktk])
            Ep, epk = f_next()
            act(Ep[:, 0:W], bT[:, 0:W], AF.Exp, [btk], [epk])
            nb, nbk = f_next()
            ts(nb[:, 0:W], bT[:, 0:W], -1.0, 80.0, ALU.mult, ALU.min, [btk], [nbk])
            En, enk = f_next()
            act(En[:, 0:W], nb[:, 0:W], AF.Exp, [nbk], [enk])
            kt, kttk = b_next()
            tt(kt[:, 0:W], kT[:, 0:W], En[:, 0:W], ALU.mult, [ktk, enk], [kttk])
            khT, khk = b_next()
            if W == 512:
                ep3 = Ep[:, 0:512].rearrange("p (c t) -> p c t", c=8)
                tt(khT[:, 0:512].rearrange("p (c t) -> p c t", c=8), kt[:, 0:512].rearrange("p (c t) -> p c t", c=8),
                   ep3[:, :, 63:64].broadcast_to([128, 8, 64]), ALU.mult, [kttk, epk], [khk])
            else:
                for (s0, cw) in chs:
                    ts(khT[:, s0:s0 + cw], kt[:, s0:s0 + cw], Ep[:, s0 + cw - 1:s0 + cw], None, ALU.mult, None,
                       [kttk, epk], [khk])
            for bi, (vp, vk) in enumerate(vps):
                n = min(4, nch - bi * 4)
                cwm = chs[0][1]
                cp(TM1[0:cwm, bi * 512: bi * 512 + n * 128], vp[0:cwm, 0:n * 128], [vk], [("TM1",)], eng="act")
            ptt, ptk = pt_next()
            for ci, (s0, cw) in enumerate(chs):
                tr(ptt[0:cw, ci * 128:(ci + 1) * 128], khT[:, s0:s0 + cw], cba("identb"), [khk, CONST], [ptk])
            cwm = chs[0][1]
            cp(TM2[0:cwm, 0:nch * 128], ptt[0:cwm, 0:nch * 128], [ptk], [("TM2",)], eng="dve")
            if full:
                zq, zqk = proj_fm(Wk, 0, ti, su)
                zg, zgk = proj_fm(Wk, 128, ti, su)
                qs, qsk = f_next()
                act(qs[:, 0:W], zq[:, 0:W], AF.Silu, [zqk], [qsk])
                gs, gsk = b_next()
                act(gs[:, 0:W], zg[:, 0:W], AF.Silu, [zgk], [gsk])
                qt, qtk = b_next()
                stt(qt[:, 0:W], qs[:, 0:W], 128.0 ** -0.5, Ep[:, 0:W], ALU.mult, ALU.mult, [qsk, epk], [qtk])
            if full:
                sc, sck = ps_next()
                for ci, (s0, cw) in enumerate(chs):
                    mm(sc[0:cw, ci * 64: ci * 64 + cw], kt[:, s0:s0 + cw], qt[:, s0:s0 + cw], True, True,
                       [kttk, qtk], [sck])
                AT, atk = b_next()
                wtot = (nch - 1) * 64 + chs[-1][1]
                stt(AT[0:cwm, 0:wtot], sc[0:cwm, 0:wtot], 1e30, cba("maskT")[0:cwm, 0:wtot], ALU.min, ALU.mult,
                    [sck, CONST], [atk])
            kvs = []
            for ci, (s0, cw) in enumerate(chs):
                if ci % 4 == 0:
                    kp, kk = ps_next()
                    kvs.append((kp, kk))
                mm(kp[:, (ci % 4) * 128:(ci % 4 + 1) * 128], TM2[0:cw, ci * 128:(ci + 1) * 128],
                   TM1[0:cw, ci * 128:(ci + 1) * 128], True, True, [("TM2",), ("TM1",)], [kk])
            XK = ("xo",)
            if full:
                cp(xin[:, 0:128], S[:], [SK], [XK], eng="dve")
            if save:
                qg, qgk = b_next()
                if ti == 0:
                    P.add("dve", lambda e, qg=qg, W=W: e.memset(qg[:, 0:W], 0.0), [], [qgk])
            for ci, (s0, cw) in enumerate(chs):
                kp, kk = kvs[ci // 4]
                dcol = Ep[:, s0 + cw - 1:s0 + cw]
                kv_ap = kp[:, (ci % 4) * 128:(ci % 4 + 1) * 128]
                if not full:
                    stt(S[:], S[:], dcol, kv_ap, ALU.mult, ALU.add, [SK, epk, kk], [SK])
                elif ci == nch - 1:
                    stt(S[:], xin[:, ci * 128:(ci + 1) * 128], dcol, kv_ap, ALU.mult, ALU.add, [XK, epk, kk], [SK])
                else:
                    stt(xin[:, (ci + 1) * 128:(ci + 2) * 128], xin[:, ci * 128:(ci + 1) * 128], dcol, kv_ap,
                        ALU.mult, ALU.add, [XK, epk, kk], [XK])
                if so and ti != 0 and not save:
                    ts(Dhg[:, hh:hh + 1], Dhg[:, hh:hh + 1], dcol, None, ALU.mult, None,
                       [("Dhg",), epk], [("Dhg",)])
            if save and ti != 0:
                GS = ("gsc",)
                cp(gsc[:, 0:1], Dhg[:, hh:hh + 1], [("Dhg",)], [GS], eng="dve")
                ep3 = Ep[:, 0:512].rearrange("p (c t) -> p c t", c=8)
                P.add("dve", lambda e, ep3=ep3, hh=hh: e.tensor_tensor_scan(
                    out=gsc[:, 1:9], data0=ep3[:, :, 63], data1=gsc[:, 16:24], initial=Dhg[:, hh:hh + 1],
                    op0=ALU.mult, op1=ALU.add), [epk, ("Dhg",), GS], [GS])
                cp(Dhg[:, hh:hh + 1], gsc[:, 8:9], [GS], [("Dhg",)], eng="dve")
                tt(qg[:, 0:512].rearrange("p (c t) -> p c t", c=8), qt[:, 0:512].rearrange("p (c t) -> p c t", c=8),
                   gsc[:, 0:8].rearrange("p (c o) -> p c o", o=1).broadcast_to([128, 8, 64]), ALU.mult,
                   [qtk, GS], [qgk])
            if full:
                cp(SBk[:, 0:nch * 128], xin[:, 0:nch * 128], [XK], [("SBk",)], eng="act")
            if ti == 0:
                state_merge(S[:], SK, stin_hg[:, hh * 128:(hh + 1) * 128], ("stin",), so)
            if not full:
                continue
            op_, ok = ps_next()
            for ci, (s0, cw) in enumerate(chs):
                mm(op_[:, s0:s0 + cw], TM1[0:cw, ci * 128:(ci + 1) * 128], AT[0:cw, ci * 64: ci * 64 + cw],
                   True, False, [("TM1",), atk], [ok])
                mm(op_[:, s0:s0 + cw], SBk[:, ci * 128:(ci + 1) * 128], qt[:, s0:s0 + cw], False, True,
                   [("SBk",), qtk], [ok])
            if save:
                ol, olk = f_next()
                cp(ol[:, 0:W], op_[:, 0:W], [ok], [olk], eng="act")
                dma("sp", hscr_o[hh][:, t0_:t0_ + W], ol[:, 0:W], [olk], [("hscr", 0, hh, ti)])
                dma("sp", hscr_q[hh][:, t0_:t0_ + W], qg[:, 0:W], [qgk], [("hscr", 1, hh, ti)])
                dma("sp", hscr_g[hh][:, t0_:t0_ + W], gs[:, 0:W], [gsk], [("hscr", 2, hh, ti)])
                continue
            osq, osk = b_next()
            act(osq[:, 0:W], op_[:, 0:W], AF.Square, [ok], [osk])
            msp, msk_ = ps_next()
            mm(msp[:, 0:W], cba("ones128"), osq[:, 0:W], True, True, [osk, CONST], [msk_])
            rstd, rk = f_next()
            act(rstd[:, 0:W], msp[:, 0:W], AF.Ln, [msk_], [rk], bias=EPS)
            act(rstd[:, 0:W], rstd[:, 0:W], AF.Exp, [rk], [rk], scale=-0.5)
            t1, t1k = f_next()
            stt(t1[:, 0:W], op_[:, 0:W], gn, rstd[:, 0:W], ALU.mult, ALU.mult, [ok, rk, CONST], [t1k])
            tt(mg[:, hh, c0:c0 + W], t1[:, 0:W], gs[:, 0:W], ALU.mult, [t1k, gsk], [("mg", hh, ti)])

    def hg_main(l, su):
        P.label = "hg"
        gn = ppc(l, 20)
        units = [(hh, ti) for hh in range(4) for ti in SUPERS[su]]

        def stage_a(hh, ti):
            c0, W = hn_cols(ti, su)
            t0_ = TILES[ti][0]
            d = dict(hh=hh, ti=ti, c0=c0, W=W)
            d["ol"], d["olk"] = f_next()
            dma("sp", d["ol"][:, 0:W], hscr_o[hh][:, t0_:t0_ + W], [("hscr", 0, hh, ti)], [d["olk"]])
            qg, qgk = b_next()
            dma("sp", qg[:, 0:W], hscr_q[hh][:, t0_:t0_ + W], [("hscr", 1, hh, ti)], [qgk])
            d["gs"], d["gsk"] = b_next()
            dma("sp", d["gs"][:, 0:W], hscr_g[hh][:, t0_:t0_ + W], [("hscr", 2, hh, ti)], [d["gsk"]])
            cps, ck = ps_next()
            mm(cps[:, 0:W], SBk[:, hh * 128:(hh + 1) * 128], qg[:, 0:W], True, True, [("SBk",), qgk], [ck])
            tt(d["ol"][:, 0:W], d["ol"][:, 0:W], cps[:, 0:W], ALU.add, [d["olk"], ck], [d["olk"]])
            d["osq"], d["osk"] = b_next()
            act(d["osq"][:, 0:W], d["ol"][:, 0:W], AF.Square, [d["olk"]], [d["osk"]])
            return d

        def stage_b(d):
            W, c0, hh, ti = d["W"], d["c0"], d["hh"], d["ti"]
            msp, mk = ps_next()
            mm(msp[:, 0:W], cba("ones128"), d["osq"][:, 0:W], True, True, [d["osk"], CONST], [mk])
            rstd, rk = f_next()
            act(rstd[:, 0:W], msp[:, 0:W], AF.Ln, [mk], [rk], bias=EPS)
            act(rstd[:, 0:W], rstd[:, 0:W], AF.Exp, [rk], [rk], scale=-0.5)
            stt(d["ol"][:, 0:W], d["ol"][:, 0:W], gn, rstd[:, 0:W], ALU.mult, ALU.mult, [d["olk"], rk, CONST],
                [d["olk"]])
            tt(mg[:, hh, c0:c0 + W], d["ol"][:, 0:W], d["gs"][:, 0:W], ALU.mult, [d["olk"], d["gsk"]],
               [("mg", hh, ti)])

        prev = None
        for (hh, ti) in units:
            cur = stage_a(hh, ti)
            if prev is not None:
                stage_b(prev)
            prev = cur
        stage_b(prev)

    def ret_group(l, su, Wk, so=False):
        P.label = "pre_ret" if so else "ret"
        gn = ppc(l, 21)
        RK = ("Rst",)
        for ti in SUPERS[su]:
            t0, W = TILES[ti]
            c0, _ = hn_cols(ti, su)
            chs = chunks_of(W)
            nch = len(chs)
            cwm = chs[0][1]
            full = (not so) or fused
            save = so and fused
            dma("sp", ropec[:, 0:W], roped[0][:, t0:t0 + W], [], [("ropec",)])
            dma("sp", ropes[:, 0:W], roped[1][:, t0:t0 + W], [], [("ropes",)])

            def rope_fm(colx, cols_):
                zx, zxk = proj_fm(Wk, colx, ti, su)
                zs, zsk = proj_fm(Wk, cols_, ti, su)
                a1, a1k = f_next()
                tt(a1[:, 0:W], zx[:, 0:W], ropec[:, 0:W], ALU.mult, [zxk, ("ropec",)], [a1k])
                a2, a2k = f_next()
                tt(a2[:, 0:W], zs[:, 0:W], ropes[:, 0:W], ALU.mult, [zsk, ("ropes",)], [a2k])
                r, rk_ = f_next()
                tt(r[:, 0:W], a1[:, 0:W], a2[:, 0:W], ALU.add, [a1k, a2k], [rk_])
                return r, rk_
            if full:
                qr, qrk = rope_fm(0, 128)
                qrb, qrbk = b_next()
                cp(qrb[:, 0:W], qr[:, 0:W], [qrk], [qrbk], eng="act")
                qw, qwk = ret_qw, ("ret_qw",)
                tt(qw[:, 0:W], qr[:, 0:W], cfa("wq")[:, 0:W], ALU.mult, [qrk, CONST], [qwk])
                if save:
                    wqt, wqtk = f_next()
                    dma("sp", wqt[:, 0:W], wqsegd[:, t0:t0 + W], [], [wqtk])
                    qwg, qwgk = b_next()
                    tt(qwg[:, 0:W], qr[:, 0:W], wqt[:, 0:W], ALU.mult, [qrk, wqtk], [qwgk])
                    dma("sp", rscr_q[:, t0:t0 + W], qwg[:, 0:W], [qwgk], [("rscr", "q", ti)])
            kr, krk = rope_fm(512, 640)
            krb, krbk = b_next()
            cp(krb[:, 0:W], kr[:, 0:W], [krk], [krbk], eng="act")
            gsl = []
            for j in range(2 if full else 0):
                zg, zgk = proj_fm(Wk, 256 + j * 128, ti, su)
                gs, gsk = ret_gs[j], ("ret_gs", j)
                act(gs[:, 0:W], zg[:, 0:W], AF.Silu, [zgk], [gsk])
                gsl.append((gs, gsk))
            for ci, (s0, cw) in enumerate(chs):
                if ci % 2 == 0:
                    vp, vk = ps_next()
                for k in range(8):
                    wap, wk = Wk[k]
                    mm(vp[0:cw, (ci % 2) * 256:(ci % 2 + 1) * 256], hn[:, k, c0 + s0:c0 + s0 + cw], wap[:, 768:1024],
                       k == 0, k == 7, [wk, ("hn", k, ti)], [vk])
                if ci % 2 == 1 or ci == nch - 1:
                    n = (ci % 2) + 1
                    cb0 = (ci - (ci % 2)) * 256
                    cp(TM1[0:cw, cb0: cb0 + n * 256], vp[0:cw, 0:n * 256], [vk], [("TM1",)], eng="act")
            ptt, ptk = pt_next()
            for ci, (s0, cw) in enumerate(chs):
                tr(ptt[0:cw, ci * 128:(ci + 1) * 128], krb[:, s0:s0 + cw], cba("identb"), [krbk, CONST], [ptk])
            wst_ap = cba("wst")[0:cwm, 0:nch * 128] if W == 512 else cba("wst16")[0:cwm, 0:128]
            tt(TM2[0:cwm, 0:nch * 128], ptt[0:cwm, 0:nch * 128], wst_ap, ALU.mult, [ptk, CONST], [("TM2",)])
            ATs = []
            wtot = (nch - 1) * 64 + chs[-1][1]
            for hd in range(4 if full else 0):
                sc, sck = ps_next()
                for ci, (s0, cw) in enumerate(chs):
                    mm(sc[0:cw, ci * 64: ci * 64 + cw], krb[32 * hd:32 * hd + 32, s0:s0 + cw],
                       qrb[32 * hd:32 * hd + 32, s0:s0 + cw], True, True, [krbk, qrbk], [sck], tp=(32 * hd, 0))
                AT, atk = b_next()
                tt(AT[0:cwm, 0:wtot], sc[0:cwm, 0:wtot], cba("dmatT")[0:cwm, hd * 512: hd * 512 + wtot], ALU.mult,
                   [sck, CONST], [atk])
                ATs.append((AT, atk))
            kvp, kvk = ps_next()
            for ci, (s0, cw) in enumerate(chs):
                for hd in range(4):
                    mm(kvp[32 * hd:32 * hd + 32, ci * 64:(ci + 1) * 64],
                       TM2[0:cw, ci * 128 + 32 * hd: ci * 128 + 32 * hd + 32],
                       TM1[0:cw, ci * 256 + 64 * hd: ci * 256 + 64 * hd + 64], True, True,
                       [("TM2",), ("TM1",)], [kvk], tp=(0, 32 * hd))
            gdec = cfa("g6416")[:, 0:1] if W == 512 else cfa("g6416")[:, 1:2]
            XK = ("xo",)
            if full:
                cp(xin[:, 0:64], Rst[:], [RK], [XK], eng="dve")
            for ci, (s0, cw) in enumerate(chs):
                kv_ap = kvp[:, ci * 64:(ci + 1) * 64]
                if not full:
                    stt(Rst[:], Rst[:], gdec, kv_ap, ALU.mult, ALU.add, [RK, kvk, CONST], [RK])
                elif ci == nch - 1:
                    stt(Rst[:], xin[:, ci * 64:(ci + 1) * 64], gdec, kv_ap, ALU.mult, ALU.add, [XK, kvk, CONST], [RK])
                else:
                    stt(xin[:, (ci + 1) * 64:(ci + 2) * 64], xin[:, ci * 64:(ci + 1) * 64], gdec, kv_ap,
                        ALU.mult, ALU.add, [XK, kvk, CONST], [XK])
            if full:
                cp(SBk[:, 0:nch * 64], xin[:, 0:nch * 64], [XK], [("SBk",)], eng="act")
            if ti == 0:
                state_merge(Rst[:], RK, stin_ret[:], ("stin",), so)
            if not full:
                continue
            for j in range(2):
                op_, ok = ps_next()
                for hh2 in range(2):
                    hd = j * 2 + hh2
                    po = 64 * hh2
                    AT, atk = ATs[hd]
                    for ci, (s0, cw) in enumerate(chs):
                        mm(op_[po:po + 64, s0:s0 + cw], TM1[0:cw, ci * 256 + 64 * hd: ci * 256 + 64 * hd + 64],
                           AT[0:cw, ci * 64: ci * 64 + cw], True, False, [("TM1",), atk], [ok], tp=(0, po))
                        mm(op_[po:po + 64, s0:s0 + cw], SBk[32 * hd:32 * hd + 32, ci * 64:(ci + 1) * 64],
                           qw[32 * hd:32 * hd + 32, s0:s0 + cw], False, True, [("SBk",), qwk], [ok], tp=(32 * hd, po))
                if save:
                    ol, olk = f_next()
                    cp(ol[:, 0:W], op_[:, 0:W], [ok], [olk], eng="act")
                    dma("sp", rscr_o[j][:, t0:t0 + W], ol[:, 0:W], [olk], [("rscr", "o", j, ti)])
                    gs, gsk = gsl[j]
                    dma("sp", rscr_g[j][:, t0:t0 + W], gs[:, 0:W], [gsk], [("rscr", "g", j, ti)])
                    continue
                ob, obk = b_next()
                cp(ob[:, 0:W], op_[:, 0:W], [ok], [obk], eng="act")
                of, ofk = f_next()
                cp(of[:, 0:W], op_[:, 0:W], [ok], [ofk], eng="act")
                mup, muk = ps_next()
                mm(mup[:, 0:W], cba("bones64"), ob[:, 0:W], True, True, [obk, CONST], [muk])
                dd, ddk = f_next()
                tt(dd[:, 0:W], of[:, 0:W], mup[:, 0:W], ALU.subtract, [ofk, muk], [ddk])
                dsq, dsk = b_next()
                act(dsq[:, 0:W], dd[:, 0:W], AF.Square, [ddk], [dsk])
                vp2, vk2 = ps_next()
                mm(vp2[:, 0:W], cba("bones64"), dsq[:, 0:W], True, True, [dsk, CONST], [vk2])
                rstd, rk = f_next()
                act(rstd[:, 0:W], vp2[:, 0:W], AF.Ln, [vk2], [rk], bias=EPS)
                act(rstd[:, 0:W], rstd[:, 0:W], AF.Exp, [rk], [rk], scale=-0.5)
                t1, t1k = f_next()
                stt(t1[:, 0:W], dd[:, 0:W], gn, rstd[:, 0:W], ALU.mult, ALU.mult, [ddk, rk, CONST], [t1k])
                gs, gsk = gsl[j]
                tt(mg[:, 4 + j, c0:c0 + W], t1[:, 0:W], gs[:, 0:W], ALU.mult, [t1k, gsk], [("mg", 4 + j, ti)])

    def ret_main(l, su):
        P.label = "ret"
        gn = ppc(l, 21)
        for ti in SUPERS[su]:
            t0, W = TILES[ti]
            c0, _ = hn_cols(ti, su)
            qwg, qwgk = b_next()
            dma("sp", qwg[:, 0:W], rscr_q[:, t0:t0 + W], [("rscr", "q", ti)], [qwgk])
            st = []
            for j in range(2):
                of, ofk = f_next()
                dma("sp", of[:, 0:W], rscr_o[j][:, t0:t0 + W], [("rscr", "o", j, ti)], [ofk])
                gs, gsk = b_next()
                dma("sp", gs[:, 0:W], rscr_g[j][:, t0:t0 + W], [("rscr", "g", j, ti)], [gsk])
                st.append(dict(j=j, of=of, ofk=ofk, gs=gs, gsk=gsk))
            for d in st:
                d["cps"], d["ck"] = ps_next()
                for hh2 in range(2):
                    hd = d["j"] * 2 + hh2
                    po = 64 * hh2
                    mm(d["cps"][po:po + 64, 0:W], SBk[32 * hd:32 * hd + 32, 512:576], qwg[32 * hd:32 * hd + 32, 0:W],
                       True, True, [("SBk",), qwgk], [d["ck"]], tp=(32 * hd, po))
            for d in st:
                tt(d["of"][:, 0:W], d["of"][:, 0:W], d["cps"][:, 0:W], ALU.add, [d["ofk"], d["ck"]], [d["ofk"]])
            for d in st:
                d["ob"], d["obk"] = b_next()
                cp(d["ob"][:, 0:W], d["of"][:, 0:W], [d["ofk"]], [d["obk"]], eng="act")
            for d in st:
                d["mup"], d["muk"] = ps_next()
                mm(d["mup"][:, 0:W], cba("bones64"), d["ob"][:, 0:W], True, True, [d["obk"], CONST], [d["muk"]])
            for d in st:
                d["dd"], d["ddk"] = f_next()
                tt(d["dd"][:, 0:W], d["of"][:, 0:W], d["mup"][:, 0:W], ALU.subtract, [d["ofk"], d["muk"]], [d["ddk"]])
            for d in st:
                d["dsq"], d["dsk"] = b_next()
                act(d["dsq"][:, 0:W], d["dd"][:, 0:W], AF.Square, [d["ddk"]], [d["dsk"]])
            for d in st:
                d["vp2"], d["vk2"] = ps_next()
                mm(d["vp2"][:, 0:W], cba("bones64"), d["dsq"][:, 0:W], True, True, [d["dsk"], CONST], [d["vk2"]])
            for d in st:
                d["rstd"], d["rk"] = f_next()
                act(d["rstd"][:, 0:W], d["vp2"][:, 0:W], AF.Ln, [d["vk2"]], [d["rk"]], bias=EPS)
            for d in st:
                act(d["rstd"][:, 0:W], d["rstd"][:, 0:W], AF.Exp, [d["rk"]], [d["rk"]], scale=-0.5)
            for d in st:
                stt(d["dd"][:, 0:W], d["dd"][:, 0:W], gn, d["rstd"][:, 0:W], ALU.mult, ALU.mult,
                    [d["ddk"], d["rk"], CONST], [d["ddk"]])
            for d in st:
                tt(mg[:, 4 + d["j"], c0:c0 + W], d["dd"][:, 0:W], d["gs"][:, 0:W], ALU.mult, [d["ddk"], d["gsk"]],
                   [("mg", 4 + d["j"], ti)])

    def lru_group(l, su, Wk, so=False):
        P.label = "pre_lru" if so else "lru"
        for ti in SUPERS[su]:
            c0, W = hn_cols(ti, su)
            for j in range(2):
                LK = ("lxs", j)
                t0_ = TILES[ti][0]
                zx, zxk = proj_fm(Wk, j * 128, ti, su)
                if (not so) or fused:
                    zy, zyk = proj_fm(Wk, 256 + j * 128, ti, su)
                cp(lxs[j][:, 3:3 + W], zx[:, 0:W], [zxk], [LK], eng="act")
                xc, xck = f_next()
                ts(xc[:, 0:W], lxs[j][:, 0:W], ppc(l, 22 + j * 4 + 0), ppc(l, 30 + j), ALU.mult, ALU.add,
                   [LK, CONST], [xck])
                for tap in range(1, 4):
                    stt(xc[:, 0:W], lxs[j][:, tap:tap + W], ppc(l, 22 + j * 4 + tap), xc[:, 0:W], ALU.mult, ALU.add,
                        [LK, xck, CONST], [xck])
                cp(lxs[j][:, 0:3], lxs[j][:, W:W + 3], [LK], [LK], eng="dve")
                xcb, xcbk = b_next()
                cp(xcb[:, 0:W], xc[:, 0:W], [xck], [xcbk], eng="act")
                gap, gak = ps_next()
                mm(gap[:, 0:W], lrug[:, 0 * 2 + j, :], xcb[:, 0:W], True, True, [xcbk, ("lrug",)], [gak])
                gip, gik = ps_next()
                mm(gip[:, 0:W], lrug[:, 1 * 2 + j, :], xcb[:, 0:W], True, True, [xcbk, ("lrug",)], [gik])
                r, rk_ = f_next()
                act(r[:, 0:W], gap[:, 0:W], AF.Sigmoid, [gak, CONST], [rk_], bias=ppc(l, 32 + j))
                ii, iik = f_next()
                act(ii[:, 0:W], gip[:, 0:W], AF.Sigmoid, [gik, CONST], [iik], bias=ppc(l, 34 + j))
                a, ak = f_next()
                act(a[:, 0:W], r[:, 0:W], AF.Exp, [rk_, ("cng",)], [ak], scale=cng[:, l, j, 0:1])
                a2, a2k = f_next()
                act(a2[:, 0:W], r[:, 0:W], AF.Exp, [rk_, ("cng",)], [a2k], scale=cng[:, l, j, 1:2])
                th, thk = f_next()
                act(th[:, 0:W], r[:, 0:W], AF.Tanh, [rk_, ("cng",)], [thk], scale=cng[:, l, j, 0:1])
                stt(a2[:, 0:W], a2[:, 0:W], 1.0, th[:, 0:W], ALU.add, ALU.mult, [a2k, thk], [a2k])
                act(a2[:, 0:W], a2[:, 0:W], AF.Sqrt, [a2k], [a2k], scale=-1.0)
                tt(ii[:, 0:W], ii[:, 0:W], xc[:, 0:W], ALU.mult, [iik, xck], [iik])
                tt(ii[:, 0:W], ii[:, 0:W], a2[:, 0:W], ALU.mult, [iik, a2k], [iik])
                hs, hsk = f_next()
                P.add("dve", lambda e, hs=hs, a=a, ii=ii, W=W, j=j: e.tensor_tensor_scan(
                    out=hs[:, 0:W], data0=a[:, 0:W], data1=ii[:, 0:W], initial=hl[:, j:j + 1],
                    op0=ALU.mult, op1=ALU.add), [ak, iik, ("hl",)], [hsk])
                cp(hl[:, j:j + 1], hs[:, W - 1:W], [hsk], [("hl",)], eng="dve")
                if ti == 0:
                    state_merge(hl[:, j:j + 1], ("hl",), stin_lru[:, j:j + 1], ("stin",), so)
                if so:
                    if ti != 0:
                        P.add("dve", lambda e, r=r, W=W, j=j: e.reduce_sum(
                            out=rsum[:, 2 + j:3 + j], in_=r[:, 0:W], axis=mybir.AxisListType.X), [rk_], [("rsum",)])
                        tt(rsum[:, j:j + 1], rsum[:, j:j + 1], rsum[:, 2 + j:3 + j], ALU.add, [("rsum",)], [("rsum",)])
                    if fused:
                        ge, gek = f_next()
                        act(ge[:, 0:W], zy[:, 0:W], AF.Gelu_apprx_tanh, [zyk], [gek])
                        dma("sp", lscr[0 + j][:, t0_:t0_ + W], a[:, 0:W], [ak], [("lscr", 0, j, ti)])
                        dma("sp", lscr[2 + j][:, t0_:t0_ + W], ii[:, 0:W], [iik], [("lscr", 1, j, ti)])
                        dma("sp", lscr[4 + j][:, t0_:t0_ + W], ge[:, 0:W], [gek], [("lscr", 2, j, ti)])
                    continue
                ge, gek = f_next()
                act(ge[:, 0:W], zy[:, 0:W], AF.Gelu_apprx_tanh, [zyk], [gek])
                tt(mg[:, 6 + j, c0:c0 + W], ge[:, 0:W], hs[:, 0:W], ALU.mult, [gek, hsk], [("mg", 6 + j, ti)])

    def lru_main(l, su):
        P.label = "lru"
        for ti in SUPERS[su]:
            c0, W = hn_cols(ti, su)
            t0_ = TILES[ti][0]
            for j in range(2):
                la, lak = f_next()
                dma("sp", la[:, 0:W], lscr[0 + j][:, t0_:t0_ + W], [("lscr", 0, j, ti)], [lak])
                lu, luk = f_next()
                dma("sp", lu[:, 0:W], lscr[2 + j][:, t0_:t0_ + W], [("lscr", 1, j, ti)], [luk])
                lg_, lgk = f_next()
                dma("sp", lg_[:, 0:W], lscr[4 + j][:, t0_:t0_ + W], [("lscr", 2, j, ti)], [lgk])
                hs, hsk = f_next()
                P.add("dve", lambda e, hs=hs, la=la, lu=lu, W=W, j=j: e.tensor_tensor_scan(
                    out=hs[:, 0:W], data0=la[:, 0:W], data1=lu[:, 0:W], initial=hl[:, j:j + 1],
                    op0=ALU.mult, op1=ALU.add), [lak, luk, ("hl",)], [hsk])
                cp(hl[:, j:j + 1], hs[:, W - 1:W], [hsk], [("hl",)], eng="dve")
                if ti == 0:
                    state_merge(hl[:, j:j + 1], ("hl",), stin_lru[:, j:j + 1], ("stin",), False)
                tt(mg[:, 6 + j, c0:c0 + W], lg_[:, 0:W], hs[:, 0:W], ALU.mult, [lgk, hsk], [("mg", 6 + j, ti)])

    STAGES = []

    def stage(load, comp):
        STAGES.append((load, comp))

    def run_stages():
        n = len(STAGES)
        loaded = {}
        nxt = 0

        def issue_next_load(after):
            nonlocal nxt
            nxt = max(nxt, after)
            while nxt < n and STAGES[nxt][0] is None:
                nxt += 1
            if nxt < n:
                loaded[nxt] = STAGES[nxt][0]()
                nxt += 1
        issue_next_load(0)
        for i in range(n):
            ld, comp = STAGES[i]
            if ld is None:
                comp(None)
                continue
            if i not in loaded:
                loaded[i] = ld()
                nxt = max(nxt, i + 1)
            issue_next_load(i + 1)
            comp(loaded.pop(i))

    def wout_phase(l, su):
        stage(lambda: load_w(woutd[l], 0, 1024, wu_next(2)), lambda Wk: wout_compute(l, su, Wk))

    def wout_compute(l, su, Wk):
        P.label = "wout"
        for ti in (SUPERS[su] if su == 0 else SUPERS[su][::-1]):
            t0, W = TILES[ti]
            c0, _ = hn_cols(ti, su)
            for m in range(8):
                pst, pk = ps_next()
                for k in range(8):
                    wap, wk = Wk[k]
                    mm(pst[:, 0:W], wap[:, m * 128:(m + 1) * 128], mg[:, k, c0:c0 + W], k == 0, k == 7,
                       [wk, ("mg", k, ti)], [pk])
                tt(h[:, m, t0:t0 + W], h[:, m, t0:t0 + W], pst[:, 0:W], ALU.add, [hkey(m, ti), pk], [hkey(m, ti)])

    def ffn_phase(l, su):
        for (j0, j1) in FFN_GROUPS:
            for jb in range(j0, j1, 4):
                nb_ = min(4, j1 - jb)

                def ld(jb=jb, nb_=nb_):
                    Wg = load_w(wupd[l], jb * 128, nb_ * 128, wu_next(1))
                    Wv = load_w(wupd[l], DFF + jb * 128, nb_ * 128, wu_next(1))
                    return (Wg, Wv)
                stage(ld, lambda W2, jb=jb, nb_=nb_, j0=j0: ffn_up_block(l, su, j0, jb, nb_, W2))
            stage(lambda j0=j0, j1=j1: load_w(wdnd[l], 0, 1024, wu_next(2), row0=j0 * 128, nrows=(j1 - j0) * 128),
                  lambda Wd, j0=j0, j1=j1: ffn_down(l, su, j0, j1, Wd))

    def ffn_up_block(l, su, j0, jb, nb_, W2):
        P.label = "ffn"
        Wg, Wv = W2
        for jq in range(nb_):
            j = jb + jq
            jj = j - j0
            cg, cv = j, 22 + j
            cp(ubg[:, 0:2], uhalo[:, cg, :], [("uhalo",)], [("ubg",)], eng="dve")
            cp(ubv[:, 0:2], uhalo[:, cv, :], [("uhalo",)], [("ubv",)], eng="dve")
            for ti in SUPERS[su]:
                c0, W = hn_cols(ti, su)
                ugp, ugk = proj_fm(Wg, jq * 128, ti, su)
                uvp, uvk = proj_fm(Wv, jq * 128, ti, su)
                cp(ubg[:, 2:2 + W], ugp[:, 0:W], [ugk], [("ubg",)], eng="act")
                cp(ubv[:, 2:2 + W], uvp[:, 0:W], [uvk], [("ubv",)], eng="act")
                cgt, cgk = f_next()
                cvt, cvk = f_next()
                pairs = ((ubg, ("ubg",), cg, cgt, cgk), (ubv, ("ubv",), cv, cvt, cvk))
                for (ub, ubk, cc, c_, ck) in pairs:
                    ts(c_[:, 0:W], ub[:, 0:W], ppc(l, 38 + cc * 3 + 0), ppc(l, 170 + cc), ALU.mult, ALU.add,
                       [ubk, CONST], [ck])
                for tap in (1, 2):
                    for (ub, ubk, cc, c_, ck) in pairs:
                        stt(c_[:, 0:W], ub[:, tap:tap + W], ppc(l, 38 + cc * 3 + tap), c_[:, 0:W], ALU.mult, ALU.add,
                            [ubk, ck, CONST], [ck])
                for (ub, ubk, cc, c_, ck) in pairs:
                    cp(ub[:, 0:2], ub[:, W:W + 2], [ubk], [ubk], eng="dve")
                act(cgt[:, 0:W], cgt[:, 0:W], AF.Silu, [cgk], [cgk])
                tt(mg[:, jj, c0:c0 + W], cgt[:, 0:W], cvt[:, 0:W], ALU.mult, [cgk, cvk], [("mg", jj, ti)])
            cp(uhalo[:, cg, :], ubg[:, 0:2], [("ubg",)], [("uhalo",)], eng="dve")
            cp(uhalo[:, cv, :], ubv[:, 0:2], [("ubv",)], [("uhalo",)], eng="dve")

    def ffn_down(l, su, j0, j1, Wd):
        P.label = "ffn_down"
        nj = j1 - j0
        last = (su == 1 and j1 == NJ)
        for ti in (SUPERS[su][::-1] if last else SUPERS[su]):
            t0, W = TILES[ti]
            c0, _ = hn_cols(ti, su)
            for m in range(8):
                pst, pk = ps_next()
                for jj in range(nj):
                    wap, wk = Wd[jj]
                    mm(pst[:, 0:W], wap[:, m * 128:(m + 1) * 128], mg[:, jj, c0:c0 + W], jj == 0, jj == nj - 1,
                       [wk, ("mg", jj, ti)], [pk])
                tt(h[:, m, t0:t0 + W], h[:, m, t0:t0 + W], pst[:, 0:W], ALU.add, [hkey(m, ti), pk],
                   [hkey(m, ti)])

    def dump_dbg(src3, reads):
        dma("sp", dbgd[:], src3, reads, [("dbg",)])

    ALLST = [("Shg", i) for i in range(4)] + [("Dhg",), ("Rst",), ("hl",), ("alog",)]
    RG = [[0, 1, 2, 3], [4, 5, 6, 7]]

    def reset_states():
        P.add("dve", lambda e: e.memset(STT[:], 0.0), [], ALLST)
        P.add("dve", lambda e: e.memset(Dhg[:], 1.0), [], [("Dhg",)])
        P.add("dve", lambda e: e.memset(rsum[:], 0.0), [], [("rsum",)])

    def halo_exchange(l, sub, apply_now=True):
        P.label = "halo_x"
        dma("sp", xh_src.rearrange("p (k t) -> p k t", k=8), h[:, :, LT - 3:LT],
            [hkey(k, 4) for k in range(8)], [("xh_src",)])
        P.add("pool", lambda e: e.collective_compute("AllGather", ALU.bypass, replica_groups=RG,
                                                     ins=[xh_src[:]], outs=[xh_dst[:]]),
              [("xh_src",)], [("xh_dst",)], cc=True)
        dma("sp", hgat[:], xh_dst.rearrange("(j p) c -> p j c", p=128), [("xh_dst",)], [("hgat",)])
        if not apply_now:
            return
        halo_apply()

    def halo_apply():
        P.label = "halo_x"
        ht2 = halo_t[:].rearrange("p k t -> p (k t)")
        ts(ht2, hgat[:, 0, :], msk[:, 2:3], None, ALU.mult, None, [("hgat",), CONST], [("halo_t",)])
        for j in range(1, 4):
            stt(ht2, hgat[:, j, :], msk[:, 2 + j:3 + j], ht2, ALU.mult, ALU.add, [("hgat",), ("halo_t",), CONST],
                [("halo_t",)])
        stt(h[:, :, 13:16], h[:, :, 13:16], msk[:, 0:1], halo_t[:], ALU.mult, ALU.add,
            [("halo_t",), CONST] + [hkey(k, 0) for k in range(8)], [hkey(k, 0) for k in range(8)])

    def state_exchange(l):
        P.label = "state_x"
        tt(alog[:], rsum[:, 0:2], cng[:, l, :, 0], ALU.mult, [("rsum",), ("cng",)], [("alog",)])
        dma("sp", st_src[:], STT[:], ALLST, [("st_src",)])
        P.add("pool", lambda e: e.collective_compute("AllGather", ALU.bypass, replica_groups=RG,
                                                     ins=[st_src[:]], outs=[st_dst[:]]),
              [("st_src",)], [("st_dst",)], cc=True)
        SI = ("stin",)
        P.add("dve", lambda e: e.memset(STIN[:], 0.0), [], [SI])
        G = xin[:, 0:NST]
        GK = ("xo",)
        for j in range(3):
            dma("sp", G, st_dst[j * 128:(j + 1) * 128, :], [("st_dst",)], [GK])
            fs = msk[:, 6 + j:7 + j]
            for hh in range(4):
                cs = slice(hh * 128, (hh + 1) * 128)
                stt(G[:, cs], STIN[:, cs], G[:, 512 + hh:513 + hh], G[:, cs], ALU.mult, ALU.add, [SI, GK], [GK])
            stt(G[:, 516:580], STIN[:, 516:580], cfa("g2048"), G[:, 516:580], ALU.mult, ALU.add, [SI, GK, CONST], [GK])
            act(G[:, 582:584], G[:, 582:584], AF.Exp, [GK], [GK])
            tt(G[:, 582:584], G[:, 582:584], STIN[:, 580:582], ALU.mult, [GK, SI], [GK])
            tt(G[:, 580:582], G[:, 580:582], G[:, 582:584], ALU.add, [GK], [GK])
            tt(G[:, 0:582], G[:, 0:582], STIN[:, 0:582], ALU.subtract, [GK, SI], [GK])
            stt(STIN[:, 0:582], G[:, 0:582], fs, STIN[:, 0:582], ALU.mult, ALU.add, [GK, SI, CONST], [SI])
        cp(SBk[:, 0:512], STIN[:, 0:512], [SI], [("SBk",)], eng="act")
        cp(SBk[:, 512:576], STIN[:, 516:580], [SI], [("SBk",)], eng="act")

    def mixer_pass(l, so):
        def prologue(_):
            for j in range(2):
                P.add("dve", lambda e, j=j: e.memset(lxs[j][:, 0:3], 0.0), [], [("lxs", j)])
        stage(None, prologue)
        for su in range(2):
            if fused and not so:
                stage(None, lambda _, su=su: hg_main(l, su))
                stage(None, lambda _, su=su: ret_main(l, su))
                stage(None, lambda _, su=su: lru_main(l, su))
            else:
                stage(None, lambda _, su=su: norm_phase(lambda k: ppc(l, k), su))
                for hh in range(4):
                    stage(lambda hh=hh: load_w(winx[l], hh * 512, 512, wu_next(1)),
                          lambda Wk, hh=hh, su=su: hg_group(l, hh, su, Wk, so))
                stage(lambda: load_w(winx[l], 2048, 1024, wu_next(2)), lambda Wk, su=su: ret_group(l, su, Wk, so))
                if fused and so and su == 0:
                    stage(None, lambda _: (halo_apply(), norm_phase(lambda k: ppc(l, k), 0, tiles=[0])))
                stage(lambda: load_w(winx[l], 3072, 512, wu_next(1)), lambda Wk, su=su: lru_group(l, su, Wk, so))
            if so:
                continue
            if dbg == ("mg", l) and su == 0:
                stage(None, lambda _: dma("pool", dbgd[:, :, 0:1040], mg[:],
                                          [("mg", k, ti) for k in range(8) for ti in range(3)], [("dbg",)]))
            wout_phase(l, su)

    stage(None, lambda _: reset_states())
    for l in range(nlayers):
        stage(None, lambda _, l=l: dma("pool", lrug[:], lrugd[l * 4:(l + 1) * 4].rearrange("g p c -> p g c"), [],
                                       [("lrug",)]))
        if fused:
            stage(None, lambda _, l=l: halo_exchange(l, 0, apply_now=False))
            mixer_pass(l, True)
            stage(None, lambda _, l=l: (state_exchange(l), reset_states()))
        else:
            def pre(_, l=l):
                dma("sp", STIN[:], st_in[l], [], [("stin",)])
                halo_dump(l, 0)
                halo_inject(l, 0)
            stage(None, pre)
        mixer_pass(l, False)

        def mid(_, l=l):
            if dbg == ("hmid", l):
                dump_dbg(h[:], [hkey(k, ti) for k in range(8) for ti in range(5)])
            if not fused:
                dma("sp", st_out[l], STT[:], ALLST, [("st_out", l)])
            reset_states()
            if fused:
                halo_exchange(l, 1, apply_now=False)
            else:
                halo_dump(l, 1)
                halo_inject(l, 1)
            P.add("dve", lambda e: e.memset(uhalo[:], 0.0), [], [("uhalo",)])
        stage(None, mid)
        for su in range(2):
            if fused and su == 0:
                def ffn_norm0(_, l=l):
                    norm_phase(lambda k: ppc(l, 8 + k), 0, tiles=[1, 2])
                    halo_apply()
                    norm_phase(lambda k: ppc(l, 8 + k), 0, tiles=[0])
                stage(None, ffn_norm0)
            else:
                stage(None, lambda _, l=l, su=su: norm_phase(lambda k: ppc(l, 8 + k), su))
            ffn_phase(l, su)
        if dbg == ("h", l):
            stage(None, lambda _: dump_dbg(h[:], [hkey(k, ti) for k in range(8) for ti in range(5)]))
    run_stages()

    fg = lambda k: pp[:, NL * PL + k: NL * PL + k + 1]
    for ti in range(1, 5):
        t0, W = TILES[ti]
        pst, pk = ps_next()
        for k in range(8):
            act(sq[:, k % 4, 0:W], h[:, k, t0:t0 + W], AF.Square, [hkey(k, ti)], [("sq", k % 4)])
            mm(pst[:, 0:W], cba("ones1024"), sq[:, k % 4, 0:W], k == 0, k == 7, [("sq", k % 4), CONST], [pk])
        rstd, rk = f_next()
        act(rstd[:, 0:W], pst[:, 0:W], AF.Ln, [pk], [rk], bias=EPS)
        act(rstd[:, 0:W], rstd[:, 0:W], AF.Exp, [rk], [rk], scale=-0.5)
        for k in range(8):
            stt(h[:, k, t0:t0 + W], h[:, k, t0:t0 + W], fg(k), rstd[:, 0:W], ALU.mult, ALU.mult,
                [hkey(k, ti), rk, CONST], [hkey(k, ti)])
        for tb in range(4):
            tt0 = t0 + tb * 128
            for kg in range(2):
                pst2, pk2 = ps_next()
                for kk in range(4):
                    k = kg * 4 + kk
                    tr(pst2[:, kk * 128:(kk + 1) * 128], h[:, k, tt0:tt0 + 128], ident, [hkey(k, ti), CONST], [pk2])
                cp(osb[:, kg * 512:(kg + 1) * 512], pst2[:, :], [pk2], [("xo",)], eng="act" if kg else "dve")
            dma("sp", outd[tt0 - PRE: tt0 - PRE + 128, :], osb[:], [("xo",)], [("out", ti, tb)])
    fin_reads = [("out", ti, tb) for ti in range(1, 5) for tb in range(4)]
    if not fused:
        fin_reads += [("halo_out", i) for i in range(2 * nlayers)]
        fin_reads += [("st_out", l) for l in range(nlayers)]
    if dbg is not None:
        fin_reads.append(("dbg",))
    P.add("sp", None, fin_reads, [])
    nops, nwait = P.emit()
    _CACHE["last_ops"] = [(o["eng"], o["label"]) for o in P.ops if o["fn"] is not None]
    return nc, nops, nwait


FUSED = True


def _get_program(fused):
    key = ("nc", fused)
    if key not in _CACHE:
        _CACHE[key] = build_program(fused=fused)[0]
    return _CACHE[key]


def prepare_static(inp):
    cfd, cbd = make_consts()
    cstf = np.concatenate([cfd[n] for n, _ in CF_ORDER], axis=1).astype(np.float32)
    cstb = np.concatenate([cbd[n] for n, _ in CB_ORDER], axis=1).astype(np.float32)
    perm = winx_perm()
    st = {
        "pp": pack_params(inp),
        "cstf": np.ascontiguousarray(cstf),
        "cstb": np.ascontiguousarray(cstb),
        "w_in_x": np.ascontiguousarray(inp["w_in"][:, :, perm]),
        "w_out": np.ascontiguousarray(inp["w_out"]),
        "w_up": np.ascontiguousarray(inp["ffn_w_up"]),
        "w_down": np.ascontiguousarray(inp["ffn_w_down"]),
        "lrug": np.ascontiguousarray(pack_lrug(inp).reshape(NL * 4, 128, 128)),
    }
    return st


def seg_inputs(inp, st, b, s, states=None):
    x = inp["x"]
    xs = np.zeros((LT, D), np.float32)
    if s == 0:
        xs[:PRE] = inp["meta_tokens"]
    xs[PRE:] = x[b, s * NR:(s + 1) * NR]
    pos = np.arange(LT, dtype=np.float32) + s * NR
    m = np.zeros((128, 9), np.float32)
    m[:, 0] = 1.0 if s == 0 else 0.0
    m[:, 1] = 0.0 if s == 0 else 1.0
    if s > 0:
        m[:, 2 + (s - 1)] = 1.0
    for j in range(3):
        m[:, 6 + j] = 1.0 if j < s else 0.0
    d = dict(st)
    d["xs"] = xs
    d["rope"] = rope_tables(pos)
    d["wqseg"] = _WQSEG
    d["msk"] = m
    if states is not None:
        d.update(states)
    return d


def zero_states():
    return {
        "st_in": np.zeros((NL, 128, 584), np.float32),
        "halo_in": np.zeros((NL * 2, 128, 24), np.float32),
    }


def kernel(**inputs):
    inp = {k: np.asarray(v) for k, v in inputs.items()}
    st = prepare_static(inp)
    out = np.zeros((2, NSEG * NR, D), np.float32)
    if FUSED:
        nc = _get_program(True)
        in_maps = [seg_inputs(inp, st, r // NSEG, r % NSEG) for r in range(8)]
        res = run_bass_kernel_spmd(nc, in_maps, core_ids=list(range(8)))
        for r in range(8):
            b, s = r // NSEG, r % NSEG
            out[b, s * NR:(s + 1) * NR] = res.results[r]["out"]
        return out
    nc = _get_program(False)
    states = [zero_states(), zero_states()]
    for s in range(NSEG):
        in_maps = [seg_inputs(inp, st, b, s, states[b]) for b in range(2)]
        res = run_bass_kernel_spmd(nc, in_maps, core_ids=[0, 1])
        for b in range(2):
            r = res.results[b]
            out[b, s * NR:(s + 1) * NR] = r["out"]
            states[b] = {"st_in": r["st_out"], "halo_in": r["halo_out"]}
    return out
```

```python
import numpy as np
import concourse.bass as bass
import concourse.mybir as mybir
from concourse.bass_utils import run_bass_kernel_spmd

F32 = mybir.dt.float32
BF16 = mybir.dt.bfloat16
AF = mybir.ActivationFunctionType
ALU = mybir.AluOpType

D = 1024
NL = 4
PRE = 16
NR = 2048
LT = PRE + NR
NSEG = 4
DFF = 2816
NJ = DFF // 128
EPS = 1e-6
WINX = 3584
PL = 214
NPP = NL * PL + 8
TILES = [(0, 16)] + [(16 + 512 * i, 512) for i in range(4)]
SUPERS = [[0, 1, 2], [3, 4]]
SUP_COLS = [(0, 1040), (1040, 1024)]
FFN_GROUPS = [(0, 8), (8, 16), (16, 22)]
SEM_MAX = 60000
_CACHE = {}


def chunks_of(W):
    if W <= 64:
        return [(0, W)]
    return [(i * 64, 64) for i in range(W // 64)]


class Prog:
    def __init__(self, nc):
        self.nc = nc
        self.ops = []
        self.label = ""

    def add(self, eng, fn, reads=(), writes=(), dma=False, cc=False):
        self.ops.append({"eng": eng, "fn": fn, "r": tuple(reads), "w": tuple(writes), "dma": dma or cc,
                         "inc": False, "deps": (), "cc": cc, "label": self.label})

    def emit(self):
        nc = self.nc
        ops = self.ops
        engobj = {"pe": nc.tensor, "act": nc.scalar, "dve": nc.vector, "pool": nc.gpsimd, "sp": nc.sync}
        lastw = {}
        readers = {}
        for i, op in enumerate(ops):
            deps = set()
            raw = set()
            for r in op["r"]:
                if r in lastw:
                    deps.add(lastw[r])
                    raw.add(lastw[r])
            for w in op["w"]:
                if w in lastw:
                    deps.add(lastw[w])
                deps.update(readers.get(w, ()))
            need = []
            for j in deps:
                oj = ops[j]
                if (not oj["dma"]) and (not op["dma"]) and oj["eng"] == op["eng"]:
                    if not (j in raw and op["eng"] != "pe"):
                        continue
                need.append(j)
            op["deps"] = need
            for j in need:
                ops[j]["inc"] = True
            for r in op["r"]:
                readers.setdefault(r, []).append(i)
            for w in op["w"]:
                lastw[w] = i
                readers[w] = []
        cnt = {}
        esems = {}
        NDS = 6
        dsems = {}
        dslot_n = {}
        dslot_val = {}
        for i, op in enumerate(ops):
            if not op["inc"]:
                continue
            e = op["eng"]
            if op["cc"]:
                op["prev"] = (None, 0)
                op["sem"] = (nc.alloc_semaphore(f"cc_{i}"), 1)
            elif op["dma"]:
                if e not in dsems:
                    dsems[e] = [nc.alloc_semaphore(f"dq_{e}_{k}") for k in range(NDS)]
                    dslot_n[e] = 0
                    dslot_val[e] = [0] * NDS
                slot = dslot_n[e] % NDS
                dslot_n[e] += 1
                op["prev"] = (dsems[e][slot], dslot_val[e][slot])
                dslot_val[e][slot] += 16
                assert dslot_val[e][slot] < 65000
                op["sem"] = (dsems[e][slot], dslot_val[e][slot])
            else:
                c = cnt.get(e, 0)
                ep = c // SEM_MAX
                lst = esems.setdefault(e, [])
                while len(lst) <= ep:
                    lst.append(nc.alloc_semaphore(f"es_{e}_{len(lst)}"))
                op["sem"] = (lst[ep], c % SEM_MAX + 1)
                cnt[e] = c + 1
        waited = {}
        nwait = 0
        for i, op in enumerate(ops):
            e = op["eng"]
            E = engobj[e]
            best = {}
            for j in op["deps"]:
                s, v = ops[j]["sem"]
                k = id(s)
                if k not in best or best[k][1] < v:
                    best[k] = (s, v)
            if op["dma"] and op["inc"] and op["prev"][1] > 0:
                s, v = op["prev"]
                k = id(s)
                if k not in best or best[k][1] < v:
                    best[k] = (s, v)
            for k, (s, v) in best.items():
                if waited.get((e, k), 0) >= v:
                    continue
                E.wait_ge(s, v)
                waited[(e, k)] = v
                nwait += 1
            if op["fn"] is None:
                continue
            ins = op["fn"](E)
            if op["inc"]:
                s, v = op["sem"]
                ins.then_inc(s, 16 if (op["dma"] and not op["cc"]) else 1)
        return len(ops), nwait


def _gammas():
    return (1.0 - np.exp2(-5.0 - np.arange(4, dtype=np.float64)))


def make_consts():
    g = _gammas()
    cf = {}
    cf["ident"] = np.eye(128, dtype=np.float32)
    cm = np.ones((128, 512), np.float32)
    cm[:, ::64] = 0.0
    cf["cmask"] = cm
    t = np.arange(64)
    wq = np.zeros((128, 512), np.float64)
    for p in range(128):
        wq[p] = np.tile(g[p // 32] ** (t + 1.0), 8)
    cf["wq"] = wq.astype(np.float32)
    gg = np.zeros((128, 2), np.float64)
    for p in range(128):
        gg[p, 0] = g[p // 32] ** 64
        gg[p, 1] = g[p // 32] ** 16
    cf["g6416"] = gg.astype(np.float32)
    g2 = np.zeros((128, 1), np.float64)
    for p in range(128):
        g2[p, 0] = g[p // 32] ** 2048
    cf["g2048"] = g2.astype(np.float32)
    cb = {}
    cb["identb"] = np.eye(128, dtype=np.float32)
    cb["ones1024"] = np.full((128, 128), 1.0 / 1024, np.float32)
    cb["ones128"] = np.full((128, 128), 1.0 / 128, np.float32)
    bo = np.zeros((128, 128), np.float32)
    bo[:64, :64] = 1.0 / 64
    bo[64:, 64:] = 1.0 / 64
    cb["bones64"] = bo
    m = np.zeros((128, 512), np.float32)
    s = np.arange(64)[:, None]
    tt = np.arange(64)[None, :]
    m[:64] = np.tile((s <= tt).astype(np.float32), (1, 8))
    cb["maskT"] = m
    dm = np.zeros((128, 4 * 512), np.float64)
    for hd in range(4):
        blk = np.where(tt >= s, g[hd] ** np.maximum(tt - s, 0), 0.0) * 32 ** -0.5
        dm[:64, hd * 512:(hd + 1) * 512] = np.tile(blk, (1, 8))
    cb["dmatT"] = dm.astype(np.float32)
    ws = np.zeros((128, 1024), np.float64)
    ws16 = np.zeros((128, 128), np.float64)
    for hd in range(4):
        col = (g[hd] ** (63.0 - np.arange(64))) * 32 ** -0.5
        col16 = (g[hd] ** (15.0 - np.arange(16))) * 32 ** -0.5
        for c in range(8):
            ws[:64, c * 128 + hd * 32: c * 128 + hd * 32 + 32] = col[:, None]
        ws16[:16, hd * 32: hd * 32 + 32] = col16[:, None]
    cb["wst"] = ws.astype(np.float32)
    cb["wst16"] = ws16.astype(np.float32)
    return cf, cb


CF_ORDER = [("ident", 128), ("cmask", 512), ("wq", 512), ("g6416", 2), ("g2048", 1)]
CB_ORDER = [("identb", 128), ("ones1024", 128), ("ones128", 128), ("bones64", 128), ("maskT", 512),
            ("dmatT", 2048), ("wst", 1024), ("wst16", 128)]


def _offsets(order):
    o = {}
    c = 0
    for n, w in order:
        o[n] = (c, w)
        c += w
    return o, c


CF_OFF, NCF = _offsets(CF_ORDER)
CB_OFF, NCB = _offsets(CB_ORDER)


def rope_tables(pos):
    theta = 10000.0 ** (-np.linspace(0.0, 1.0, 16, dtype=np.float32))
    ang = (pos.astype(np.float32)[:, None] * theta[None, :]).astype(np.float32)
    cos = np.cos(ang).astype(np.float32)
    sin = np.sin(ang).astype(np.float32)
    out = np.zeros((2, 128, pos.shape[0]), np.float32)
    for hd in range(4):
        for dd in range(32):
            p = hd * 32 + dd
            out[0, p] = cos[:, dd % 16]
            out[1, p] = (-sin[:, dd] if dd < 16 else sin[:, dd - 16])
    return out


def make_wqseg():
    g = _gammas()
    out = np.zeros((128, LT), np.float64)
    t = np.arange(NR, dtype=np.float64)
    for p in range(128):
        out[p, PRE:] = g[p // 32] ** (t + 1.0)
    return out.astype(np.float32)


_WQSEG = make_wqseg()


def winx_perm():
    hq, hf, hi, hg, rq, rk, rv, rg, lx, ly = 0, 512, 1024, 1536, 2048, 2176, 2304, 2560, 2816, 3072
    cols = []
    for hh in range(4):
        r = np.arange(hh * 128, (hh + 1) * 128)
        cols += [hq + r, hg + r, hf + r, hi + r]
    sw = np.concatenate([np.concatenate([np.arange(h * 32 + 16, h * 32 + 32), np.arange(h * 32, h * 32 + 16)])
                         for h in range(4)])
    cols += [rq + np.arange(128), rq + sw, rg + np.arange(256), rk + np.arange(128), rk + sw, rv + np.arange(256)]
    cols += [lx + np.arange(256), ly + np.arange(256)]
    p = np.concatenate(cols)
    assert p.shape[0] == WINX
    return p


def pack_params(inp):
    pp = np.zeros((128, NPP), np.float32)
    for l in range(NL):
        b = l * PL
        pp[:, b + 0:b + 8] = inp["norm_mix_g"][l].reshape(8, 128).T
        pp[:, b + 8:b + 16] = inp["norm_ffn_g"][l].reshape(8, 128).T
        pp[:, b + 16:b + 20] = inp["hg_lb_logits"][l].reshape(4, 128).T
        pp[:, b + 20] = inp["hg_norm_g"][l]
        pp[:, b + 21] = np.tile(inp["ret_norm_g"][l], 2)
        cw = inp["lru_conv_w"][l]
        for j in range(2):
            for tap in range(4):
                pp[:, b + 22 + j * 4 + tap] = cw[tap, j * 128:(j + 1) * 128]
        pp[:, b + 30:b + 32] = inp["lru_conv_b"][l].reshape(2, 128).T
        pp[:, b + 32:b + 34] = inp["lru_gate_a_b"][l].reshape(2, 128).T
        pp[:, b + 34:b + 36] = inp["lru_gate_i_b"][l].reshape(2, 128).T
        pp[:, b + 36:b + 38] = inp["lru_lambda"][l].reshape(2, 128).T
        fw = inp["ffn_conv_w"][l]
        for c in range(44):
            for tap in range(3):
                pp[:, b + 38 + c * 3 + tap] = fw[tap, c * 128:(c + 1) * 128]
        pp[:, b + 170:b + 214] = inp["ffn_conv_b"][l].reshape(44, 128).T
    pp[:, NL * PL:NL * PL + 8] = inp["final_norm_g"].reshape(8, 128).T
    return pp


def pack_lrug(inp):
    out = np.zeros((NL, 2, 2, 128, 128), np.float32)
    for l in range(NL):
        for gi, nm in enumerate(("lru_gate_a_w", "lru_gate_i_w")):
            w = inp[nm][l]
            for j in range(2):
                for bb in range(2):
                    out[l, gi, j, bb * 64:(bb + 1) * 64, bb * 64:(bb + 1) * 64] = w[j * 2 + bb]
    return out


def build_program(nlayers=NL, fused=False, dbg=None):
    nc = bass.Bass("TRN2", target_bir_lowering=False)
    P = Prog(nc)

    def din(name, shape):
        return nc.dram_tensor(name, shape, F32, kind="ExternalInput").ap()

    def dout(name, shape):
        return nc.dram_tensor(name, shape, F32, kind="ExternalOutput").ap()

    xs = din("xs", [LT, D])
    ppd = din("pp", [128, NPP])
    cstf = din("cstf", [128, NCF])
    cstb = din("cstb", [128, NCB])
    roped = din("rope", [2, 128, LT])
    wqsegd = din("wqseg", [128, LT])
    mskd = din("msk", [128, 9])
    winx = din("w_in_x", [NL, D, WINX])
    woutd = din("w_out", [NL, D, D])
    wupd = din("w_up", [NL, D, 2 * DFF])
    wdnd = din("w_down", [NL, DFF, D])
    lrugd = din("lrug", [NL * 4, 128, 128])
    outd = dout("out", [NR, D])
    NST = 584
    if not fused:
        st_in = din("st_in", [NL, 128, NST])
        halo_in = din("halo_in", [NL * 2, 128, 24])
        st_out = dout("st_out", [NL, 128, NST])
        halo_out = dout("halo_out", [NL * 2, 128, 24])
    else:
        xh_src = nc.dram_tensor("xh_src", [128, 24], F32, kind="Internal").ap()
        xh_dst = nc.dram_tensor("xh_dst", [4 * 128, 24], F32, kind="Internal").ap()
        st_src = nc.dram_tensor("st_src", [128, NST], F32, kind="Internal").ap()
        st_dst = nc.dram_tensor("st_dst", [4 * 128, NST], F32, kind="Internal").ap()
        lscr = nc.dram_tensor("lscr", [6, 128, LT], F32, kind="Internal").ap()
        hscr_o = nc.dram_tensor("hscr_o", [4, 128, LT], F32, kind="Internal").ap()
        hscr_q = nc.dram_tensor("hscr_q", [4, 128, LT], BF16, kind="Internal").ap()
        hscr_g = nc.dram_tensor("hscr_g", [4, 128, LT], BF16, kind="Internal").ap()
        rscr_o = nc.dram_tensor("rscr_o", [2, 128, LT], F32, kind="Internal").ap()
        rscr_q = nc.dram_tensor("rscr_q", [128, LT], BF16, kind="Internal").ap()
        rscr_g = nc.dram_tensor("rscr_g", [2, 128, LT], BF16, kind="Internal").ap()
    dbgd = None
    if dbg is not None:
        dbgd = dout("dbg", [128, 8, LT])

    def sb(name, shape, dt=F32):
        return nc.alloc_sbuf_tensor(name, shape, dt)

    h = sb("h", [128, 8, LT])
    hn = sb("hn", [128, 8, 1040], BF16)
    mg = sb("mg", [128, 8, 1040], BF16)
    pp = sb("pp_sb", [128, NPP])
    der = sb("der", [128, 4, 4, 3])
    cng = sb("cng", [128, 4, 2, 2])
    cf = sb("cf", [128, NCF])
    cbt = sb("cbt", [128, NCB], BF16)
    msk = sb("msk_sb", [128, 9])
    ropec = sb("ropec", [128, 512])
    ropes = sb("ropes", [128, 512])
    lrug = sb("lrug_sb", [128, 4, 128], BF16)
    NWU = 4
    wb = [sb(f"wb{i}", [128, 4096], BF16) for i in range(NWU)]
    NFS = 8
    Fs = [sb(f"F{i}", [128, 512]) for i in range(NFS)]
    NBS = 8
    Bs = [sb(f"B{i}", [128, 512], BF16) for i in range(NBS)]
    TM1 = sb("TM1", [64, 2048], BF16)
    TM2 = sb("TM2", [64, 1024], BF16)
    SBk = sb("SBk", [128, 1024], BF16)
    sq = sb("sq", [128, 4, 512], BF16)
    STT = sb("STT", [128, 584])
    Shg = [STT[:, i * 128:(i + 1) * 128] for i in range(4)]
    Dhg = STT[:, 512:516]
    Rst = STT[:, 516:580]
    hl = STT[:, 580:582]
    alog = STT[:, 582:584]
    rsum = sb("rsum", [128, 4])
    STIN = sb("STIN", [128, 584])
    lxs = [sb(f"lxs{j}", [128, 515]) for j in range(2)]
    ubg = sb("ubg", [128, 514])
    ubv = sb("ubv", [128, 514])
    uhalo = sb("uhalo", [128, 44, 2])
    stin_hg = STIN[:, 0:512]
    stin_ret = STIN[:, 516:580]
    stin_lru = STIN[:, 580:582]
    hgat = sb("hgat", [128, 4, 24])
    halo_sb = sb("halo_sb", [128, 8, 3])
    halo_t = sb("halo_t", [128, 8, 3])
    xin = sb("xin", [128, 1024])
    osb = xin
    tiny = sb("tiny", [128, 64])
    gsc = sb("gsc", [128, 32])
    ret_qw = sb("ret_qw", [128, 512], BF16)
    ret_gs = [sb(f"ret_gs{j}", [128, 512], BF16) for j in range(2)]

    NPS = 6
    PS = [nc.alloc_psum_tensor(f"ps{i}", [128, 512], F32) for i in range(NPS)]
    PT = [nc.alloc_psum_tensor(f"pt{i}", [128, 1024], BF16) for i in range(2)]
    ctr = {"ps": 0, "pt": 0, "F": 0, "B": 0, "wu": 0}

    def ps_next():
        i = ctr["ps"] % NPS
        ctr["ps"] += 1
        return PS[i], ("ps", i)

    def pt_next():
        i = ctr["pt"] % 2
        ctr["pt"] += 1
        return PT[i], ("pt", i)

    def f_next():
        i = ctr["F"] % NFS
        ctr["F"] += 1
        return Fs[i], ("F", i)

    def b_next():
        i = ctr["B"] % NBS
        ctr["B"] += 1
        return Bs[i], ("B", i)

    def wu_next(n=1):
        res = []
        for _ in range(n):
            i = ctr["wu"] % NWU
            ctr["wu"] += 1
            res.append((wb[i], ("wu", i)))
        return res

    def cfa(name):
        o, w = CF_OFF[name]
        return cf[:, o:o + w]

    def cba(name):
        o, w = CB_OFF[name]
        return cbt[:, o:o + w]

    def ppc(l, c, n=1):
        return pp[:, l * PL + c: l * PL + c + n]

    def act(out, in_, func, reads, writes, bias=None, scale=None):
        kw = {}
        if bias is not None:
            kw["bias"] = bias
        if scale is not None:
            kw["scale"] = scale
        P.add("act", lambda e: e.activation(out=out, in_=in_, func=func, **kw), reads, writes)

    def tt(out, in0, in1, op, reads, writes, eng="dve"):
        P.add(eng, lambda e: e.tensor_tensor(out=out, in0=in0, in1=in1, op=op), reads, writes)

    def ts(out, in0, s1, s2, op0, op1, reads, writes, eng="dve"):
        if op1 is None:
            P.add(eng, lambda e: e.tensor_scalar(out=out, in0=in0, scalar1=s1, scalar2=None, op0=op0), reads, writes)
        else:
            P.add(eng, lambda e: e.tensor_scalar(out=out, in0=in0, scalar1=s1, scalar2=s2, op0=op0, op1=op1),
                  reads, writes)

    def stt(out, in0, scalar, in1, op0, op1, reads, writes, eng="dve"):
        P.add(eng, lambda e: e.scalar_tensor_tensor(out=out, in0=in0, scalar=scalar, in1=in1, op0=op0, op1=op1),
              reads, writes)

    def cp(out, in_, reads, writes, eng="dve"):
        if eng == "act":
            P.add("act", lambda e: e.activation(out=out, in_=in_, func=AF.Copy), reads, writes)
        else:
            P.add(eng, lambda e: e.tensor_copy(out=out, in_=in_), reads, writes)

    def mm(out, lhsT, rhs, start, stop, reads, writes, tp=None):
        if tp is None:
            P.add("pe", lambda e: e.matmul(out, lhsT, rhs, start=start, stop=stop), reads, writes)
        else:
            P.add("pe", lambda e: e.matmul(out, lhsT, rhs, start=start, stop=stop, tile_position=tp), reads, writes)

    def tr(out, in_, ident, reads, writes):
        P.add("pe", lambda e: e.transpose(out, in_, ident), reads, writes)

    def dma(q, out, in_, reads, writes):
        P.add(q, lambda e: e.dma_start(out=out, in_=in_), reads, writes, dma=True)

    CONST = ("const",)

    dma("sp", pp[:], ppd[:], [], [CONST])
    dma("sp", cf[:], cstf[:], [], [CONST])
    dma("pool", cbt[:], cstb[:], [], [CONST])
    dma("sp", msk[:], mskd[:], [], [CONST])
    P.add("dve", lambda e: e.memset(uhalo[:], 0.0), [], [("uhalo",)])
    P.add("dve", lambda e: e.memset(gsc[:], 0.0), [], [("gsc",)])
    for j in range(2):
        P.add("dve", lambda e, j=j: e.memset(lxs[j][:], 0.0), [], [("lxs", j)])

    def lg(l):
        return pp[:, l * PL + 16: l * PL + 20]
    mx = tiny[:, 0:4]
    ex = tiny[:, 4:20].rearrange("p (l h) -> p l h", l=4)
    ssum = tiny[:, 20:24]
    TK = ("tiny",)
    tt(mx, lg(0), lg(1), ALU.max, [CONST], [TK])
    tt(mx, mx, lg(2), ALU.max, [CONST, TK], [TK])
    tt(mx, mx, lg(3), ALU.max, [CONST, TK], [TK])
    for l in range(4):
        tt(ex[:, l, :], lg(l), mx, ALU.subtract, [CONST, TK], [TK])
    act(tiny[:, 4:20], tiny[:, 4:20], AF.Exp, [TK], [TK])
    tt(ssum, ex[:, 0, :], ex[:, 1, :], ALU.add, [TK], [TK])
    tt(ssum, ssum, ex[:, 2, :], ALU.add, [TK], [TK])
    tt(ssum, ssum, ex[:, 3, :], ALU.add, [TK], [TK])
    P.add("dve", lambda e: e.reciprocal(out=ssum, in_=ssum), [TK], [TK])
    for l in range(4):
        tt(ex[:, l, :], ex[:, l, :], ssum, ALU.mult, [TK], [TK])
    DK = ("der",)
    P.add("dve", lambda e: e.memset(der[:, 0, :, 0], 0.0), [], [DK])
    for l in range(1, 4):
        tt(der[:, l, :, 0], der[:, l - 1, :, 0], ex[:, l, :], ALU.add, [TK, DK], [DK])
    for l in range(4):
        ts(der[:, l, :, 1], der[:, l, :, 0], -1.0, 1.0, ALU.mult, ALU.add, [DK], [DK])
        ts(der[:, l, :, 2], der[:, l, :, 0], 1.0, None, ALU.subtract, None, [DK], [DK])
    CK = ("cng",)
    for l in range(4):
        lam = pp[:, l * PL + 36: l * PL + 38]
        act(tiny[:, 32:34], lam, AF.Exp, [CONST, TK], [TK], scale=-1.0)
        act(tiny[:, 34:36], tiny[:, 32:34], AF.Ln, [TK], [TK], bias=1.0)
        ts(cng[:, l, :, 0], tiny[:, 34:36], -8.0, None, ALU.mult, None, [TK], [CK])
        ts(cng[:, l, :, 1], tiny[:, 34:36], -16.0, None, ALU.mult, None, [TK], [CK])

    ident = cfa("ident")
    blocks = [(0, 16)] + [(16 + 128 * i, 128) for i in range(16)]

    def hkey(k, ti):
        return ("h", k, ti)

    def tile_of(t0):
        for ti, (a, w) in enumerate(TILES):
            if a <= t0 < a + w:
                return ti
        raise ValueError

    for (t0, nt) in blocks:
        ti = tile_of(t0)
        dma("sp", xin[0:nt, :], xs[t0:t0 + nt, :], [], [("xo",)])
        for kg in range(2):
            pst, pk = ps_next()
            for kk in range(4):
                k = kg * 4 + kk
                tr(pst[:, kk * 128: kk * 128 + nt], xin[0:nt, k * 128:(k + 1) * 128], ident[0:nt, 0:nt],
                   [("xo",), CONST], [pk])
            cp(h[:, kg * 4:(kg + 1) * 4, t0:t0 + nt],
               pst[:, :].rearrange("p (k t) -> p k t", k=4)[:, :, 0:nt],
               [pk], [hkey(k, ti) for k in range(kg * 4, kg * 4 + 4)], eng="act" if kg else "dve")

    if dbg == ("x", 0):
        dma("sp", dbgd[:], h[:], [hkey(k, ti) for k in range(8) for ti in range(5)], [("dbg",)])
    def halo_inject(l, sub):
        idx = l * 2 + sub
        dma("sp", halo_sb[:].rearrange("p k t -> p (k t)"), halo_in[idx], [], [("halo_sb",)])
        ts(halo_t[:], halo_sb[:], msk[:, 1:2], None, ALU.mult, None, [("halo_sb",), CONST], [("halo_t",)])
        stt(h[:, :, 13:16], h[:, :, 13:16], msk[:, 0:1], halo_t[:], ALU.mult, ALU.add,
            [("halo_t",), CONST] + [hkey(k, 0) for k in range(8)], [hkey(k, 0) for k in range(8)])

    def halo_dump(l, sub):
        idx = l * 2 + sub
        dma("sp", halo_out[idx].rearrange("p (k t) -> p k t", k=8), h[:, :, LT - 3:LT],
            [hkey(k, 4) for k in range(8)], [("halo_out", idx)])

    def norm_phase(gcol_fn, su, tiles=None):
        P.label = "norm"
        c_base = SUP_COLS[su][0]
        for ti in (SUPERS[su] if tiles is None else tiles):
            t0, W = TILES[ti]
            c0 = t0 - c_base
            pst, pk = ps_next()
            for k in range(8):
                act(sq[:, k % 4, 0:W], h[:, k, t0:t0 + W], AF.Square, [hkey(k, ti)], [("sq", k % 4)])
                mm(pst[:, 0:W], cba("ones1024"), sq[:, k % 4, 0:W], k == 0, k == 7, [("sq", k % 4), CONST], [pk])
            rstd, rk = f_next()
            act(rstd[:, 0:W], pst[:, 0:W], AF.Ln, [pk], [rk], bias=EPS)
            act(rstd[:, 0:W], rstd[:, 0:W], AF.Exp, [rk], [rk], scale=-0.5)
            for k in range(8):
                stt(hn[:, k, c0:c0 + W], h[:, k, t0:t0 + W], gcol_fn(k), rstd[:, 0:W], ALU.mult, ALU.mult,
                    [hkey(k, ti), rk, CONST], [("hn", k, ti)])

    def load_w(dram2d, c0, ncols, units, row0=0, nrows=D):
        nk = nrows // 128
        per_unit = 4096 // ncols if ncols <= 4096 else 0
        views = []
        src = dram2d[row0:row0 + nrows, c0:c0 + ncols].rearrange("(k p) c -> p k c", p=128)
        if nk * ncols <= 4096:
            wt, wk = units[0]
            dst = wt[:, 0:nk * ncols].rearrange("p (k c) -> p k c", k=nk)
            dma("pool", dst, src, [], [wk])
            return [(dst[:, k, :], wk) for k in range(nk)]
        kpu = 4096 // ncols
        res = []
        for ui in range((nk + kpu - 1) // kpu):
            wt, wk = units[ui]
            k0 = ui * kpu
            k1 = min(nk, k0 + kpu)
            dst = wt[:, 0:(k1 - k0) * ncols].rearrange("p (k c) -> p k c", k=k1 - k0)
            dma("pool", dst, src[:, k0:k1, :], [], [wk])
            for k in range(k0, k1):
                res.append((dst[:, k - k0, :], wk))
        return res

    def hn_cols(ti, su):
        t0, W = TILES[ti]
        c0 = t0 - SUP_COLS[su][0]
        return c0, W

    def proj_fm(Wk, col0, ti, su, ncols=128):
        c0, W = hn_cols(ti, su)
        pst, pk = ps_next()
        for k in range(8):
            wap, wk = Wk[k]
            mm(pst[0:ncols, 0:W], wap[:, col0:col0 + ncols], hn[:, k, c0:c0 + W], k == 0, k == 7,
               [wk, ("hn", k, ti)], [pk])
        return pst, pk

    def state_merge(state_ap, skey, in_ap, in_key, so=False):
        if so:
            ts(state_ap, state_ap, msk[:, 0:1], None, ALU.mult, None, [skey, CONST], [skey])
        else:
            stt(state_ap, state_ap, msk[:, 0:1], in_ap, ALU.mult, ALU.add, [skey, in_key, CONST], [skey])

    def hg_group(l, hh, su, Wk, so=False):
        P.label = "pre_hg" if so else "hg"
        lbc = der[:, l, hh, 0:1]
        omlc = der[:, l, hh, 1:2]
        nomlc = der[:, l, hh, 2:3]
        gn = ppc(l, 20)
        S = Shg[hh]
        SK = ("Shg", hh)
        for ti in SUPERS[su]:
            c0, W = hn_cols(ti, su)
            chs = chunks_of(W)
            nch = len(chs)
            full = (not so) or fused
            save = so and fused
            t0_ = TILES[ti][0]
            zf, zfk = proj_fm(Wk, 256, ti, su)
            vps = []
            for ci, (s0, cw) in enumerate(chs):
                if ci % 4 == 0:
                    vp, vk = ps_next()
                    vps.append((vp, vk))
                for k in range(8):
                    wap, wk = Wk[k]
                    mm(vp[0:cw, (ci % 4) * 128:(ci % 4 + 1) * 128], hn[:, k, c0 + s0:c0 + s0 + cw], wap[:, 384:512],
                       k == 0, k == 7, [wk, ("hn", k, ti)], [vk])
            sg, sgk = f_next()
            act(sg[:, 0:W], zf[:, 0:W], AF.Sigmoid, [zfk], [sgk])
            logf, lfk = f_next()
            act(logf[:, 0:W], sg[:, 0:W], AF.Ln, [sgk, ("der",)], [lfk], bias=lbc, scale=omlc)
            bT, btk = f_next()
            P.add("dve", lambda e, bT=bT, logf=logf, W=W: e.tensor_tensor_scan(
                out=bT[:, 0:W], data0=cfa("cmask")[:, 0:W], data1=logf[:, 0:W], initial=0.0,
                op0=ALU.mult, op1=ALU.add), [lfk, CONST], [btk])
            kT, ktk = f_next()
            ts(kT[:, 0:W], sg[:, 0:W], nomlc, omlc, ALU.mult, ALU.add, [sgk, ("der",)], [ktk])
            Ep, epk = f_next()
            act(Ep[:, 0:W], bT[:, 0:W], AF.Exp, [btk], [epk])
            nb, nbk = f_next()
            ts(nb[:, 0:W], bT[:, 0:W], -1.0, 80.0, ALU.mult, ALU.min, [btk], [nbk])
            En, enk = f_next()
            act(En[:, 0:W], nb[:, 0:W], AF.Exp, [nbk], [enk])
            kt, kttk = b_next()
            tt(kt[:, 0:W], kT[:, 0:W], En[:, 0:W], ALU.mult, [ktk, enk], [kttk])
            khT, khk = b_next()
            if W == 512:
                ep3 = Ep[:, 0:512].rearrange("p (c t) -> p c t", c=8)
                tt(khT[:, 0:512].rearrange("p (c t) -> p c t", c=8), kt[:, 0:512].rearrange("p (c t) -> p c t", c=8),
                   ep3[:, :, 63:64].broadcast_to([128, 8, 64]), ALU.mult, [kttk, epk], [khk])
            else:
                for (s0, cw) in chs:
                    ts(khT[:, s0:s0 + cw], kt[:, s0:s0 + cw], Ep[:, s0 + cw - 1:s0 + cw], None, ALU.mult, None,
                       [kttk, epk], [khk])
            for bi, (vp, vk) in enumerate(vps):
                n = min(4, nch - bi * 4)
                cwm = chs[0][1]
                cp(TM1[0:cwm, bi * 512: bi * 512 + n * 128], vp[0:cwm, 0:n * 128], [vk], [("TM1",)], eng="act")
            ptt, ptk = pt_next()
            for ci, (s0, cw) in enumerate(chs):
                tr(ptt[0:cw, ci * 128:(ci + 1) * 128], khT[:, s0:s0 + cw], cba("identb"), [khk, CONST], [ptk])
            cwm = chs[0][1]
            cp(TM2[0:cwm, 0:nch * 128], ptt[0:cwm, 0:nch * 128], [ptk], [("TM2",)], eng="dve")
            if full:
                zq, zqk = proj_fm(Wk, 0, ti, su)
                zg, zgk = proj_fm(Wk, 128, ti, su)
                qs, qsk = f_next()
                act(qs[:, 0:W], zq[:, 0:W], AF.Silu, [zqk], [qsk])
                gs, gsk = b_next()
                act(gs[:, 0:W], zg[:, 0:W], AF.Silu, [zgk], [gsk])
                qt, qtk = b_next()
                stt(qt[:, 0:W], qs[:, 0:W], 128.0 ** -0.5, Ep[:, 0:W], ALU.mult, ALU.mult, [qsk, epk], [qtk])
            if full:
                sc, sck = ps_next()
                for ci, (s0, cw) in enumerate(chs):
                    mm(sc[0:cw, ci * 64: ci * 64 + cw], kt[:, s0:s0 + cw], qt[:, s0:s0 + cw], True, True,
                       [kttk, qtk], [sck])
                AT, atk = b_next()
                wtot = (nch - 1) * 64 + chs[-1][1]
                stt(AT[0:cwm, 0:wtot], sc[0:cwm, 0:wtot], 1e30, cba("maskT")[0:cwm, 0:wtot], ALU.min, ALU.mult,
                    [sck, CONST], [atk])
            kvs = []
            for ci, (s0, cw) in enumerate(chs):
                if ci % 4 == 0:
                    kp, kk = ps_next()
                    kvs.append((kp, kk))
                mm(kp[:, (ci % 4) * 128:(ci % 4 + 1) * 128], TM2[0:cw, ci * 128:(ci + 1) * 128],
                   TM1[0:cw, ci * 128:(ci + 1) * 128], True, True, [("TM2",), ("TM1",)], [kk])
            XK = ("xo",)
            if full:
                cp(xin[:, 0:128], S[:], [SK], [XK], eng="dve")
            if save:
                qg, qgk = b_next()
                if ti == 0:
                    P.add("dve", lambda e, qg=qg, W=W: e.memset(qg[:, 0:W], 0.0), [], [qgk])
            for ci, (s0, cw) in enumerate(chs):
                kp, kk = kvs[ci // 4]
                dcol = Ep[:, s0 + cw - 1:s0 + cw]
                kv_ap = kp[:, (ci % 4) * 128:(ci % 4 + 1) * 128]
                if not full:
                    stt(S[:], S[:], dcol, kv_ap, ALU.mult, ALU.add, [SK, epk, kk], [SK])
                elif ci == nch - 1:
                    stt(S[:], xin[:, ci * 128:(ci + 1) * 128], dcol, kv_ap, ALU.mult, ALU.add, [XK, epk, kk], [SK])
                else:
                    stt(xin[:, (ci + 1) * 128:(ci + 2) * 128], xin[:, ci * 128:(ci + 1) * 128], dcol, kv_ap,
                        ALU.mult, ALU.add, [XK, epk, kk], [XK])
                if so and ti != 0 and not save:
                    ts(Dhg[:, hh:hh + 1], Dhg[:, hh:hh + 1], dcol, None, ALU.mult, None,
                       [("Dhg",), epk], [("Dhg",)])
            if save and ti != 0:
                GS = ("gsc",)
                cp(gsc[:, 0:1], Dhg[:, hh:hh + 1], [("Dhg",)], [GS], eng="dve")
                ep3 = Ep[:, 0:512].rearrange("p (c t) -> p c t", c=8)
                P.add("dve", lambda e, ep3=ep3, hh=hh: e.tensor_tensor_scan(
                    out=gsc[:, 1:9], data0=ep3[:, :, 63], data1=gsc[:, 16:24], initial=Dhg[:, hh:hh + 1],
                    op0=ALU.mult, op1=ALU.add), [epk, ("Dhg",), GS], [GS])
                cp(Dhg[:, hh:hh + 1], gsc[:, 8:9], [GS], [("Dhg",)], eng="dve")
                tt(qg[:, 0:512].rearrange("p (c t) -> p c t", c=8), qt[:, 0:512].rearrange("p (c t) -> p c t", c=8),
                   gsc[:, 0:8].rearrange("p (c o) -> p c o", o=1).broadcast_to([128, 8, 64]), ALU.mult,
                   [qtk, GS], [qgk])
            if full:
                cp(SBk[:, 0:nch * 128], xin[:, 0:nch * 128], [XK], [("SBk",)], eng="act")
            if ti == 0:
                state_merge(S[:], SK, stin_hg[:, hh * 128:(hh + 1) * 128], ("stin",), so)
            if not full:
                continue
            op_, ok = ps_next()
            for ci, (s0, cw) in enumerate(chs):
                mm(op_[:, s0:s0 + cw], TM1[0:cw, ci * 128:(ci + 1) * 128], AT[0:cw, ci * 64: ci * 64 + cw],
                   True, False, [("TM1",), atk], [ok])
                mm(op_[:, s0:s0 + cw], SBk[:, ci * 128:(ci + 1) * 128], qt[:, s0:s0 + cw], False, True,
                   [("SBk",), qtk], [ok])
            if save:
                ol, olk = f_next()
                cp(ol[:, 0:W], op_[:, 0:W], [ok], [olk], eng="act")
                dma("sp", hscr_o[hh][:, t0_:t0_ + W], ol[:, 0:W], [olk], [("hscr", 0, hh, ti)])
                dma("sp", hscr_q[hh][:, t0_:t0_ + W], qg[:, 0:W], [qgk], [("hscr", 1, hh, ti)])
                dma("sp", hscr_g[hh][:, t0_:t0_ + W], gs[:, 0:W], [gsk], [("hscr", 2, hh, ti)])
                continue
            osq, osk = b_next()
            act(osq[:, 0:W], op_[:, 0:W], AF.Square, [ok], [osk])
            msp, msk_ = ps_next()
            mm(msp[:, 0:W], cba("ones128"), osq[:, 0:W], True, True, [osk, CONST], [msk_])
            rstd, rk = f_next()
            act(rstd[:, 0:W], msp[:, 0:W], AF.Ln, [msk_], [rk], bias=EPS)
            act(rstd[:, 0:W], rstd[:, 0:W], AF.Exp, [rk], [rk], scale=-0.5)
            t1, t1k = f_next()
            stt(t1[:, 0:W], op_[:, 0:W], gn, rstd[:, 0:W], ALU.mult, ALU.mult, [ok, rk, CONST], [t1k])
            tt(mg[:, hh, c0:c0 + W], t1[:, 0:W], gs[:, 0:W], ALU.mult, [t1k, gsk], [("mg", hh, ti)])

    def hg_main(l, su):
        P.label = "hg"
        gn = ppc(l, 20)
        units = [(hh, ti) for hh in range(4) for ti in SUPERS[su]]

        def stage_a(hh, ti):
            c0, W = hn_cols(ti, su)
            t0_ = TILES[ti][0]
            d = dict(hh=hh, ti=ti, c0=c0, W=W)
            d["ol"], d["olk"] = f_next()
            dma("sp", d["ol"][:, 0:W], hscr_o[hh][:, t0_:t0_ + W], [("hscr", 0, hh, ti)], [d["olk"]])
            qg, qgk = b_next()
            dma("sp", qg[:, 0:W], hscr_q[hh][:, t0_:t0_ + W], [("hscr", 1, hh, ti)], [qgk])
            d["gs"], d["gsk"] = b_next()
            dma("sp", d["gs"][:, 0:W], hscr_g[hh][:, t0_:t0_ + W], [("hscr", 2, hh, ti)], [d["gsk"]])
            cps, ck = ps_next()
            mm(cps[:, 0:W], SBk[:, hh * 128:(hh + 1) * 128], qg[:, 0:W], True, True, [("SBk",), qgk], [ck])
            tt(d["ol"][:, 0:W], d["ol"][:, 0:W], cps[:, 0:W], ALU.add, [d["olk"], ck], [d["olk"]])
            d["osq"], d["osk"] = b_next()
            act(d["osq"][:, 0:W], d["ol"][:, 0:W], AF.Square, [d["olk"]], [d["osk"]])
            return d

        def stage_b(d):
            W, c0, hh, ti = d["W"], d["c0"], d["hh"], d["ti"]
            msp, mk = ps_next()
            mm(msp[:, 0:W], cba("ones128"), d["osq"][:, 0:W], True, True, [d["osk"], CONST], [mk])
            rstd, rk = f_next()
            act(rstd[:, 0:W], msp[:, 0:W], AF.Ln, [mk], [rk], bias=EPS)
            act(rstd[:, 0:W], rstd[:, 0:W], AF.Exp, [rk], [rk], scale=-0.5)
            stt(d["ol"][:, 0:W], d["ol"][:, 0:W], gn, rstd[:, 0:W], ALU.mult, ALU.mult, [d["olk"], rk, CONST],
                [d["olk"]])
            tt(mg[:, hh, c0:c0 + W], d["ol"][:, 0:W], d["gs"][:, 0:W], ALU.mult, [d["olk"], d["gsk"]],
               [("mg", hh, ti)])

        prev = None
        for (hh, ti) in units:
            cur = stage_a(hh, ti)
            if prev is not None:
                stage_b(prev)
            prev = cur
        stage_b(prev)

    def ret_group(l, su, Wk, so=False):
        P.label = "pre_ret" if so else "ret"
        gn = ppc(l, 21)
        RK = ("Rst",)
        for ti in SUPERS[su]:
            t0, W = TILES[ti]
            c0, _ = hn_cols(ti, su)
            chs = chunks_of(W)
            nch = len(chs)
            cwm = chs[0][1]
            full = (not so) or fused
            save = so and fused
            dma("sp", ropec[:, 0:W], roped[0][:, t0:t0 + W], [], [("ropec",)])
            dma("sp", ropes[:, 0:W], roped[1][:, t0:t0 + W], [], [("ropes",)])

            def rope_fm(colx, cols_):
                zx, zxk = proj_fm(Wk, colx, ti, su)
                zs, zsk = proj_fm(Wk, cols_, ti, su)
                a1, a1k = f_next()
                tt(a1[:, 0:W], zx[:, 0:W], ropec[:, 0:W], ALU.mult, [zxk, ("ropec",)], [a1k])
                a2, a2k = f_next()
                tt(a2[:, 0:W], zs[:, 0:W], ropes[:, 0:W], ALU.mult, [zsk, ("ropes",)], [a2k])
                r, rk_ = f_next()
                tt(r[:, 0:W], a1[:, 0:W], a2[:, 0:W], ALU.add, [a1k, a2k], [rk_])
                return r, rk_
            if full:
                qr, qrk = rope_fm(0, 128)
                qrb, qrbk = b_next()
                cp(qrb[:, 0:W], qr[:, 0:W], [qrk], [qrbk], eng="act")
                qw, qwk = ret_qw, ("ret_qw",)
                tt(qw[:, 0:W], qr[:, 0:W], cfa("wq")[:, 0:W], ALU.mult, [qrk, CONST], [qwk])
                if save:
                    wqt, wqtk = f_next()
                    dma("sp", wqt[:, 0:W], wqsegd[:, t0:t0 + W], [], [wqtk])
                    qwg, qwgk = b_next()
                    tt(qwg[:, 0:W], qr[:, 0:W], wqt[:, 0:W], ALU.mult, [qrk, wqtk], [qwgk])
                    dma("sp", rscr_q[:, t0:t0 + W], qwg[:, 0:W], [qwgk], [("rscr", "q", ti)])
            kr, krk = rope_fm(512, 640)
            krb, krbk = b_next()
            cp(krb[:, 0:W], kr[:, 0:W], [krk], [krbk], eng="act")
            gsl = []
            for j in range(2 if full else 0):
                zg, zgk = proj_fm(Wk, 256 + j * 128, ti, su)
                gs, gsk = ret_gs[j], ("ret_gs", j)
                act(gs[:, 0:W], zg[:, 0:W], AF.Silu, [zgk], [gsk])
                gsl.append((gs, gsk))
            for ci, (s0, cw) in enumerate(chs):
                if ci % 2 == 0:
                    vp, vk = ps_next()
                for k in range(8):
                    wap, wk = Wk[k]
                    mm(vp[0:cw, (ci % 2) * 256:(ci % 2 + 1) * 256], hn[:, k, c0 + s0:c0 + s0 + cw], wap[:, 768:1024],
                       k == 0, k == 7, [wk, ("hn", k, ti)], [vk])
                if ci % 2 == 1 or ci == nch - 1:
                    n = (ci % 2) + 1
                    cb0 = (ci - (ci % 2)) * 256
                    cp(TM1[0:cw, cb0: cb0 + n * 256], vp[0:cw, 0:n * 256], [vk], [("TM1",)], eng="act")
            ptt, ptk = pt_next()
            for ci, (s0, cw) in enumerate(chs):
                tr(ptt[0:cw, ci * 128:(ci + 1) * 128], krb[:, s0:s0 + cw], cba("identb"), [krbk, CONST], [ptk])
            wst_ap = cba("wst")[0:cwm, 0:nch * 128] if W == 512 else cba("wst16")[0:cwm, 0:128]
            tt(TM2[0:cwm, 0:nch * 128], ptt[0:cwm, 0:nch * 128], wst_ap, ALU.mult, [ptk, CONST], [("TM2",)])
            ATs = []
            wtot = (nch - 1) * 64 + chs[-1][1]
            for hd in range(4 if full else 0):
                sc, sck = ps_next()
                for ci, (s0, cw) in enumerate(chs):
                    mm(sc[0:cw, ci * 64: ci * 64 + cw], krb[32 * hd:32 * hd + 32, s0:s0 + cw],
                       qrb[32 * hd:32 * hd + 32, s0:s0 + cw], True, True, [krbk, qrbk], [sck], tp=(32 * hd, 0))
                AT, atk = b_next()
                tt(AT[0:cwm, 0:wtot], sc[0:cwm, 0:wtot], cba("dmatT")[0:cwm, hd * 512: hd * 512 + wtot], ALU.mult,
                   [sck, CONST], [atk])
                ATs.append((AT, atk))
            kvp, kvk = ps_next()
            for ci, (s0, cw) in enumerate(chs):
                for hd in range(4):
                    mm(kvp[32 * hd:32 * hd + 32, ci * 64:(ci + 1) * 64],
                       TM2[0:cw, ci * 128 + 32 * hd: ci * 128 + 32 * hd + 32],
                       TM1[0:cw, ci * 256 + 64 * hd: ci * 256 + 64 * hd + 64], True, True,
                       [("TM2",), ("TM1",)], [kvk], tp=(0, 32 * hd))
            gdec = cfa("g6416")[:, 0:1] if W == 512 else cfa("g6416")[:, 1:2]
            XK = ("xo",)
            if full:
                cp(xin[:, 0:64], Rst[:], [RK], [XK], eng="dve")
            for ci, (s0, cw) in enumerate(chs):
                kv_ap = kvp[:, ci * 64:(ci + 1) * 64]
                if not full:
                    stt(Rst[:], Rst[:], gdec, kv_ap, ALU.mult, ALU.add, [RK, kvk, CONST], [RK])
                elif ci == nch - 1:
                    stt(Rst[:], xin[:, ci * 64:(ci + 1) * 64], gdec, kv_ap, ALU.mult, ALU.add, [XK, kvk, CONST], [RK])
                else:
                    stt(xin[:, (ci + 1) * 64:(ci + 2) * 64], xin[:, ci * 64:(ci + 1) * 64], gdec, kv_ap,
                        ALU.mult, ALU.add, [XK, kvk, CONST], [XK])
            if full:
                cp(SBk[:, 0:nch * 64], xin[:, 0:nch * 64], [XK], [("SBk",)], eng="act")
            if ti == 0:
                state_merge(Rst[:], RK, stin_ret[:], ("stin",), so)
            if not full:
                continue
            for j in range(2):
                op_, ok = ps_next()
                for hh2 in range(2):
                    hd = j * 2 + hh2
                    po = 64 * hh2
                    AT, atk = ATs[hd]
                    for ci, (s0, cw) in enumerate(chs):
                        mm(op_[po:po + 64, s0:s0 + cw], TM1[0:cw, ci * 256 + 64 * hd: ci * 256 + 64 * hd + 64],
                           AT[0:cw, ci * 64: ci * 64 + cw], True, False, [("TM1",), atk], [ok], tp=(0, po))
                        mm(op_[po:po + 64, s0:s0 + cw], SBk[32 * hd:32 * hd + 32, ci * 64:(ci + 1) * 64],
                           qw[32 * hd:32 * hd + 32, s0:s0 + cw], False, True, [("SBk",), qwk], [ok], tp=(32 * hd, po))
                if save:
                    ol, olk = f_next()
                    cp(ol[:, 0:W], op_[:, 0:W], [ok], [olk], eng="act")
                    dma("sp", rscr_o[j][:, t0:t0 + W], ol[:, 0:W], [olk], [("rscr", "o", j, ti)])
                    gs, gsk = gsl[j]
                    dma("sp", rscr_g[j][:, t0:t0 + W], gs[:, 0:W], [gsk], [("rscr", "g", j, ti)])
                    continue
                ob, obk = b_next()
                cp(ob[:, 0:W], op_[:, 0:W], [ok], [obk], eng="act")
                of, ofk = f_next()
                cp(of[:, 0:W], op_[:, 0:W], [ok], [ofk], eng="act")
                mup, muk = ps_next()
                mm(mup[:, 0:W], cba("bones64"), ob[:, 0:W], True, True, [obk, CONST], [muk])
                dd, ddk = f_next()
                tt(dd[:, 0:W], of[:, 0:W], mup[:, 0:W], ALU.subtract, [ofk, muk], [ddk])
                dsq, dsk = b_next()
                act(dsq[:, 0:W], dd[:, 0:W], AF.Square, [ddk], [dsk])
                vp2, vk2 = ps_next()
                mm(vp2[:, 0:W], cba("bones64"), dsq[:, 0:W], True, True, [dsk, CONST], [vk2])
                rstd, rk = f_next()
                act(rstd[:, 0:W], vp2[:, 0:W], AF.Ln, [vk2], [rk], bias=EPS)
                act(rstd[:, 0:W], rstd[:, 0:W], AF.Exp, [rk], [rk], scale=-0.5)
                t1, t1k = f_next()
                stt(t1[:, 0:W], dd[:, 0:W], gn, rstd[:, 0:W], ALU.mult, ALU.mult, [ddk, rk, CONST], [t1k])
                gs, gsk = gsl[j]
                tt(mg[:, 4 + j, c0:c0 + W], t1[:, 0:W], gs[:, 0:W], ALU.mult, [t1k, gsk], [("mg", 4 + j, ti)])

    def ret_main(l, su):
        P.label = "ret"
        gn = ppc(l, 21)
        for ti in SUPERS[su]:
            t0, W = TILES[ti]
            c0, _ = hn_cols(ti, su)
            qwg, qwgk = b_next()
            dma("sp", qwg[:, 0:W], rscr_q[:, t0:t0 + W], [("rscr", "q", ti)], [qwgk])
            st = []
            for j in range(2):
                of, ofk = f_next()
                dma("sp", of[:, 0:W], rscr_o[j][:, t0:t0 + W], [("rscr", "o", j, ti)], [ofk])
                gs, gsk = b_next()
                dma("sp", gs[:, 0:W], rscr_g[j][:, t0:t0 + W], [("rscr", "g", j, ti)], [gsk])
                st.append(dict(j=j, of=of, ofk=ofk, gs=gs, gsk=gsk))
            for d in st:
                d["cps"], d["ck"] = ps_next()
                for hh2 in range(2):
                    hd = d["j"] * 2 + hh2
                    po = 64 * hh2
                    mm(d["cps"][po:po + 64, 0:W], SBk[32 * hd:32 * hd + 32, 512:576], qwg[32 * hd:32 * hd + 32, 0:W],
                       True, True, [("SBk",), qwgk], [d["ck"]], tp=(32 * hd, po))
            for d in st:
                tt(d["of"][:, 0:W], d["of"][:, 0:W], d["cps"][:, 0:W], ALU.add, [d["ofk"], d["ck"]], [d["ofk"]])
            for d in st:
                d["ob"], d["obk"] = b_next()
                cp(d["ob"][:, 0:W], d["of"][:, 0:W], [d["ofk"]], [d["obk"]], eng="act")
            for d in st:
                d["mup"], d["muk"] = ps_next()
                mm(d["mup"][:, 0:W], cba("bones64"), d["ob"][:, 0:W], True, True, [d["obk"], CONST], [d["muk"]])
            for d in st:
                d["dd"], d["ddk"] = f_next()
                tt(d["dd"][:, 0:W], d["of"][:, 0:W], d["mup"][:, 0:W], ALU.subtract, [d["ofk"], d["muk"]], [d["ddk"]])
            for d in st:
                d["dsq"], d["dsk"] = b_next()
                act(d["dsq"][:, 0:W], d["dd"][:, 0:W], AF.Square, [d["ddk"]], [d["dsk"]])
            for d in st:
                d["vp2"], d["vk2"] = ps_next()
                mm(d["vp2"][:, 0:W], cba("bones64"), d["dsq"][:, 0:W], True, True, [d["dsk"], CONST], [d["vk2"]])
            for d in st:
                d["rstd"], d["rk"] = f_next()
                act(d["rstd"][:, 0:W], d["vp2"][:, 0:W], AF.Ln, [d["vk2"]], [d["rk"]], bias=EPS)
            for d in st:
                act(d["rstd"][:, 0:W], d["rstd"][:, 0:W], AF.Exp, [d["rk"]], [d["rk"]], scale=-0.5)
            for d in st:
                stt(d["dd"][:, 0:W], d["dd"][:, 0:W], gn, d["rstd"][:, 0:W], ALU.mult, ALU.mult,
                    [d["ddk"], d["rk"], CONST], [d["ddk"]])
            for d in st:
                tt(mg[:, 4 + d["j"], c0:c0 + W], d["dd"][:, 0:W], d["gs"][:, 0:W], ALU.mult, [d["ddk"], d["gsk"]],
                   [("mg", 4 + d["j"], ti)])

    def lru_group(l, su, Wk, so=False):
        P.label = "pre_lru" if so else "lru"
        for ti in SUPERS[su]:
            c0, W = hn_cols(ti, su)
            for j in range(2):
                LK = ("lxs", j)
                t0_ = TILES[ti][0]
                zx, zxk = proj_fm(Wk, j * 128, ti, su)
                if (not so) or fused:
                    zy, zyk = proj_fm(Wk, 256 + j * 128, ti, su)
                cp(lxs[j][:, 3:3 + W], zx[:, 0:W], [zxk], [LK], eng="act")
                xc, xck = f_next()
                ts(xc[:, 0:W], lxs[j][:, 0:W], ppc(l, 22 + j * 4 + 0), ppc(l, 30 + j), ALU.mult, ALU.add,
                   [LK, CONST], [xck])
                for tap in range(1, 4):
                    stt(xc[:, 0:W], lxs[j][:, tap:tap + W], ppc(l, 22 + j * 4 + tap), xc[:, 0:W], ALU.mult, ALU.add,
                        [LK, xck, CONST], [xck])
                cp(lxs[j][:, 0:3], lxs[j][:, W:W + 3], [LK], [LK], eng="dve")
                xcb, xcbk = b_next()
                cp(xcb[:, 0:W], xc[:, 0:W], [xck], [xcbk], eng="act")
                gap, gak = ps_next()
                mm(gap[:, 0:W], lrug[:, 0 * 2 + j, :], xcb[:, 0:W], True, True, [xcbk, ("lrug",)], [gak])
                gip, gik = ps_next()
                mm(gip[:, 0:W], lrug[:, 1 * 2 + j, :], xcb[:, 0:W], True, True, [xcbk, ("lrug",)], [gik])
                r, rk_ = f_next()
                act(r[:, 0:W], gap[:, 0:W], AF.Sigmoid, [gak, CONST], [rk_], bias=ppc(l, 32 + j))
                ii, iik = f_next()
                act(ii[:, 0:W], gip[:, 0:W], AF.Sigmoid, [gik, CONST], [iik], bias=ppc(l, 34 + j))
                a, ak = f_next()
                act(a[:, 0:W], r[:, 0:W], AF.Exp, [rk_, ("cng",)], [ak], scale=cng[:, l, j, 0:1])
                a2, a2k = f_next()
                act(a2[:, 0:W], r[:, 0:W], AF.Exp, [rk_, ("cng",)], [a2k], scale=cng[:, l, j, 1:2])
                th, thk = f_next()
                act(th[:, 0:W], r[:, 0:W], AF.Tanh, [rk_, ("cng",)], [thk], scale=cng[:, l, j, 0:1])
                stt(a2[:, 0:W], a2[:, 0:W], 1.0, th[:, 0:W], ALU.add, ALU.mult, [a2k, thk], [a2k])
                act(a2[:, 0:W], a2[:, 0:W], AF.Sqrt, [a2k], [a2k], scale=-1.0)
                tt(ii[:, 0:W], ii[:, 0:W], xc[:, 0:W], ALU.mult, [iik, xck], [iik])
                tt(ii[:, 0:W], ii[:, 0:W], a2[:, 0:W], ALU.mult, [iik, a2k], [iik])
                hs, hsk = f_next()
                P.add("dve", lambda e, hs=hs, a=a, ii=ii, W=W, j=j: e.tensor_tensor_scan(
                    out=hs[:, 0:W], data0=a[:, 0:W], data1=ii[:, 0:W], initial=hl[:, j:j + 1],
                    op0=ALU.mult, op1=ALU.add), [ak, iik, ("hl",)], [hsk])
                cp(hl[:, j:j + 1], hs[:, W - 1:W], [hsk], [("hl",)], eng="dve")
                if ti == 0:
                    state_merge(hl[:, j:j + 1], ("hl",), stin_lru[:, j:j + 1], ("stin",), so)
                if so:
                    if ti != 0:
                        P.add("dve", lambda e, r=r, W=W, j=j: e.reduce_sum(
                            out=rsum[:, 2 + j:3 + j], in_=r[:, 0:W], axis=mybir.AxisListType.X), [rk_], [("rsum",)])
                        tt(rsum[:, j:j + 1], rsum[:, j:j + 1], rsum[:, 2 + j:3 + j], ALU.add, [("rsum",)], [("rsum",)])
                    if fused:
                        ge, gek = f_next()
                        act(ge[:, 0:W], zy[:, 0:W], AF.Gelu_apprx_tanh, [zyk], [gek])
                        dma("sp", lscr[0 + j][:, t0_:t0_ + W], a[:, 0:W], [ak], [("lscr", 0, j, ti)])
                        dma("sp", lscr[2 + j][:, t0_:t0_ + W], ii[:, 0:W], [iik], [("lscr", 1, j, ti)])
                        dma("sp", lscr[4 + j][:, t0_:t0_ + W], ge[:, 0:W], [gek], [("lscr", 2, j, ti)])
                    continue
                ge, gek = f_next()
                act(ge[:, 0:W], zy[:, 0:W], AF.Gelu_apprx_tanh, [zyk], [gek])
                tt(mg[:, 6 + j, c0:c0 + W], ge[:, 0:W], hs[:, 0:W], ALU.mult, [gek, hsk], [("mg", 6 + j, ti)])

    def lru_main(l, su):
        P.label = "lru"
        for ti in SUPERS[su]:
            c0, W = hn_cols(ti, su)
            t0_ = TILES[ti][0]
            for j in range(2):
                la, lak = f_next()
                dma("sp", la[:, 0:W], lscr[0 + j][:, t0_:t0_ + W], [("lscr", 0, j, ti)], [lak])
                lu, luk = f_next()
                dma("sp", lu[:, 0:W], lscr[2 + j][:, t0_:t0_ + W], [("lscr", 1, j, ti)], [luk])
                lg_, lgk = f_next()
                dma("sp", lg_[:, 0:W], lscr[4 + j][:, t0_:t0_ + W], [("lscr", 2, j, ti)], [lgk])
                hs, hsk = f_next()
                P.add("dve", lambda e, hs=hs, la=la, lu=lu, W=W, j=j: e.tensor_tensor_scan(
                    out=hs[:, 0:W], data0=la[:, 0:W], data1=lu[:, 0:W], initial=hl[:, j:j + 1],
                    op0=ALU.mult, op1=ALU.add), [lak, luk, ("hl",)], [hsk])
                cp(hl[:, j:j + 1], hs[:, W - 1:W], [hsk], [("hl",)], eng="dve")
                if ti == 0:
                    state_merge(hl[:, j:j + 1], ("hl",), stin_lru[:, j:j + 1], ("stin",), False)
                tt(mg[:, 6 + j, c0:c0 + W], lg_[:, 0:W], hs[:, 0:W], ALU.mult, [lgk, hsk], [("mg", 6 + j, ti)])

    STAGES = []

    def stage(load, comp):
        STAGES.append((load, comp))

    def run_stages():
        n = len(STAGES)
        loaded = {}
        nxt = 0

        def issue_next_load(after):
            nonlocal nxt
            nxt = max(nxt, after)
            while nxt < n and STAGES[nxt][0] is None:
                nxt += 1
            if nxt < n:
                loaded[nxt] = STAGES[nxt][0]()
                nxt += 1
        issue_next_load(0)
        for i in range(n):
            ld, comp = STAGES[i]
            if ld is None:
                comp(None)
                continue
            if i not in loaded:
                loaded[i] = ld()
                nxt = max(nxt, i + 1)
            issue_next_load(i + 1)
            comp(loaded.pop(i))

    def wout_phase(l, su):
        stage(lambda: load_w(woutd[l], 0, 1024, wu_next(2)), lambda Wk: wout_compute(l, su, Wk))

    def wout_compute(l, su, Wk):
        P.label = "wout"
        for ti in (SUPERS[su] if su == 0 else SUPERS[su][::-1]):
            t0, W = TILES[ti]
            c0, _ = hn_cols(ti, su)
            for m in range(8):
                pst, pk = ps_next()
                for k in range(8):
                    wap, wk = Wk[k]
                    mm(pst[:, 0:W], wap[:, m * 128:(m + 1) * 128], mg[:, k, c0:c0 + W], k == 0, k == 7,
                       [wk, ("mg", k, ti)], [pk])
                tt(h[:, m, t0:t0 + W], h[:, m, t0:t0 + W], pst[:, 0:W], ALU.add, [hkey(m, ti), pk], [hkey(m, ti)])

    def ffn_phase(l, su):
        for (j0, j1) in FFN_GROUPS:
            for jb in range(j0, j1, 4):
                nb_ = min(4, j1 - jb)

                def ld(jb=jb, nb_=nb_):
                    Wg = load_w(wupd[l], jb * 128, nb_ * 128, wu_next(1))
                    Wv = load_w(wupd[l], DFF + jb * 128, nb_ * 128, wu_next(1))
                    return (Wg, Wv)
                stage(ld, lambda W2, jb=jb, nb_=nb_, j0=j0: ffn_up_block(l, su, j0, jb, nb_, W2))
            stage(lambda j0=j0, j1=j1: load_w(wdnd[l], 0, 1024, wu_next(2), row0=j0 * 128, nrows=(j1 - j0) * 128),
                  lambda Wd, j0=j0, j1=j1: ffn_down(l, su, j0, j1, Wd))

    def ffn_up_block(l, su, j0, jb, nb_, W2):
        P.label = "ffn"
        Wg, Wv = W2
        for jq in range(nb_):
            j = jb + jq
            jj = j - j0
            cg, cv = j, 22 + j
            cp(ubg[:, 0:2], uhalo[:, cg, :], [("uhalo",)], [("ubg",)], eng="dve")
            cp(ubv[:, 0:2], uhalo[:, cv, :], [("uhalo",)], [("ubv",)], eng="dve")
            for ti in SUPERS[su]:
                c0, W = hn_cols(ti, su)
                ugp, ugk = proj_fm(Wg, jq * 128, ti, su)
                uvp, uvk = proj_fm(Wv, jq * 128, ti, su)
                cp(ubg[:, 2:2 + W], ugp[:, 0:W], [ugk], [("ubg",)], eng="act")
                cp(ubv[:, 2:2 + W], uvp[:, 0:W], [uvk], [("ubv",)], eng="act")
                cgt, cgk = f_next()
                cvt, cvk = f_next()
                pairs = ((ubg, ("ubg",), cg, cgt, cgk), (ubv, ("ubv",), cv, cvt, cvk))
                for (ub, ubk, cc, c_, ck) in pairs:
                    ts(c_[:, 0:W], ub[:, 0:W], ppc(l, 38 + cc * 3 + 0), ppc(l, 170 + cc), ALU.mult, ALU.add,
                       [ubk, CONST], [ck])
                for tap in (1, 2):
                    for (ub, ubk, cc, c_, ck) in pairs:
                        stt(c_[:, 0:W], ub[:, tap:tap + W], ppc(l, 38 + cc * 3 + tap), c_[:, 0:W], ALU.mult, ALU.add,
                            [ubk, ck, CONST], [ck])
                for (ub, ubk, cc, c_, ck) in pairs:
                    cp(ub[:, 0:2], ub[:, W:W + 2], [ubk], [ubk], eng="dve")
                act(cgt[:, 0:W], cgt[:, 0:W], AF.Silu, [cgk], [cgk])
                tt(mg[:, jj, c0:c0 + W], cgt[:, 0:W], cvt[:, 0:W], ALU.mult, [cgk, cvk], [("mg", jj, ti)])
            cp(uhalo[:, cg, :], ubg[:, 0:2], [("ubg",)], [("uhalo",)], eng="dve")
            cp(uhalo[:, cv, :], ubv[:, 0:2], [("ubv",)], [("uhalo",)], eng="dve")

    def ffn_down(l, su, j0, j1, Wd):
        P.label = "ffn_down"
        nj = j1 - j0
        last = (su == 1 and j1 == NJ)
        for ti in (SUPERS[su][::-1] if last else SUPERS[su]):
            t0, W = TILES[ti]
            c0, _ = hn_cols(ti, su)
            for m in range(8):
                pst, pk = ps_next()
                for jj in range(nj):
                    wap, wk = Wd[jj]
                    mm(pst[:, 0:W], wap[:, m * 128:(m + 1) * 128], mg[:, jj, c0:c0 + W], jj == 0, jj == nj - 1,
                       [wk, ("mg", jj, ti)], [pk])
                tt(h[:, m, t0:t0 + W], h[:, m, t0:t0 + W], pst[:, 0:W], ALU.add, [hkey(m, ti), pk],
                   [hkey(m, ti)])

    def dump_dbg(src3, reads):
        dma("sp", dbgd[:], src3, reads, [("dbg",)])

    ALLST = [("Shg", i) for i in range(4)] + [("Dhg",), ("Rst",), ("hl",), ("alog",)]
    RG = [[0, 1, 2, 3], [4, 5, 6, 7]]

    def reset_states():
        P.add("dve", lambda e: e.memset(STT[:], 0.0), [], ALLST)
        P.add("dve", lambda e: e.memset(Dhg[:], 1.0), [], [("Dhg",)])
        P.add("dve", lambda e: e.memset(rsum[:], 0.0), [], [("rsum",)])

    def halo_exchange(l, sub, apply_now=True):
        P.label = "halo_x"
        dma("sp", xh_src.rearrange("p (k t) -> p k t", k=8), h[:, :, LT - 3:LT],
            [hkey(k, 4) for k in range(8)], [("xh_src",)])
        P.add("pool", lambda e: e.collective_compute("AllGather", ALU.bypass, replica_groups=RG,
                                                     ins=[xh_src[:]], outs=[xh_dst[:]]),
              [("xh_src",)], [("xh_dst",)], cc=True)
        dma("sp", hgat[:], xh_dst.rearrange("(j p) c -> p j c", p=128), [("xh_dst",)], [("hgat",)])
        if not apply_now:
            return
        halo_apply()

    def halo_apply():
        P.label = "halo_x"
        ht2 = halo_t[:].rearrange("p k t -> p (k t)")
        ts(ht2, hgat[:, 0, :], msk[:, 2:3], None, ALU.mult, None, [("hgat",), CONST], [("halo_t",)])
        for j in range(1, 4):
            stt(ht2, hgat[:, j, :], msk[:, 2 + j:3 + j], ht2, ALU.mult, ALU.add, [("hgat",), ("halo_t",), CONST],
                [("halo_t",)])
        stt(h[:, :, 13:16], h[:, :, 13:16], msk[:, 0:1], halo_t[:], ALU.mult, ALU.add,
            [("halo_t",), CONST] + [hkey(k, 0) for k in range(8)], [hkey(k, 0) for k in range(8)])

    def state_exchange(l):
        P.label = "state_x"
        tt(alog[:], rsum[:, 0:2], cng[:, l, :, 0], ALU.mult, [("rsum",), ("cng",)], [("alog",)])
        dma("sp", st_src[:], STT[:], ALLST, [("st_src",)])
        P.add("pool", lambda e: e.collective_compute("AllGather", ALU.bypass, replica_groups=RG,
                                                     ins=[st_src[:]], outs=[st_dst[:]]),
              [("st_src",)], [("st_dst",)], cc=True)
        SI = ("stin",)
        P.add("dve", lambda e: e.memset(STIN[:], 0.0), [], [SI])
        G = xin[:, 0:NST]
        GK = ("xo",)
        for j in range(3):
            dma("sp", G, st_dst[j * 128:(j + 1) * 128, :], [("st_dst",)], [GK])
            fs = msk[:, 6 + j:7 + j]
            for hh in range(4):
                cs = slice(hh * 128, (hh + 1) * 128)
                stt(G[:, cs], STIN[:, cs], G[:, 512 + hh:513 + hh], G[:, cs], ALU.mult, ALU.add, [SI, GK], [GK])
            stt(G[:, 516:580], STIN[:, 516:580], cfa("g2048"), G[:, 516:580], ALU.mult, ALU.add, [SI, GK, CONST], [GK])
            act(G[:, 582:584], G[:, 582:584], AF.Exp, [GK], [GK])
            tt(G[:, 582:584], G[:, 582:584], STIN[:, 580:582], ALU.mult, [GK, SI], [GK])
            tt(G[:, 580:582], G[:, 580:582], G[:, 582:584], ALU.add, [GK], [GK])
            tt(G[:, 0:582], G[:, 0:582], STIN[:, 0:582], ALU.subtract, [GK, SI], [GK])
            stt(STIN[:, 0:582], G[:, 0:582], fs, STIN[:, 0:582], ALU.mult, ALU.add, [GK, SI, CONST], [SI])
        cp(SBk[:, 0:512], STIN[:, 0:512], [SI], [("SBk",)], eng="act")
        cp(SBk[:, 512:576], STIN[:, 516:580], [SI], [("SBk",)], eng="act")

    def mixer_pass(l, so):
        def prologue(_):
            for j in range(2):
                P.add("dve", lambda e, j=j: e.memset(lxs[j][:, 0:3], 0.0), [], [("lxs", j)])
        stage(None, prologue)
        for su in range(2):
            if fused and not so:
                stage(None, lambda _, su=su: hg_main(l, su))
                stage(None, lambda _, su=su: ret_main(l, su))
                stage(None, lambda _, su=su: lru_main(l, su))
            else:
                stage(None, lambda _, su=su: norm_phase(lambda k: ppc(l, k), su))
                for hh in range(4):
                    stage(lambda hh=hh: load_w(winx[l], hh * 512, 512, wu_next(1)),
                          lambda Wk, hh=hh, su=su: hg_group(l, hh, su, Wk, so))
                stage(lambda: load_w(winx[l], 2048, 1024, wu_next(2)), lambda Wk, su=su: ret_group(l, su, Wk, so))
                if fused and so and su == 0:
                    stage(None, lambda _: (halo_apply(), norm_phase(lambda k: ppc(l, k), 0, tiles=[0])))
                stage(lambda: load_w(winx[l], 3072, 512, wu_next(1)), lambda Wk, su=su: lru_group(l, su, Wk, so))
            if so:
                continue
            if dbg == ("mg", l) and su == 0:
                stage(None, lambda _: dma("pool", dbgd[:, :, 0:1040], mg[:],
                                          [("mg", k, ti) for k in range(8) for ti in range(3)], [("dbg",)]))
            wout_phase(l, su)

    stage(None, lambda _: reset_states())
    for l in range(nlayers):
        stage(None, lambda _, l=l: dma("pool", lrug[:], lrugd[l * 4:(l + 1) * 4].rearrange("g p c -> p g c"), [],
                                       [("lrug",)]))
        if fused:
            stage(None, lambda _, l=l: halo_exchange(l, 0, apply_now=False))
            mixer_pass(l, True)
            stage(None, lambda _, l=l: (state_exchange(l), reset_states()))
        else:
            def pre(_, l=l):
                dma("sp", STIN[:], st_in[l], [], [("stin",)])
                halo_dump(l, 0)
                halo_inject(l, 0)
            stage(None, pre)
        mixer_pass(l, False)

        def mid(_, l=l):
            if dbg == ("hmid", l):
                dump_dbg(h[:], [hkey(k, ti) for k in range(8) for ti in range(5)])
            if not fused:
                dma("sp", st_out[l], STT[:], ALLST, [("st_out", l)])
            reset_states()
            if fused:
                halo_exchange(l, 1, apply_now=False)
            else:
                halo_dump(l, 1)
                halo_inject(l, 1)
            P.add("dve", lambda e: e.memset(uhalo[:], 0.0), [], [("uhalo",)])
        stage(None, mid)
        for su in range(2):
            if fused and su == 0:
                def ffn_norm0(_, l=l):
                    norm_phase(lambda k: ppc(l, 8 + k), 0, tiles=[1, 2])
                    halo_apply()
                    norm_phase(lambda k: ppc(l, 8 + k), 0, tiles=[0])
                stage(None, ffn_norm0)
            else:
                stage(None, lambda _, l=l, su=su: norm_phase(lambda k: ppc(l, 8 + k), su))
            ffn_phase(l, su)
        if dbg == ("h", l):
            stage(None, lambda _: dump_dbg(h[:], [hkey(k, ti) for k in range(8) for ti in range(5)]))
    run_stages()

    fg = lambda k: pp[:, NL * PL + k: NL * PL + k + 1]
    for ti in range(1, 5):
        t0, W = TILES[ti]
        pst, pk = ps_next()
        for k in range(8):
            act(sq[:, k % 4, 0:W], h[:, k, t0:t0 + W], AF.Square, [hkey(k, ti)], [("sq", k % 4)])
            mm(pst[:, 0:W], cba("ones1024"), sq[:, k % 4, 0:W], k == 0, k == 7, [("sq", k % 4), CONST], [pk])
        rstd, rk = f_next()
        act(rstd[:, 0:W], pst[:, 0:W], AF.Ln, [pk], [rk], bias=EPS)
        act(rstd[:, 0:W], rstd[:, 0:W], AF.Exp, [rk], [rk], scale=-0.5)
        for k in range(8):
            stt(h[:, k, t0:t0 + W], h[:, k, t0:t0 + W], fg(k), rstd[:, 0:W], ALU.mult, ALU.mult,
                [hkey(k, ti), rk, CONST], [hkey(k, ti)])
        for tb in range(4):
            tt0 = t0 + tb * 128
            for kg in range(2):
                pst2, pk2 = ps_next()
                for kk in range(4):
                    k = kg * 4 + kk
                    tr(pst2[:, kk * 128:(kk + 1) * 128], h[:, k, tt0:tt0 + 128], ident, [hkey(k, ti), CONST], [pk2])
                ob_, obk_ = f_next()
                cp(ob_[:, :], pst2[:, :], [pk2], [obk_], eng="act" if kg else "dve")
                dma("sp", outd[tt0 - PRE: tt0 - PRE + 128, kg * 512:(kg + 1) * 512], ob_[:, :], [obk_],
                    [("out", ti, tb, kg)])
    fin_reads = [("out", ti, tb, kg) for ti in range(1, 5) for tb in range(4) for kg in range(2)]
    if not fused:
        fin_reads += [("halo_out", i) for i in range(2 * nlayers)]
        fin_reads += [("st_out", l) for l in range(nlayers)]
    if dbg is not None:
        fin_reads.append(("dbg",))
    P.add("sp", None, fin_reads, [])
    nops, nwait = P.emit()
    _CACHE["last_ops"] = [(o["eng"], o["label"]) for o in P.ops if o["fn"] is not None]
    return nc, nops, nwait


FUSED = True


def _get_program(fused):
    key = ("nc", fused)
    if key not in _CACHE:
        _CACHE[key] = build_program(fused=fused)[0]
    return _CACHE[key]


def prepare_static(inp):
    cfd, cbd = make_consts()
    cstf = np.concatenate([cfd[n] for n, _ in CF_ORDER], axis=1).astype(np.float32)
    cstb = np.concatenate([cbd[n] for n, _ in CB_ORDER], axis=1).astype(np.float32)
    perm = winx_perm()
    st = {
        "pp": pack_params(inp),
        "cstf": np.ascontiguousarray(cstf),
        "cstb": np.ascontiguousarray(cstb),
        "w_in_x": np.ascontiguousarray(inp["w_in"][:, :, perm]),
        "w_out": np.ascontiguousarray(inp["w_out"]),
        "w_up": np.ascontiguousarray(inp["ffn_w_up"]),
        "w_down": np.ascontiguousarray(inp["ffn_w_down"]),
        "lrug": np.ascontiguousarray(pack_lrug(inp).reshape(NL * 4, 128, 128)),
    }
    return st


def seg_inputs(inp, st, b, s, states=None):
    x = inp["x"]
    xs = np.zeros((LT, D), np.float32)
    if s == 0:
        xs[:PRE] = inp["meta_tokens"]
    xs[PRE:] = x[b, s * NR:(s + 1) * NR]
    pos = np.arange(LT, dtype=np.float32) + s * NR
    m = np.zeros((128, 9), np.float32)
    m[:, 0] = 1.0 if s == 0 else 0.0
    m[:, 1] = 0.0 if s == 0 else 1.0
    if s > 0:
        m[:, 2 + (s - 1)] = 1.0
    for j in range(3):
        m[:, 6 + j] = 1.0 if j < s else 0.0
    d = dict(st)
    d["xs"] = xs
    d["rope"] = rope_tables(pos)
    d["wqseg"] = _WQSEG
    d["msk"] = m
    if states is not None:
        d.update(states)
    return d


def zero_states():
    return {
        "st_in": np.zeros((NL, 128, 584), np.float32),
        "halo_in": np.zeros((NL * 2, 128, 24), np.float32),
    }


def kernel(**inputs):
    inp = {k: np.asarray(v) for k, v in inputs.items()}
    st = prepare_static(inp)
    out = np.zeros((2, NSEG * NR, D), np.float32)
    if FUSED:
        nc = _get_program(True)
        in_maps = [seg_inputs(inp, st, r // NSEG, r % NSEG) for r in range(8)]
        res = run_bass_kernel_spmd(nc, in_maps, core_ids=list(range(8)))
        for r in range(8):
            b, s = r // NSEG, r % NSEG
            out[b, s * NR:(s + 1) * NR] = res.results[r]["out"]
        return out
    nc = _get_program(False)
    states = [zero_states(), zero_states()]
    for s in range(NSEG):
        in_maps = [seg_inputs(inp, st, b, s, states[b]) for b in range(2)]
        res = run_bass_kernel_spmd(nc, in_maps, core_ids=[0, 1])
        for b in range(2):
            r = res.results[b]
            out[b, s * NR:(s + 1) * NR] = r["out"]
            states[b] = {"st_in": r["st_out"], "halo_in": r["halo_out"]}
    return out
```

```python
import numpy as np
import concourse.bass as bass
import concourse.mybir as mybir
from concourse.bass_utils import run_bass_kernel_spmd

F32 = mybir.dt.float32
BF16 = mybir.dt.bfloat16
AF = mybir.ActivationFunctionType
ALU = mybir.AluOpType

D = 1024
NL = 4
PRE = 16
NR = 2048
LT = PRE + NR
NSEG = 4
DFF = 2816
NJ = DFF // 128
EPS = 1e-6
WINX = 3584
PL = 214
NPP = NL * PL + 8
TILES = [(0, 16)] + [(16 + 512 * i, 512) for i in range(4)]
SUPERS = [[0, 1, 2], [3, 4]]
SUP_COLS = [(0, 1040), (1040, 1024)]
FFN_GROUPS = [(0, 8), (8, 16), (16, 22)]
SEM_MAX = 60000
_CACHE = {}


def chunks_of(W):
    if W <= 64:
        return [(0, W)]
    return [(i * 64, 64) for i in range(W // 64)]


class Prog:
    def __init__(self, nc):
        self.nc = nc
        self.ops = []
        self.label = ""

    def add(self, eng, fn, reads=(), writes=(), dma=False, cc=False):
        self.ops.append({"eng": eng, "fn": fn, "r": tuple(reads), "w": tuple(writes), "dma": dma or cc,
                         "inc": False, "deps": (), "cc": cc, "label": self.label})

    def emit(self):
        nc = self.nc
        ops = self.ops
        engobj = {"pe": nc.tensor, "act": nc.scalar, "dve": nc.vector, "pool": nc.gpsimd, "sp": nc.sync}
        lastw = {}
        readers = {}
        for i, op in enumerate(ops):
            deps = set()
            raw = set()
            for r in op["r"]:
                if r in lastw:
                    deps.add(lastw[r])
                    raw.add(lastw[r])
            for w in op["w"]:
                if w in lastw:
                    deps.add(lastw[w])
                deps.update(readers.get(w, ()))
            need = []
            for j in deps:
                oj = ops[j]
                if (not oj["dma"]) and (not op["dma"]) and oj["eng"] == op["eng"]:
                    if not (j in raw and op["eng"] != "pe"):
                        continue
                need.append(j)
            op["deps"] = need
            for j in need:
                ops[j]["inc"] = True
            for r in op["r"]:
                readers.setdefault(r, []).append(i)
            for w in op["w"]:
                lastw[w] = i
                readers[w] = []
        cnt = {}
        esems = {}
        NDS = 6
        dsems = {}
        dslot_n = {}
        dslot_val = {}
        for i, op in enumerate(ops):
            if not op["inc"]:
                continue
            e = op["eng"]
            if op["cc"]:
                op["prev"] = (None, 0)
                op["sem"] = (nc.alloc_semaphore(f"cc_{i}"), 1)
            elif op["dma"]:
                if e not in dsems:
                    dsems[e] = [nc.alloc_semaphore(f"dq_{e}_{k}") for k in range(NDS)]
                    dslot_n[e] = 0
                    dslot_val[e] = [0] * NDS
                slot = dslot_n[e] % NDS
                dslot_n[e] += 1
                op["prev"] = (dsems[e][slot], dslot_val[e][slot])
                dslot_val[e][slot] += 16
                assert dslot_val[e][slot] < 65000
                op["sem"] = (dsems[e][slot], dslot_val[e][slot])
            else:
                c = cnt.get(e, 0)
                ep = c // SEM_MAX
                lst = esems.setdefault(e, [])
                while len(lst) <= ep:
                    lst.append(nc.alloc_semaphore(f"es_{e}_{len(lst)}"))
                op["sem"] = (lst[ep], c % SEM_MAX + 1)
                cnt[e] = c + 1
        waited = {}
        nwait = 0
        for i, op in enumerate(ops):
            e = op["eng"]
            E = engobj[e]
            best = {}
            for j in op["deps"]:
                s, v = ops[j]["sem"]
                k = id(s)
                if k not in best or best[k][1] < v:
                    best[k] = (s, v)
            if op["dma"] and op["inc"] and op["prev"][1] > 0:
                s, v = op["prev"]
                k = id(s)
                if k not in best or best[k][1] < v:
                    best[k] = (s, v)
            for k, (s, v) in best.items():
                if waited.get((e, k), 0) >= v:
                    continue
                E.wait_ge(s, v)
                waited[(e, k)] = v
                nwait += 1
            if op["fn"] is None:
                continue
            ins = op["fn"](E)
            if op["inc"]:
                s, v = op["sem"]
                ins.then_inc(s, 16 if (op["dma"] and not op["cc"]) else 1)
        return len(ops), nwait


def _gammas():
    return (1.0 - np.exp2(-5.0 - np.arange(4, dtype=np.float64)))


def make_consts():
    g = _gammas()
    cf = {}
    cf["ident"] = np.eye(128, dtype=np.float32)
    cm = np.ones((128, 512), np.float32)
    cm[:, ::64] = 0.0
    cf["cmask"] = cm
    t = np.arange(64)
    wq = np.zeros((128, 512), np.float64)
    for p in range(128):
        wq[p] = np.tile(g[p // 32] ** (t + 1.0), 8)
    cf["wq"] = wq.astype(np.float32)
    gg = np.zeros((128, 2), np.float64)
    for p in range(128):
        gg[p, 0] = g[p // 32] ** 64
        gg[p, 1] = g[p // 32] ** 16
    cf["g6416"] = gg.astype(np.float32)
    g2 = np.zeros((128, 1), np.float64)
    for p in range(128):
        g2[p, 0] = g[p // 32] ** 2048
    cf["g2048"] = g2.astype(np.float32)
    cb = {}
    cb["identb"] = np.eye(128, dtype=np.float32)
    cb["ones1024"] = np.full((128, 128), 1.0 / 1024, np.float32)
    cb["ones128"] = np.full((128, 128), 1.0 / 128, np.float32)
    bo = np.zeros((128, 128), np.float32)
    bo[:64, :64] = 1.0 / 64
    bo[64:, 64:] = 1.0 / 64
    cb["bones64"] = bo
    m = np.zeros((128, 512), np.float32)
    s = np.arange(64)[:, None]
    tt = np.arange(64)[None, :]
    m[:64] = np.tile((s <= tt).astype(np.float32), (1, 8))
    cb["maskT"] = m
    dm = np.zeros((128, 4 * 512), np.float64)
    for hd in range(4):
        blk = np.where(tt >= s, g[hd] ** np.maximum(tt - s, 0), 0.0) * 32 ** -0.5
        dm[:64, hd * 512:(hd + 1) * 512] = np.tile(blk, (1, 8))
    cb["dmatT"] = dm.astype(np.float32)
    ws = np.zeros((128, 1024), np.float64)
    ws16 = np.zeros((128, 128), np.float64)
    for hd in range(4):
        col = (g[hd] ** (63.0 - np.arange(64))) * 32 ** -0.5
        col16 = (g[hd] ** (15.0 - np.arange(16))) * 32 ** -0.5
        for c in range(8):
            ws[:64, c * 128 + hd * 32: c * 128 + hd * 32 + 32] = col[:, None]
        ws16[:16, hd * 32: hd * 32 + 32] = col16[:, None]
    cb["wst"] = ws.astype(np.float32)
    cb["wst16"] = ws16.astype(np.float32)
    return cf, cb


CF_ORDER = [("ident", 128), ("cmask", 512), ("wq", 512), ("g6416", 2), ("g2048", 1)]
CB_ORDER = [("identb", 128), ("ones1024", 128), ("ones128", 128), ("bones64", 128), ("maskT", 512),
            ("dmatT", 2048), ("wst", 1024), ("wst16", 128)]


def _offsets(order):
    o = {}
    c = 0
    for n, w in order:
        o[n] = (c, w)
        c += w
    return o, c


CF_OFF, NCF = _offsets(CF_ORDER)
CB_OFF, NCB = _offsets(CB_ORDER)


def rope_tables(pos):
    theta = 10000.0 ** (-np.linspace(0.0, 1.0, 16, dtype=np.float32))
    ang = (pos.astype(np.float32)[:, None] * theta[None, :]).astype(np.float32)
    cos = np.cos(ang).astype(np.float32)
    sin = np.sin(ang).astype(np.float32)
    out = np.zeros((2, 128, pos.shape[0]), np.float32)
    for hd in range(4):
        for dd in range(32):
            p = hd * 32 + dd
            out[0, p] = cos[:, dd % 16]
            out[1, p] = (-sin[:, dd] if dd < 16 else sin[:, dd - 16])
    return out


def make_wqseg():
    g = _gammas()
    out = np.zeros((128, LT), np.float64)
    t = np.arange(NR, dtype=np.float64)
    for p in range(128):
        out[p, PRE:] = g[p // 32] ** (t + 1.0)
    return out.astype(np.float32)


_WQSEG = make_wqseg()


def winx_perm():
    hq, hf, hi, hg, rq, rk, rv, rg, lx, ly = 0, 512, 1024, 1536, 2048, 2176, 2304, 2560, 2816, 3072
    cols = []
    for hh in range(4):
        r = np.arange(hh * 128, (hh + 1) * 128)
        cols += [hq + r, hg + r, hf + r, hi + r]
    sw = np.concatenate([np.concatenate([np.arange(h * 32 + 16, h * 32 + 32), np.arange(h * 32, h * 32 + 16)])
                         for h in range(4)])
    cols += [rq + np.arange(128), rq + sw, rg + np.arange(256), rk + np.arange(128), rk + sw, rv + np.arange(256)]
    cols += [lx + np.arange(256), ly + np.arange(256)]
    p = np.concatenate(cols)
    assert p.shape[0] == WINX
    return p


def pack_params(inp):
    pp = np.zeros((128, NPP), np.float32)
    for l in range(NL):
        b = l * PL
        pp[:, b + 0:b + 8] = inp["norm_mix_g"][l].reshape(8, 128).T
        pp[:, b + 8:b + 16] = inp["norm_ffn_g"][l].reshape(8, 128).T
        pp[:, b + 16:b + 20] = inp["hg_lb_logits"][l].reshape(4, 128).T
        pp[:, b + 20] = inp["hg_norm_g"][l]
        pp[:, b + 21] = np.tile(inp["ret_norm_g"][l], 2)
        cw = inp["lru_conv_w"][l]
        for j in range(2):
            for tap in range(4):
                pp[:, b + 22 + j * 4 + tap] = cw[tap, j * 128:(j + 1) * 128]
        pp[:, b + 30:b + 32] = inp["lru_conv_b"][l].reshape(2, 128).T
        pp[:, b + 32:b + 34] = inp["lru_gate_a_b"][l].reshape(2, 128).T
        pp[:, b + 34:b + 36] = inp["lru_gate_i_b"][l].reshape(2, 128).T
        pp[:, b + 36:b + 38] = inp["lru_lambda"][l].reshape(2, 128).T
        fw = inp["ffn_conv_w"][l]
        for c in range(44):
            for tap in range(3):
                pp[:, b + 38 + c * 3 + tap] = fw[tap, c * 128:(c + 1) * 128]
        pp[:, b + 170:b + 214] = inp["ffn_conv_b"][l].reshape(44, 128).T
    pp[:, NL * PL:NL * PL + 8] = inp["final_norm_g"].reshape(8, 128).T
    return pp


def pack_lrug(inp):
    out = np.zeros((NL, 2, 2, 128, 128), np.float32)
    for l in range(NL):
        for gi, nm in enumerate(("lru_gate_a_w", "lru_gate_i_w")):
            w = inp[nm][l]
            for j in range(2):
                for bb in range(2):
                    out[l, gi, j, bb * 64:(bb + 1) * 64, bb * 64:(bb + 1) * 64] = w[j * 2 + bb]
    return out


def build_program(nlayers=NL, fused=False, dbg=None):
    nc = bass.Bass("TRN2", target_bir_lowering=False)
    P = Prog(nc)

    def din(name, shape):
        return nc.dram_tensor(name, shape, F32, kind="ExternalInput").ap()

    def dout(name, shape):
        return nc.dram_tensor(name, shape, F32, kind="ExternalOutput").ap()

    xs = din("xs", [LT, D])
    ppd = din("pp", [128, NPP])
    cstf = din("cstf", [128, NCF])
    cstb = din("cstb", [128, NCB])
    roped = din("rope", [2, 128, LT])
    wqsegd = din("wqseg", [128, LT])
    mskd = din("msk", [128, 9])
    winx = din("w_in_x", [NL, D, WINX])
    woutd = din("w_out", [NL, D, D])
    wupd = din("w_up", [NL, D, 2 * DFF])
    wdnd = din("w_down", [NL, DFF, D])
    lrugd = din("lrug", [NL * 4, 128, 128])
    outd = dout("out", [NR, D])
    NST = 584
    if not fused:
        st_in = din("st_in", [NL, 128, NST])
        halo_in = din("halo_in", [NL * 2, 128, 24])
        st_out = dout("st_out", [NL, 128, NST])
        halo_out = dout("halo_out", [NL * 2, 128, 24])
    else:
        xh_src = nc.dram_tensor("xh_src", [128, 24], F32, kind="Internal").ap()
        xh_dst = nc.dram_tensor("xh_dst", [4 * 128, 24], F32, kind="Internal").ap()
        st_src = nc.dram_tensor("st_src", [128, NST], F32, kind="Internal").ap()
        st_dst = nc.dram_tensor("st_dst", [4 * 128, NST], F32, kind="Internal").ap()
        lscr = nc.dram_tensor("lscr", [6, 128, LT], F32, kind="Internal").ap()
        hscr_o = nc.dram_tensor("hscr_o", [4, 128, LT], F32, kind="Internal").ap()
        hscr_q = nc.dram_tensor("hscr_q", [4, 128, LT], BF16, kind="Internal").ap()
        hscr_g = nc.dram_tensor("hscr_g", [4, 128, LT], BF16, kind="Internal").ap()
        rscr_o = nc.dram_tensor("rscr_o", [2, 128, LT], F32, kind="Internal").ap()
        rscr_q = nc.dram_tensor("rscr_q", [128, LT], BF16, kind="Internal").ap()
        rscr_g = nc.dram_tensor("rscr_g", [2, 128, LT], BF16, kind="Internal").ap()
    dbgd = None
    if dbg is not None:
        dbgd = dout("dbg", [128, 8, LT])

    def sb(name, shape, dt=F32):
        return nc.alloc_sbuf_tensor(name, shape, dt)

    h = sb("h", [128, 8, LT])
    hn = sb("hn", [128, 8, 1040], BF16)
    mg = sb("mg", [128, 8, 1040], BF16)
    pp = sb("pp_sb", [128, NPP])
    der = sb("der", [128, 4, 4, 3])
    cng = sb("cng", [128, 4, 2, 2])
    cf = sb("cf", [128, NCF])
    cbt = sb("cbt", [128, NCB], BF16)
    msk = sb("msk_sb", [128, 9])
    ropec = sb("ropec", [128, 512])
    ropes = sb("ropes", [128, 512])
    lrug = sb("lrug_sb", [128, 4, 128], BF16)
    NWU = 4
    wb = [sb(f"wb{i}", [128, 4096], BF16) for i in range(NWU)]
    NFS = 8
    Fs = [sb(f"F{i}", [128, 512]) for i in range(NFS)]
    NBS = 8
    Bs = [sb(f"B{i}", [128, 512], BF16) for i in range(NBS)]
    TM1 = sb("TM1", [64, 2048], BF16)
    TM2 = sb("TM2", [64, 1024], BF16)
    SBk = sb("SBk", [128, 1024], BF16)
    sq = sb("sq", [128, 4, 512], BF16)
    STT = sb("STT", [128, 584])
    Shg = [STT[:, i * 128:(i + 1) * 128] for i in range(4)]
    Dhg = STT[:, 512:516]
    Rst = STT[:, 516:580]
    hl = STT[:, 580:582]
    alog = STT[:, 582:584]
    rsum = sb("rsum", [128, 4])
    STIN = sb("STIN", [128, 584])
    lxs = [sb(f"lxs{j}", [128, 515]) for j in range(2)]
    ubg = sb("ubg", [128, 514])
    ubv = sb("ubv", [128, 514])
    uhalo = sb("uhalo", [128, 44, 2])
    stin_hg = STIN[:, 0:512]
    stin_ret = STIN[:, 516:580]
    stin_lru = STIN[:, 580:582]
    hgat = sb("hgat", [128, 4, 24])
    halo_sb = sb("halo_sb", [128, 8, 3])
    halo_t = sb("halo_t", [128, 8, 3])
    xin = sb("xin", [128, 1024])
    osb = xin
    tiny = sb("tiny", [128, 64])
    gsc = sb("gsc", [128, 32])
    ret_qw = sb("ret_qw", [128, 512], BF16)
    ret_gs = [sb(f"ret_gs{j}", [128, 512], BF16) for j in range(2)]

    NPS = 6
    PS = [nc.alloc_psum_tensor(f"ps{i}", [128, 512], F32) for i in range(NPS)]
    PT = [nc.alloc_psum_tensor(f"pt{i}", [128, 1024], BF16) for i in range(2)]
    ctr = {"ps": 0, "pt": 0, "F": 0, "B": 0, "wu": 0}

    def ps_next():
        i = ctr["ps"] % NPS
        ctr["ps"] += 1
        return PS[i], ("ps", i)

    def pt_next():
        i = ctr["pt"] % 2
        ctr["pt"] += 1
        return PT[i], ("pt", i)

    def f_next():
        i = ctr["F"] % NFS
        ctr["F"] += 1
        return Fs[i], ("F", i)

    def b_next():
        i = ctr["B"] % NBS
        ctr["B"] += 1
        return Bs[i], ("B", i)

    def wu_next(n=1):
        res = []
        for _ in range(n):
            i = ctr["wu"] % NWU
            ctr["wu"] += 1
            res.append((wb[i], ("wu", i)))
        return res

    def cfa(name):
        o, w = CF_OFF[name]
        return cf[:, o:o + w]

    def cba(name):
        o, w = CB_OFF[name]
        return cbt[:, o:o + w]

    def ppc(l, c, n=1):
        return pp[:, l * PL + c: l * PL + c + n]

    def act(out, in_, func, reads, writes, bias=None, scale=None):
        kw = {}
        if bias is not None:
            kw["bias"] = bias
        if scale is not None:
            kw["scale"] = scale
        P.add("act", lambda e: e.activation(out=out, in_=in_, func=func, **kw), reads, writes)

    def tt(out, in0, in1, op, reads, writes, eng="dve"):
        P.add(eng, lambda e: e.tensor_tensor(out=out, in0=in0, in1=in1, op=op), reads, writes)

    def ts(out, in0, s1, s2, op0, op1, reads, writes, eng="dve"):
        if op1 is None:
            P.add(eng, lambda e: e.tensor_scalar(out=out, in0=in0, scalar1=s1, scalar2=None, op0=op0), reads, writes)
        else:
            P.add(eng, lambda e: e.tensor_scalar(out=out, in0=in0, scalar1=s1, scalar2=s2, op0=op0, op1=op1),
                  reads, writes)

    def stt(out, in0, scalar, in1, op0, op1, reads, writes, eng="dve"):
        P.add(eng, lambda e: e.scalar_tensor_tensor(out=out, in0=in0, scalar=scalar, in1=in1, op0=op0, op1=op1),
              reads, writes)

    def cp(out, in_, reads, writes, eng="dve"):
        if eng == "act":
            P.add("act", lambda e: e.activation(out=out, in_=in_, func=AF.Copy), reads, writes)
        else:
            P.add(eng, lambda e: e.tensor_copy(out=out, in_=in_), reads, writes)

    def mm(out, lhsT, rhs, start, stop, reads, writes, tp=None):
        if tp is None:
            P.add("pe", lambda e: e.matmul(out, lhsT, rhs, start=start, stop=stop), reads, writes)
        else:
            P.add("pe", lambda e: e.matmul(out, lhsT, rhs, start=start, stop=stop, tile_position=tp), reads, writes)

    def tr(out, in_, ident, reads, writes):
        P.add("pe", lambda e: e.transpose(out, in_, ident), reads, writes)

    def dma(q, out, in_, reads, writes):
        P.add(q, lambda e: e.dma_start(out=out, in_=in_), reads, writes, dma=True)

    CONST = ("const",)

    dma("sp", pp[:], ppd[:], [], [CONST])
    dma("sp", cf[:], cstf[:], [], [CONST])
    dma("pool", cbt[:], cstb[:], [], [CONST])
    dma("sp", msk[:], mskd[:], [], [CONST])
    P.add("dve", lambda e: e.memset(uhalo[:], 0.0), [], [("uhalo",)])
    P.add("dve", lambda e: e.memset(gsc[:], 0.0), [], [("gsc",)])
    for j in range(2):
        P.add("dve", lambda e, j=j: e.memset(lxs[j][:], 0.0), [], [("lxs", j)])

    def lg(l):
        return pp[:, l * PL + 16: l * PL + 20]
    mx = tiny[:, 0:4]
    ex = tiny[:, 4:20].rearrange("p (l h) -> p l h", l=4)
    ssum = tiny[:, 20:24]
    TK = ("tiny",)
    tt(mx, lg(0), lg(1), ALU.max, [CONST], [TK])
    tt(mx, mx, lg(2), ALU.max, [CONST, TK], [TK])
    tt(mx, mx, lg(3), ALU.max, [CONST, TK], [TK])
    for l in range(4):
        tt(ex[:, l, :], lg(l), mx, ALU.subtract, [CONST, TK], [TK])
    act(tiny[:, 4:20], tiny[:, 4:20], AF.Exp, [TK], [TK])
    tt(ssum, ex[:, 0, :], ex[:, 1, :], ALU.add, [TK], [TK])
    tt(ssum, ssum, ex[:, 2, :], ALU.add, [TK], [TK])
    tt(ssum, ssum, ex[:, 3, :], ALU.add, [TK], [TK])
    P.add("dve", lambda e: e.reciprocal(out=ssum, in_=ssum), [TK], [TK])
    for l in range(4):
        tt(ex[:, l, :], ex[:, l, :], ssum, ALU.mult, [TK], [TK])
    DK = ("der",)
    P.add("dve", lambda e: e.memset(der[:, 0, :, 0], 0.0), [], [DK])
    for l in range(1, 4):
        tt(der[:, l, :, 0], der[:, l - 1, :, 0], ex[:, l, :], ALU.add, [TK, DK], [DK])
    for l in range(4):
        ts(der[:, l, :, 1], der[:, l, :, 0], -1.0, 1.0, ALU.mult, ALU.add, [DK], [DK])
        ts(der[:, l, :, 2], der[:, l, :, 0], 1.0, None, ALU.subtract, None, [DK], [DK])
    CK = ("cng",)
    for l in range(4):
        lam = pp[:, l * PL + 36: l * PL + 38]
        act(tiny[:, 32:34], lam, AF.Exp, [CONST, TK], [TK], scale=-1.0)
        act(tiny[:, 34:36], tiny[:, 32:34], AF.Ln, [TK], [TK], bias=1.0)
        ts(cng[:, l, :, 0], tiny[:, 34:36], -8.0, None, ALU.mult, None, [TK], [CK])
        ts(cng[:, l, :, 1], tiny[:, 34:36], -16.0, None, ALU.mult, None, [TK], [CK])

    ident = cfa("ident")
    blocks = [(0, 16)] + [(16 + 128 * i, 128) for i in range(16)]

    def hkey(k, ti):
        return ("h", k, ti)

    def tile_of(t0):
        for ti, (a, w) in enumerate(TILES):
            if a <= t0 < a + w:
                return ti
        raise ValueError

    for (t0, nt) in blocks:
        ti = tile_of(t0)
        for kg in range(2):
            xh_, xhk_ = f_next()
            dma("sp", xh_[0:nt, :], xs[t0:t0 + nt, kg * 512:(kg + 1) * 512], [], [xhk_])
            pst, pk = ps_next()
            for kk in range(4):
                k = kg * 4 + kk
                tr(pst[:, kk * 128: kk * 128 + nt], xh_[0:nt, kk * 128:(kk + 1) * 128], ident[0:nt, 0:nt],
                   [xhk_, CONST], [pk])
            cp(h[:, kg * 4:(kg + 1) * 4, t0:t0 + nt],
               pst[:, :].rearrange("p (k t) -> p k t", k=4)[:, :, 0:nt],
               [pk], [hkey(k, ti) for k in range(kg * 4, kg * 4 + 4)], eng="act" if kg else "dve")

    if dbg == ("x", 0):
        dma("sp", dbgd[:], h[:], [hkey(k, ti) for k in range(8) for ti in range(5)], [("dbg",)])
    def halo_inject(l, sub):
        idx = l * 2 + sub
        dma("sp", halo_sb[:].rearrange("p k t -> p (k t)"), halo_in[idx], [], [("halo_sb",)])
        ts(halo_t[:], halo_sb[:], msk[:, 1:2], None, ALU.mult, None, [("halo_sb",), CONST], [("halo_t",)])
        stt(h[:, :, 13:16], h[:, :, 13:16], msk[:, 0:1], halo_t[:], ALU.mult, ALU.add,
            [("halo_t",), CONST] + [hkey(k, 0) for k in range(8)], [hkey(k, 0) for k in range(8)])

    def halo_dump(l, sub):
        idx = l * 2 + sub
        dma("sp", halo_out[idx].rearrange("p (k t) -> p k t", k=8), h[:, :, LT - 3:LT],
            [hkey(k, 4) for k in range(8)], [("halo_out", idx)])

    def norm_phase(gcol_fn, su, tiles=None):
        P.label = "norm"
        c_base = SUP_COLS[su][0]
        for ti in (SUPERS[su] if tiles is None else tiles):
            t0, W = TILES[ti]
            c0 = t0 - c_base
            pst, pk = ps_next()
            for k in range(8):
                act(sq[:, k % 4, 0:W], h[:, k, t0:t0 + W], AF.Square, [hkey(k, ti)], [("sq", k % 4)])
                mm(pst[:, 0:W], cba("ones1024"), sq[:, k % 4, 0:W], k == 0, k == 7, [("sq", k % 4), CONST], [pk])
            rstd, rk = f_next()
            act(rstd[:, 0:W], pst[:, 0:W], AF.Ln, [pk], [rk], bias=EPS)
            act(rstd[:, 0:W], rstd[:, 0:W], AF.Exp, [rk], [rk], scale=-0.5)
            for k in range(8):
                stt(hn[:, k, c0:c0 + W], h[:, k, t0:t0 + W], gcol_fn(k), rstd[:, 0:W], ALU.mult, ALU.mult,
                    [hkey(k, ti), rk, CONST], [("hn", k, ti)])

    def load_w(dram2d, c0, ncols, units, row0=0, nrows=D):
        nk = nrows // 128
        per_unit = 4096 // ncols if ncols <= 4096 else 0
        views = []
        src = dram2d[row0:row0 + nrows, c0:c0 + ncols].rearrange("(k p) c -> p k c", p=128)
        if nk * ncols <= 4096:
            wt, wk = units[0]
            dst = wt[:, 0:nk * ncols].rearrange("p (k c) -> p k c", k=nk)
            dma("pool", dst, src, [], [wk])
            return [(dst[:, k, :], wk) for k in range(nk)]
        kpu = 4096 // ncols
        res = []
        for ui in range((nk + kpu - 1) // kpu):
            wt, wk = units[ui]
            k0 = ui * kpu
            k1 = min(nk, k0 + kpu)
            dst = wt[:, 0:(k1 - k0) * ncols].rearrange("p (k c) -> p k c", k=k1 - k0)
            dma("pool", dst, src[:, k0:k1, :], [], [wk])
            for k in range(k0, k1):
                res.append((dst[:, k - k0, :], wk))
        return res

    def hn_cols(ti, su):
        t0, W = TILES[ti]
        c0 = t0 - SUP_COLS[su][0]
        return c0, W

    def proj_fm(Wk, col0, ti, su, ncols=128):
        c0, W = hn_cols(ti, su)
        pst, pk = ps_next()
        for k in range(8):
            wap, wk = Wk[k]
            mm(pst[0:ncols, 0:W], wap[:, col0:col0 + ncols], hn[:, k, c0:c0 + W], k == 0, k == 7,
               [wk, ("hn", k, ti)], [pk])
        return pst, pk

    def state_merge(state_ap, skey, in_ap, in_key, so=False):
        if so:
            ts(state_ap, state_ap, msk[:, 0:1], None, ALU.mult, None, [skey, CONST], [skey])
        else:
            stt(state_ap, state_ap, msk[:, 0:1], in_ap, ALU.mult, ALU.add, [skey, in_key, CONST], [skey])

    def hg_group(l, hh, su, Wk, so=False):
        P.label = "pre_hg" if so else "hg"
        lbc = der[:, l, hh, 0:1]
        omlc = der[:, l, hh, 1:2]
        nomlc = der[:, l, hh, 2:3]
        gn = ppc(l, 20)
        S = Shg[hh]
        SK = ("Shg", hh)
        for ti in SUPERS[su]:
            c0, W = hn_cols(ti, su)
            chs = chunks_of(W)
            nch = len(chs)
            full = (not so) or fused
            save = so and fused
            t0_ = TILES[ti][0]
            zf, zfk = proj_fm(Wk, 256, ti, su)
            vps = []
            for ci, (s0, cw) in enumerate(chs):
                if ci % 4 == 0:
                    vp, vk = ps_next()
                    vps.append((vp, vk))
                for k in range(8):
                    wap, wk = Wk[k]
                    mm(vp[0:cw, (ci % 4) * 128:(ci % 4 + 1) * 128], hn[:, k, c0 + s0:c0 + s0 + cw], wap[:, 384:512],
                       k == 0, k == 7, [wk, ("hn", k, ti)], [vk])
            sg, sgk = f_next()
            act(sg[:, 0:W], zf[:, 0:W], AF.Sigmoid, [zfk], [sgk])
            logf, lfk = f_next()
            act(logf[:, 0:W], sg[:, 0:W], AF.Ln, [sgk, ("der",)], [lfk], bias=lbc, scale=omlc)
            bT, btk = f_next()
            P.add("dve", lambda e, bT=bT, logf=logf, W=W: e.tensor_tensor_scan(
                out=bT[:, 0:W], data0=cfa("cmask")[:, 0:W], data1=logf[:, 0:W], initial=0.0,
                op0=ALU.mult, op1=ALU.add), [lfk, CONST], [btk])
            kT, ktk = f_next()
            ts(kT[:, 0:W], sg[:, 0:W], nomlc, omlc, ALU.mult, ALU.add, [sgk, ("der",)], [ktk])
            Ep, epk = f_next()
            act(Ep[:, 0:W], bT[:, 0:W], AF.Exp, [btk], [epk])
            nb, nbk = f_next()
            ts(nb[:, 0:W], bT[:, 0:W], -1.0, 80.0, ALU.mult, ALU.min, [btk], [nbk])
            En, enk = f_next()
            act(En[:, 0:W], nb[:, 0:W], AF.Exp, [nbk], [enk])
            kt, kttk = b_next()
            tt(kt[:, 0:W], kT[:, 0:W], En[:, 0:W], ALU.mult, [ktk, enk], [kttk])
            khT, khk = b_next()
            if W == 512:
                ep3 = Ep[:, 0:512].rearrange("p (c t) -> p c t", c=8)
                tt(khT[:, 0:512].rearrange("p (c t) -> p c t", c=8), kt[:, 0:512].rearrange("p (c t) -> p c t", c=8),
                   ep3[:, :, 63:64].broadcast_to([128, 8, 64]), ALU.mult, [kttk, epk], [khk])
            else:
                for (s0, cw) in chs:
                    ts(khT[:, s0:s0 + cw], kt[:, s0:s0 + cw], Ep[:, s0 + cw - 1:s0 + cw], None, ALU.mult, None,
                       [kttk, epk], [khk])
            for bi, (vp, vk) in enumerate(vps):
                n = min(4, nch - bi * 4)
                cwm = chs[0][1]
                cp(TM1[0:cwm, bi * 512: bi * 512 + n * 128], vp[0:cwm, 0:n * 128], [vk], [("TM1",)], eng="act")
            ptt, ptk = pt_next()
            for ci, (s0, cw) in enumerate(chs):
                tr(ptt[0:cw, ci * 128:(ci + 1) * 128], khT[:, s0:s0 + cw], cba("identb"), [khk, CONST], [ptk])
            cwm = chs[0][1]
            cp(TM2[0:cwm, 0:nch * 128], ptt[0:cwm, 0:nch * 128], [ptk], [("TM2",)], eng="dve")
            if full:
                zq, zqk = proj_fm(Wk, 0, ti, su)
                zg, zgk = proj_fm(Wk, 128, ti, su)
                qs, qsk = f_next()
                act(qs[:, 0:W], zq[:, 0:W], AF.Silu, [zqk], [qsk])
                gs, gsk = b_next()
                act(gs[:, 0:W], zg[:, 0:W], AF.Silu, [zgk], [gsk])
                qt, qtk = b_next()
                stt(qt[:, 0:W], qs[:, 0:W], 128.0 ** -0.5, Ep[:, 0:W], ALU.mult, ALU.mult, [qsk, epk], [qtk])
            if full:
                sc, sck = ps_next()
                for ci, (s0, cw) in enumerate(chs):
                    mm(sc[0:cw, ci * 64: ci * 64 + cw], kt[:, s0:s0 + cw], qt[:, s0:s0 + cw], True, True,
                       [kttk, qtk], [sck])
                AT, atk = b_next()
                wtot = (nch - 1) * 64 + chs[-1][1]
                stt(AT[0:cwm, 0:wtot], sc[0:cwm, 0:wtot], 1e30, cba("maskT")[0:cwm, 0:wtot], ALU.min, ALU.mult,
                    [sck, CONST], [atk])
            kvs = []
            for ci, (s0, cw) in enumerate(chs):
                if ci % 4 == 0:
                    kp, kk = ps_next()
                    kvs.append((kp, kk))
                mm(kp[:, (ci % 4) * 128:(ci % 4 + 1) * 128], TM2[0:cw, ci * 128:(ci + 1) * 128],
                   TM1[0:cw, ci * 128:(ci + 1) * 128], True, True, [("TM2",), ("TM1",)], [kk])
            XK = ("xo",)
            if full:
                cp(xin[:, 0:128], S[:], [SK], [XK], eng="dve")
            if save:
                qg, qgk = b_next()
                if ti == 0:
                    P.add("dve", lambda e, qg=qg, W=W: e.memset(qg[:, 0:W], 0.0), [], [qgk])
            for ci, (s0, cw) in enumerate(chs):
                kp, kk = kvs[ci // 4]
                dcol = Ep[:, s0 + cw - 1:s0 + cw]
                kv_ap = kp[:, (ci % 4) * 128:(ci % 4 + 1) * 128]
                if not full:
                    stt(S[:], S[:], dcol, kv_ap, ALU.mult, ALU.add, [SK, epk, kk], [SK])
                elif ci == nch - 1:
                    stt(S[:], xin[:, ci * 128:(ci + 1) * 128], dcol, kv_ap, ALU.mult, ALU.add, [XK, epk, kk], [SK])
                else:
                    stt(xin[:, (ci + 1) * 128:(ci + 2) * 128], xin[:, ci * 128:(ci + 1) * 128], dcol, kv_ap,
                        ALU.mult, ALU.add, [XK, epk, kk], [XK])
                if so and ti != 0 and not save:
                    ts(Dhg[:, hh:hh + 1], Dhg[:, hh:hh + 1], dcol, None, ALU.mult, None,
                       [("Dhg",), epk], [("Dhg",)])
            if save and ti != 0:
                GS = ("gsc",)
                cp(gsc[:, 0:1], Dhg[:, hh:hh + 1], [("Dhg",)], [GS], eng="dve")
                ep3 = Ep[:, 0:512].rearrange("p (c t) -> p c t", c=8)
                P.add("dve", lambda e, ep3=ep3, hh=hh: e.tensor_tensor_scan(
                    out=gsc[:, 1:9], data0=ep3[:, :, 63], data1=gsc[:, 16:24], initial=Dhg[:, hh:hh + 1],
                    op0=ALU.mult, op1=ALU.add), [epk, ("Dhg",), GS], [GS])
                cp(Dhg[:, hh:hh + 1], gsc[:, 8:9], [GS], [("Dhg",)], eng="dve")
                tt(qg[:, 0:512].rearrange("p (c t) -> p c t", c=8), qt[:, 0:512].rearrange("p (c t) -> p c t", c=8),
                   gsc[:, 0:8].rearrange("p (c o) -> p c o", o=1).broadcast_to([128, 8, 64]), ALU.mult,
                   [qtk, GS], [qgk])
            if full:
                cp(SBk[:, 0:nch * 128], xin[:, 0:nch * 128], [XK], [("SBk",)], eng="act")
            if ti == 0:
                state_merge(S[:], SK, stin_hg[:, hh * 128:(hh + 1) * 128], ("stin",), so)
            if not full:
                continue
            op_, ok = ps_next()
            for ci, (s0, cw) in enumerate(chs):
                mm(op_[:, s0:s0 + cw], TM1[0:cw, ci * 128:(ci + 1) * 128], AT[0:cw, ci * 64: ci * 64 + cw],
                   True, False, [("TM1",), atk], [ok])
                mm(op_[:, s0:s0 + cw], SBk[:, ci * 128:(ci + 1) * 128], qt[:, s0:s0 + cw], False, True,
                   [("SBk",), qtk], [ok])
            if save:
                ol, olk = f_next()
                cp(ol[:, 0:W], op_[:, 0:W], [ok], [olk], eng="act")
                dma("sp", hscr_o[hh][:, t0_:t0_ + W], ol[:, 0:W], [olk], [("hscr", 0, hh, ti)])
                dma("sp", hscr_q[hh][:, t0_:t0_ + W], qg[:, 0:W], [qgk], [("hscr", 1, hh, ti)])
                dma("sp", hscr_g[hh][:, t0_:t0_ + W], gs[:, 0:W], [gsk], [("hscr", 2, hh, ti)])
                continue
            osq, osk = b_next()
            act(osq[:, 0:W], op_[:, 0:W], AF.Square, [ok], [osk])
            msp, msk_ = ps_next()
            mm(msp[:, 0:W], cba("ones128"), osq[:, 0:W], True, True, [osk, CONST], [msk_])
            rstd, rk = f_next()
            act(rstd[:, 0:W], msp[:, 0:W], AF.Ln, [msk_], [rk], bias=EPS)
            act(rstd[:, 0:W], rstd[:, 0:W], AF.Exp, [rk], [rk], scale=-0.5)
            t1, t1k = f_next()
            stt(t1[:, 0:W], op_[:, 0:W], gn, rstd[:, 0:W], ALU.mult, ALU.mult, [ok, rk, CONST], [t1k])
            tt(mg[:, hh, c0:c0 + W], t1[:, 0:W], gs[:, 0:W], ALU.mult, [t1k, gsk], [("mg", hh, ti)])

    def hg_main(l, su):
        P.label = "hg"
        gn = ppc(l, 20)
        units = [(hh, ti) for hh in range(4) for ti in SUPERS[su]]

        def stage_a(hh, ti):
            c0, W = hn_cols(ti, su)
            t0_ = TILES[ti][0]
            d = dict(hh=hh, ti=ti, c0=c0, W=W)
            d["ol"], d["olk"] = f_next()
            dma("sp", d["ol"][:, 0:W], hscr_o[hh][:, t0_:t0_ + W], [("hscr", 0, hh, ti)], [d["olk"]])
            qg, qgk = b_next()
            dma("sp", qg[:, 0:W], hscr_q[hh][:, t0_:t0_ + W], [("hscr", 1, hh, ti)], [qgk])
            d["gs"], d["gsk"] = b_next()
            dma("sp", d["gs"][:, 0:W], hscr_g[hh][:, t0_:t0_ + W], [("hscr", 2, hh, ti)], [d["gsk"]])
            cps, ck = ps_next()
            mm(cps[:, 0:W], SBk[:, hh * 128:(hh + 1) * 128], qg[:, 0:W], True, True, [("SBk",), qgk], [ck])
            tt(d["ol"][:, 0:W], d["ol"][:, 0:W], cps[:, 0:W], ALU.add, [d["olk"], ck], [d["olk"]])
            d["osq"], d["osk"] = b_next()
            act(d["osq"][:, 0:W], d["ol"][:, 0:W], AF.Square, [d["olk"]], [d["osk"]])
            return d

        def stage_b(d):
            W, c0, hh, ti = d["W"], d["c0"], d["hh"], d["ti"]
            msp, mk = ps_next()
            mm(msp[:, 0:W], cba("ones128"), d["osq"][:, 0:W], True, True, [d["osk"], CONST], [mk])
            rstd, rk = f_next()
            act(rstd[:, 0:W], msp[:, 0:W], AF.Ln, [mk], [rk], bias=EPS)
            act(rstd[:, 0:W], rstd[:, 0:W], AF.Exp, [rk], [rk], scale=-0.5)
            stt(d["ol"][:, 0:W], d["ol"][:, 0:W], gn, rstd[:, 0:W], ALU.mult, ALU.mult, [d["olk"], rk, CONST],
                [d["olk"]])
            tt(mg[:, hh, c0:c0 + W], d["ol"][:, 0:W], d["gs"][:, 0:W], ALU.mult, [d["olk"], d["gsk"]],
               [("mg", hh, ti)])

        prev = None
        for (hh, ti) in units:
            cur = stage_a(hh, ti)
            if prev is not None:
                stage_b(prev)
            prev = cur
        stage_b(prev)

    def ret_group(l, su, Wk, so=False):
        P.label = "pre_ret" if so else "ret"
        gn = ppc(l, 21)
        RK = ("Rst",)
        for ti in SUPERS[su]:
            t0, W = TILES[ti]
            c0, _ = hn_cols(ti, su)
            chs = chunks_of(W)
            nch = len(chs)
            cwm = chs[0][1]
            full = (not so) or fused
            save = so and fused
            dma("sp", ropec[:, 0:W], roped[0][:, t0:t0 + W], [], [("ropec",)])
            dma("sp", ropes[:, 0:W], roped[1][:, t0:t0 + W], [], [("ropes",)])

            def rope_fm(colx, cols_):
                zx, zxk = proj_fm(Wk, colx, ti, su)
                zs, zsk = proj_fm(Wk, cols_, ti, su)
                a1, a1k = f_next()
                tt(a1[:, 0:W], zx[:, 0:W], ropec[:, 0:W], ALU.mult, [zxk, ("ropec",)], [a1k])
                a2, a2k = f_next()
                tt(a2[:, 0:W], zs[:, 0:W], ropes[:, 0:W], ALU.mult, [zsk, ("ropes",)], [a2k])
                r, rk_ = f_next()
                tt(r[:, 0:W], a1[:, 0:W], a2[:, 0:W], ALU.add, [a1k, a2k], [rk_])
                return r, rk_
            if full:
                qr, qrk = rope_fm(0, 128)
                qrb, qrbk = b_next()
                cp(qrb[:, 0:W], qr[:, 0:W], [qrk], [qrbk], eng="act")
                qw, qwk = ret_qw, ("ret_qw",)
                tt(qw[:, 0:W], qr[:, 0:W], cfa("wq")[:, 0:W], ALU.mult, [qrk, CONST], [qwk])
                if save:
                    wqt, wqtk = f_next()
                    dma("sp", wqt[:, 0:W], wqsegd[:, t0:t0 + W], [], [wqtk])
                    qwg, qwgk = b_next()
                    tt(qwg[:, 0:W], qr[:, 0:W], wqt[:, 0:W], ALU.mult, [qrk, wqtk], [qwgk])
                    dma("sp", rscr_q[:, t0:t0 + W], qwg[:, 0:W], [qwgk], [("rscr", "q", ti)])
            kr, krk = rope_fm(512, 640)
            krb, krbk = b_next()
            cp(krb[:, 0:W], kr[:, 0:W], [krk], [krbk], eng="act")
            gsl = []
            for j in range(2 if full else 0):
                zg, zgk = proj_fm(Wk, 256 + j * 128, ti, su)
                gs, gsk = ret_gs[j], ("ret_gs", j)
                act(gs[:, 0:W], zg[:, 0:W], AF.Silu, [zgk], [gsk])
                gsl.append((gs, gsk))
            for ci, (s0, cw) in enumerate(chs):
                if ci % 2 == 0:
                    vp, vk = ps_next()
                for k in range(8):
                    wap, wk = Wk[k]
                    mm(vp[0:cw, (ci % 2) * 256:(ci % 2 + 1) * 256], hn[:, k, c0 + s0:c0 + s0 + cw], wap[:, 768:1024],
                       k == 0, k == 7, [wk, ("hn", k, ti)], [vk])
                if ci % 2 == 1 or ci == nch - 1:
                    n = (ci % 2) + 1
                    cb0 = (ci - (ci % 2)) * 256
                    cp(TM1[0:cw, cb0: cb0 + n * 256], vp[0:cw, 0:n * 256], [vk], [("TM1",)], eng="act")
            ptt, ptk = pt_next()
            for ci, (s0, cw) in enumerate(chs):
                tr(ptt[0:cw, ci * 128:(ci + 1) * 128], krb[:, s0:s0 + cw], cba("identb"), [krbk, CONST], [ptk])
            wst_ap = cba("wst")[0:cwm, 0:nch * 128] if W == 512 else cba("wst16")[0:cwm, 0:128]
            tt(TM2[0:cwm, 0:nch * 128], ptt[0:cwm, 0:nch * 128], wst_ap, ALU.mult, [ptk, CONST], [("TM2",)])
            ATs = []
            wtot = (nch - 1) * 64 + chs[-1][1]
            for hd in range(4 if full else 0):
                sc, sck = ps_next()
                for ci, (s0, cw) in enumerate(chs):
                    mm(sc[0:cw, ci * 64: ci * 64 + cw], krb[32 * hd:32 * hd + 32, s0:s0 + cw],
                       qrb[32 * hd:32 * hd + 32, s0:s0 + cw], True, True, [krbk, qrbk], [sck], tp=(32 * hd, 0))
                AT, atk = b_next()
                tt(AT[0:cwm, 0:wtot], sc[0:cwm, 0:wtot], cba("dmatT")[0:cwm, hd * 512: hd * 512 + wtot], ALU.mult,
                   [sck, CONST], [atk])
                ATs.append((AT, atk))
            kvp, kvk = ps_next()
            for ci, (s0, cw) in enumerate(chs):
                for hd in range(4):
                    mm(kvp[32 * hd:32 * hd + 32, ci * 64:(ci + 1) * 64],
                       TM2[0:cw, ci * 128 + 32 * hd: ci * 128 + 32 * hd + 32],
                       TM1[0:cw, ci * 256 + 64 * hd: ci * 256 + 64 * hd + 64], True, True,
                       [("TM2",), ("TM1",)], [kvk], tp=(0, 32 * hd))
            gdec = cfa("g6416")[:, 0:1] if W == 512 else cfa("g6416")[:, 1:2]
            XK = ("xo",)
            if full:
                cp(xin[:, 0:64], Rst[:], [RK], [XK], eng="dve")
            for ci, (s0, cw) in enumerate(chs):
                kv_ap = kvp[:, ci * 64:(ci + 1) * 64]
                if not full:
                    stt(Rst[:], Rst[:], gdec, kv_ap, ALU.mult, ALU.add, [RK, kvk, CONST], [RK])
                elif ci == nch - 1:
                    stt(Rst[:], xin[:, ci * 64:(ci + 1) * 64], gdec, kv_ap, ALU.mult, ALU.add, [XK, kvk, CONST], [RK])
                else:
                    stt(xin[:, (ci + 1) * 64:(ci + 2) * 64], xin[:, ci * 64:(ci + 1) * 64], gdec, kv_ap,
                        ALU.mult, ALU.add, [XK, kvk, CONST], [XK])
            if full:
                cp(SBk[:, 0:nch * 64], xin[:, 0:nch * 64], [XK], [("SBk",)], eng="act")
            if ti == 0:
                state_merge(Rst[:], RK, stin_ret[:], ("stin",), so)
            if not full:
                continue
            for j in range(2):
                op_, ok = ps_next()
                for hh2 in range(2):
                    hd = j * 2 + hh2
                    po = 64 * hh2
                    AT, atk = ATs[hd]
                    for ci, (s0, cw) in enumerate(chs):
                        mm(op_[po:po + 64, s0:s0 + cw], TM1[0:cw, ci * 256 + 64 * hd: ci * 256 + 64 * hd + 64],
                           AT[0:cw, ci * 64: ci * 64 + cw], True, False, [("TM1",), atk], [ok], tp=(0, po))
                        mm(op_[po:po + 64, s0:s0 + cw], SBk[32 * hd:32 * hd + 32, ci * 64:(ci + 1) * 64],
                           qw[32 * hd:32 * hd + 32, s0:s0 + cw], False, True, [("SBk",), qwk], [ok], tp=(32 * hd, po))
                if save:
                    ol, olk = f_next()
                    cp(ol[:, 0:W], op_[:, 0:W], [ok], [olk], eng="act")
                    dma("sp", rscr_o[j][:, t0:t0 + W], ol[:, 0:W], [olk], [("rscr", "o", j, ti)])
                    gs, gsk = gsl[j]
                    dma("sp", rscr_g[j][:, t0:t0 + W], gs[:, 0:W], [gsk], [("rscr", "g", j, ti)])
                    continue
                ob, obk = b_next()
                cp(ob[:, 0:W], op_[:, 0:W], [ok], [obk], eng="act")
                of, ofk = f_next()
                cp(of[:, 0:W], op_[:, 0:W], [ok], [ofk], eng="act")
                mup, muk = ps_next()
                mm(mup[:, 0:W], cba("bones64"), ob[:, 0:W], True, True, [obk, CONST], [muk])
                dd, ddk = f_next()
                tt(dd[:, 0:W], of[:, 0:W], mup[:, 0:W], ALU.subtract, [ofk, muk], [ddk])
                dsq, dsk = b_next()
                act(dsq[:, 0:W], dd[:, 0:W], AF.Square, [ddk], [dsk])
                vp2, vk2 = ps_next()
                mm(vp2[:, 0:W], cba("bones64"), dsq[:, 0:W], True, True, [dsk, CONST], [vk2])
                rstd, rk = f_next()
                act(rstd[:, 0:W], vp2[:, 0:W], AF.Ln, [vk2], [rk], bias=EPS)
                act(rstd[:, 0:W], rstd[:, 0:W], AF.Exp, [rk], [rk], scale=-0.5)
                t1, t1k = f_next()
                stt(t1[:, 0:W], dd[:, 0:W], gn, rstd[:, 0:W], ALU.mult, ALU.mult, [ddk, rk, CONST], [t1k])
                gs, gsk = gsl[j]
                tt(mg[:, 4 + j, c0:c0 + W], t1[:, 0:W], gs[:, 0:W], ALU.mult, [t1k, gsk], [("mg", 4 + j, ti)])

    def ret_main(l, su):
        P.label = "ret"
        gn = ppc(l, 21)
        for ti in SUPERS[su]:
            t0, W = TILES[ti]
            c0, _ = hn_cols(ti, su)
            qwg, qwgk = b_next()
            dma("sp", qwg[:, 0:W], rscr_q[:, t0:t0 + W], [("rscr", "q", ti)], [qwgk])
            st = []
            for j in range(2):
                of, ofk = f_next()
                dma("sp", of[:, 0:W], rscr_o[j][:, t0:t0 + W], [("rscr", "o", j, ti)], [ofk])
                gs, gsk = b_next()
                dma("sp", gs[:, 0:W], rscr_g[j][:, t0:t0 + W], [("rscr", "g", j, ti)], [gsk])
                st.append(dict(j=j, of=of, ofk=ofk, gs=gs, gsk=gsk))
            for d in st:
                d["cps"], d["ck"] = ps_next()
                for hh2 in range(2):
                    hd = d["j"] * 2 + hh2
                    po = 64 * hh2
                    mm(d["cps"][po:po + 64, 0:W], SBk[32 * hd:32 * hd + 32, 512:576], qwg[32 * hd:32 * hd + 32, 0:W],
                       True, True, [("SBk",), qwgk], [d["ck"]], tp=(32 * hd, po))
            for d in st:
                tt(d["of"][:, 0:W], d["of"][:, 0:W], d["cps"][:, 0:W], ALU.add, [d["ofk"], d["ck"]], [d["ofk"]])
            for d in st:
                d["ob"], d["obk"] = b_next()
                cp(d["ob"][:, 0:W], d["of"][:, 0:W], [d["ofk"]], [d["obk"]], eng="act")
            for d in st:
                d["mup"], d["muk"] = ps_next()
                mm(d["mup"][:, 0:W], cba("bones64"), d["ob"][:, 0:W], True, True, [d["obk"], CONST], [d["muk"]])
            for d in st:
                d["dd"], d["ddk"] = f_next()
                tt(d["dd"][:, 0:W], d["of"][:, 0:W], d["mup"][:, 0:W], ALU.subtract, [d["ofk"], d["muk"]], [d["ddk"]])
            for d in st:
                d["dsq"], d["dsk"] = b_next()
                act(d["dsq"][:, 0:W], d["dd"][:, 0:W], AF.Square, [d["ddk"]], [d["dsk"]])
            for d in st:
                d["vp2"], d["vk2"] = ps_next()
                mm(d["vp2"][:, 0:W], cba("bones64"), d["dsq"][:, 0:W], True, True, [d["dsk"], CONST], [d["vk2"]])
            for d in st:
                d["rstd"], d["rk"] = f_next()
                act(d["rstd"][:, 0:W], d["vp2"][:, 0:W], AF.Ln, [d["vk2"]], [d["rk"]], bias=EPS)
            for d in st:
                act(d["rstd"][:, 0:W], d["rstd"][:, 0:W], AF.Exp, [d["rk"]], [d["rk"]], scale=-0.5)
            for d in st:
                stt(d["dd"][:, 0:W], d["dd"][:, 0:W], gn, d["rstd"][:, 0:W], ALU.mult, ALU.mult,
                    [d["ddk"], d["rk"], CONST], [d["ddk"]])
            for d in st:
                tt(mg[:, 4 + d["j"], c0:c0 + W], d["dd"][:, 0:W], d["gs"][:, 0:W], ALU.mult, [d["ddk"], d["gsk"]],
                   [("mg", 4 + d["j"], ti)])

    def lru_group(l, su, Wk, so=False):
        P.label = "pre_lru" if so else "lru"
        for ti in SUPERS[su]:
            c0, W = hn_cols(ti, su)
            for j in range(2):
                LK = ("lxs", j)
                t0_ = TILES[ti][0]
                zx, zxk = proj_fm(Wk, j * 128, ti, su)
                if (not so) or fused:
                    zy, zyk = proj_fm(Wk, 256 + j * 128, ti, su)
                cp(lxs[j][:, 3:3 + W], zx[:, 0:W], [zxk], [LK], eng="act")
                xc, xck = f_next()
                ts(xc[:, 0:W], lxs[j][:, 0:W], ppc(l, 22 + j * 4 + 0), ppc(l, 30 + j), ALU.mult, ALU.add,
                   [LK, CONST], [xck])
                for tap in range(1, 4):
                    stt(xc[:, 0:W], lxs[j][:, tap:tap + W], ppc(l, 22 + j * 4 + tap), xc[:, 0:W], ALU.mult, ALU.add,
                        [LK, xck, CONST], [xck])
                cp(lxs[j][:, 0:3], lxs[j][:, W:W + 3], [LK], [LK], eng="dve")
                xcb, xcbk = b_next()
                cp(xcb[:, 0:W], xc[:, 0:W], [xck], [xcbk], eng="act")
                gap, gak = ps_next()
                mm(gap[:, 0:W], lrug[:, 0 * 2 + j, :], xcb[:, 0:W], True, True, [xcbk, ("lrug",)], [gak])
                gip, gik = ps_next()
                mm(gip[:, 0:W], lrug[:, 1 * 2 + j, :], xcb[:, 0:W], True, True, [xcbk, ("lrug",)], [gik])
                r, rk_ = f_next()
                act(r[:, 0:W], gap[:, 0:W], AF.Sigmoid, [gak, CONST], [rk_], bias=ppc(l, 32 + j))
                ii, iik = f_next()
                act(ii[:, 0:W], gip[:, 0:W], AF.Sigmoid, [gik, CONST], [iik], bias=ppc(l, 34 + j))
                a, ak = f_next()
                act(a[:, 0:W], r[:, 0:W], AF.Exp, [rk_, ("cng",)], [ak], scale=cng[:, l, j, 0:1])
                a2, a2k = f_next()
                act(a2[:, 0:W], r[:, 0:W], AF.Exp, [rk_, ("cng",)], [a2k], scale=cng[:, l, j, 1:2])
                th, thk = f_next()
                act(th[:, 0:W], r[:, 0:W], AF.Tanh, [rk_, ("cng",)], [thk], scale=cng[:, l, j, 0:1])
                stt(a2[:, 0:W], a2[:, 0:W], 1.0, th[:, 0:W], ALU.add, ALU.mult, [a2k, thk], [a2k])
                act(a2[:, 0:W], a2[:, 0:W], AF.Sqrt, [a2k], [a2k], scale=-1.0)
                tt(ii[:, 0:W], ii[:, 0:W], xc[:, 0:W], ALU.mult, [iik, xck], [iik])
                tt(ii[:, 0:W], ii[:, 0:W], a2[:, 0:W], ALU.mult, [iik, a2k], [iik])
                hs, hsk = f_next()
                P.add("dve", lambda e, hs=hs, a=a, ii=ii, W=W, j=j: e.tensor_tensor_scan(
                    out=hs[:, 0:W], data0=a[:, 0:W], data1=ii[:, 0:W], initial=hl[:, j:j + 1],
                    op0=ALU.mult, op1=ALU.add), [ak, iik, ("hl",)], [hsk])
                cp(hl[:, j:j + 1], hs[:, W - 1:W], [hsk], [("hl",)], eng="dve")
                if ti == 0:
                    state_merge(hl[:, j:j + 1], ("hl",), stin_lru[:, j:j + 1], ("stin",), so)
                if so:
                    if ti != 0:
                        P.add("dve", lambda e, r=r, W=W, j=j: e.reduce_sum(
                            out=rsum[:, 2 + j:3 + j], in_=r[:, 0:W], axis=mybir.AxisListType.X), [rk_], [("rsum",)])
                        tt(rsum[:, j:j + 1], rsum[:, j:j + 1], rsum[:, 2 + j:3 + j], ALU.add, [("rsum",)], [("rsum",)])
                    if fused:
                        ge, gek = f_next()
                        act(ge[:, 0:W], zy[:, 0:W], AF.Gelu_apprx_tanh, [zyk], [gek])
                        dma("sp", lscr[0 + j][:, t0_:t0_ + W], a[:, 0:W], [ak], [("lscr", 0, j, ti)])
                        dma("sp", lscr[2 + j][:, t0_:t0_ + W], ii[:, 0:W], [iik], [("lscr", 1, j, ti)])
                        dma("sp", lscr[4 + j][:, t0_:t0_ + W], ge[:, 0:W], [gek], [("lscr", 2, j, ti)])
                    continue
                ge, gek = f_next()
                act(ge[:, 0:W], zy[:, 0:W], AF.Gelu_apprx_tanh, [zyk], [gek])
                tt(mg[:, 6 + j, c0:c0 + W], ge[:, 0:W], hs[:, 0:W], ALU.mult, [gek, hsk], [("mg", 6 + j, ti)])

    def lru_main(l, su):
        P.label = "lru"
        for ti in SUPERS[su]:
            c0, W = hn_cols(ti, su)
            t0_ = TILES[ti][0]
            for j in range(2):
                la, lak = f_next()
                dma("sp", la[:, 0:W], lscr[0 + j][:, t0_:t0_ + W], [("lscr", 0, j, ti)], [lak])
                lu, luk = f_next()
                dma("sp", lu[:, 0:W], lscr[2 + j][:, t0_:t0_ + W], [("lscr", 1, j, ti)], [luk])
                lg_, lgk = f_next()
                dma("sp", lg_[:, 0:W], lscr[4 + j][:, t0_:t0_ + W], [("lscr", 2, j, ti)], [lgk])
                hs, hsk = f_next()
                P.add("dve", lambda e, hs=hs, la=la, lu=lu, W=W, j=j: e.tensor_tensor_scan(
                    out=hs[:, 0:W], data0=la[:, 0:W], data1=lu[:, 0:W], initial=hl[:, j:j + 1],
                    op0=ALU.mult, op1=ALU.add), [lak, luk, ("hl",)], [hsk])
                cp(hl[:, j:j + 1], hs[:, W - 1:W], [hsk], [("hl",)], eng="dve")
                if ti == 0:
                    state_merge(hl[:, j:j + 1], ("hl",), stin_lru[:, j:j + 1], ("stin",), False)
                tt(mg[:, 6 + j, c0:c0 + W], lg_[:, 0:W], hs[:, 0:W], ALU.mult, [lgk, hsk], [("mg", 6 + j, ti)])

    STAGES = []

    def stage(load, comp):
        STAGES.append((load, comp))

    def run_stages():
        n = len(STAGES)
        loaded = {}
        nxt = 0

        def issue_next_load(after):
            nonlocal nxt
            nxt = max(nxt, after)
            while nxt < n and STAGES[nxt][0] is None:
                nxt += 1
            if nxt < n:
                loaded[nxt] = STAGES[nxt][0]()
                nxt += 1
        issue_next_load(0)
        for i in range(n):
            ld, comp = STAGES[i]
            if ld is None:
                comp(None)
                continue
            if i not in loaded:
                loaded[i] = ld()
                nxt = max(nxt, i + 1)
            issue_next_load(i + 1)
            comp(loaded.pop(i))

    def wout_phase(l, su):
        stage(lambda: load_w(woutd[l], 0, 1024, wu_next(2)), lambda Wk: wout_compute(l, su, Wk))

    def wout_compute(l, su, Wk):
        P.label = "wout"
        for ti in (SUPERS[su] if su == 0 else SUPERS[su][::-1]):
            t0, W = TILES[ti]
            c0, _ = hn_cols(ti, su)
            for m in range(8):
                pst, pk = ps_next()
                for k in range(8):
                    wap, wk = Wk[k]
                    mm(pst[:, 0:W], wap[:, m * 128:(m + 1) * 128], mg[:, k, c0:c0 + W], k == 0, k == 7,
                       [wk, ("mg", k, ti)], [pk])
                tt(h[:, m, t0:t0 + W], h[:, m, t0:t0 + W], pst[:, 0:W], ALU.add, [hkey(m, ti), pk], [hkey(m, ti)])

    def ffn_phase(l, su):
        for (j0, j1) in FFN_GROUPS:
            for jb in range(j0, j1, 4):
                nb_ = min(4, j1 - jb)

                def ld(jb=jb, nb_=nb_):
                    Wg = load_w(wupd[l], jb * 128, nb_ * 128, wu_next(1))
                    Wv = load_w(wupd[l], DFF + jb * 128, nb_ * 128, wu_next(1))
                    return (Wg, Wv)
                stage(ld, lambda W2, jb=jb, nb_=nb_, j0=j0: ffn_up_block(l, su, j0, jb, nb_, W2))
            stage(lambda j0=j0, j1=j1: load_w(wdnd[l], 0, 1024, wu_next(2), row0=j0 * 128, nrows=(j1 - j0) * 128),
                  lambda Wd, j0=j0, j1=j1: ffn_down(l, su, j0, j1, Wd))

    def ffn_up_block(l, su, j0, jb, nb_, W2):
        P.label = "ffn"
        Wg, Wv = W2
        for jq in range(nb_):
            j = jb + jq
            jj = j - j0
            cg, cv = j, 22 + j
            cp(ubg[:, 0:2], uhalo[:, cg, :], [("uhalo",)], [("ubg",)], eng="dve")
            cp(ubv[:, 0:2], uhalo[:, cv, :], [("uhalo",)], [("ubv",)], eng="dve")
            for ti in SUPERS[su]:
                c0, W = hn_cols(ti, su)
                ugp, ugk = proj_fm(Wg, jq * 128, ti, su)
                uvp, uvk = proj_fm(Wv, jq * 128, ti, su)
                cp(ubg[:, 2:2 + W], ugp[:, 0:W], [ugk], [("ubg",)], eng="act")
                cp(ubv[:, 2:2 + W], uvp[:, 0:W], [uvk], [("ubv",)], eng="act")
                cgt, cgk = f_next()
                cvt, cvk = f_next()
                pairs = ((ubg, ("ubg",), cg, cgt, cgk), (ubv, ("ubv",), cv, cvt, cvk))
                for (ub, ubk, cc, c_, ck) in pairs:
                    ts(c_[:, 0:W], ub[:, 0:W], ppc(l, 38 + cc * 3 + 0), ppc(l, 170 + cc), ALU.mult, ALU.add,
                       [ubk, CONST], [ck])
                for tap in (1, 2):
                    for (ub, ubk, cc, c_, ck) in pairs:
                        stt(c_[:, 0:W], ub[:, tap:tap + W], ppc(l, 38 + cc * 3 + tap), c_[:, 0:W], ALU.mult, ALU.add,
                            [ubk, ck, CONST], [ck])
                for (ub, ubk, cc, c_, ck) in pairs:
                    cp(ub[:, 0:2], ub[:, W:W + 2], [ubk], [ubk], eng="dve")
                act(cgt[:, 0:W], cgt[:, 0:W], AF.Silu, [cgk], [cgk])
                tt(mg[:, jj, c0:c0 + W], cgt[:, 0:W], cvt[:, 0:W], ALU.mult, [cgk, cvk], [("mg", jj, ti)])
            cp(uhalo[:, cg, :], ubg[:, 0:2], [("ubg",)], [("uhalo",)], eng="dve")
            cp(uhalo[:, cv, :], ubv[:, 0:2], [("ubv",)], [("uhalo",)], eng="dve")

    def ffn_down(l, su, j0, j1, Wd):
        P.label = "ffn_down"
        nj = j1 - j0
        last = (su == 1 and j1 == NJ)
        for ti in (SUPERS[su][::-1] if last else SUPERS[su]):
            t0, W = TILES[ti]
            c0, _ = hn_cols(ti, su)
            for m in range(8):
                pst, pk = ps_next()
                for jj in range(nj):
                    wap, wk = Wd[jj]
                    mm(pst[:, 0:W], wap[:, m * 128:(m + 1) * 128], mg[:, jj, c0:c0 + W], jj == 0, jj == nj - 1,
                       [wk, ("mg", jj, ti)], [pk])
                tt(h[:, m, t0:t0 + W], h[:, m, t0:t0 + W], pst[:, 0:W], ALU.add, [hkey(m, ti), pk],
                   [hkey(m, ti)])

    def dump_dbg(src3, reads):
        dma("sp", dbgd[:], src3, reads, [("dbg",)])

    ALLST = [("Shg", i) for i in range(4)] + [("Dhg",), ("Rst",), ("hl",), ("alog",)]
    RG = [[0, 1, 2, 3], [4, 5, 6, 7]]

    def reset_states():
        P.add("dve", lambda e: e.memset(STT[:], 0.0), [], ALLST)
        P.add("dve", lambda e: e.memset(Dhg[:], 1.0), [], [("Dhg",)])
        P.add("dve", lambda e: e.memset(rsum[:], 0.0), [], [("rsum",)])

    def halo_exchange(l, sub, apply_now=True):
        P.label = "halo_x"
        dma("sp", xh_src.rearrange("p (k t) -> p k t", k=8), h[:, :, LT - 3:LT],
            [hkey(k, 4) for k in range(8)], [("xh_src",)])
        P.add("pool", lambda e: e.collective_compute("AllGather", ALU.bypass, replica_groups=RG,
                                                     ins=[xh_src[:]], outs=[xh_dst[:]]),
              [("xh_src",)], [("xh_dst",)], cc=True)
        dma("sp", hgat[:], xh_dst.rearrange("(j p) c -> p j c", p=128), [("xh_dst",)], [("hgat",)])
        if not apply_now:
            return
        halo_apply()

    def halo_apply():
        P.label = "halo_x"
        ht2 = halo_t[:].rearrange("p k t -> p (k t)")
        ts(ht2, hgat[:, 0, :], msk[:, 2:3], None, ALU.mult, None, [("hgat",), CONST], [("halo_t",)])
        for j in range(1, 4):
            stt(ht2, hgat[:, j, :], msk[:, 2 + j:3 + j], ht2, ALU.mult, ALU.add, [("hgat",), ("halo_t",), CONST],
                [("halo_t",)])
        stt(h[:, :, 13:16], h[:, :, 13:16], msk[:, 0:1], halo_t[:], ALU.mult, ALU.add,
            [("halo_t",), CONST] + [hkey(k, 0) for k in range(8)], [hkey(k, 0) for k in range(8)])

    def state_exchange(l):
        P.label = "state_x"
        tt(alog[:], rsum[:, 0:2], cng[:, l, :, 0], ALU.mult, [("rsum",), ("cng",)], [("alog",)])
        dma("sp", st_src[:], STT[:], ALLST, [("st_src",)])
        P.add("pool", lambda e: e.collective_compute("AllGather", ALU.bypass, replica_groups=RG,
                                                     ins=[st_src[:]], outs=[st_dst[:]]),
              [("st_src",)], [("st_dst",)], cc=True)
        SI = ("stin",)
        P.add("dve", lambda e: e.memset(STIN[:], 0.0), [], [SI])
        G = xin[:, 0:NST]
        GK = ("xo",)
        for j in range(3):
            dma("sp", G, st_dst[j * 128:(j + 1) * 128, :], [("st_dst",)], [GK])
            fs = msk[:, 6 + j:7 + j]
            for hh in range(4):
                cs = slice(hh * 128, (hh + 1) * 128)
                stt(G[:, cs], STIN[:, cs], G[:, 512 + hh:513 + hh], G[:, cs], ALU.mult, ALU.add, [SI, GK], [GK])
            stt(G[:, 516:580], STIN[:, 516:580], cfa("g2048"), G[:, 516:580], ALU.mult, ALU.add, [SI, GK, CONST], [GK])
            act(G[:, 582:584], G[:, 582:584], AF.Exp, [GK], [GK])
            tt(G[:, 582:584], G[:, 582:584], STIN[:, 580:582], ALU.mult, [GK, SI], [GK])
            tt(G[:, 580:582], G[:, 580:582], G[:, 582:584], ALU.add, [GK], [GK])
            tt(G[:, 0:582], G[:, 0:582], STIN[:, 0:582], ALU.subtract, [GK, SI], [GK])
            stt(STIN[:, 0:582], G[:, 0:582], fs, STIN[:, 0:582], ALU.mult, ALU.add, [GK, SI, CONST], [SI])
        cp(SBk[:, 0:512], STIN[:, 0:512], [SI], [("SBk",)], eng="act")
        cp(SBk[:, 512:576], STIN[:, 516:580], [SI], [("SBk",)], eng="act")

    def mixer_pass(l, so):
        def prologue(_):
            for j in range(2):
                P.add("dve", lambda e, j=j: e.memset(lxs[j][:, 0:3], 0.0), [], [("lxs", j)])
        stage(None, prologue)
        for su in range(2):
            if fused and not so:
                stage(None, lambda _, su=su: hg_main(l, su))
                stage(None, lambda _, su=su: ret_main(l, su))
                stage(None, lambda _, su=su: lru_main(l, su))
            else:
                stage(None, lambda _, su=su: norm_phase(lambda k: ppc(l, k), su))
                for hh in range(4):
                    stage(lambda hh=hh: load_w(winx[l], hh * 512, 512, wu_next(1)),
                          lambda Wk, hh=hh, su=su: hg_group(l, hh, su, Wk, so))
                stage(lambda: load_w(winx[l], 2048, 1024, wu_next(2)), lambda Wk, su=su: ret_group(l, su, Wk, so))
                if fused and so and su == 0:
                    stage(None, lambda _: (halo_apply(), norm_phase(lambda k: ppc(l, k), 0, tiles=[0])))
                stage(lambda: load_w(winx[l], 3072, 512, wu_next(1)), lambda Wk, su=su: lru_group(l, su, Wk, so))
            if so:
                continue
            if dbg == ("mg", l) and su == 0:
                stage(None, lambda _: dma("pool", dbgd[:, :, 0:1040], mg[:],
                                          [("mg", k, ti) for k in range(8) for ti in range(3)], [("dbg",)]))
            wout_phase(l, su)

    stage(None, lambda _: reset_states())
    for l in range(nlayers):
        stage(None, lambda _, l=l: dma("pool", lrug[:], lrugd[l * 4:(l + 1) * 4].rearrange("g p c -> p g c"), [],
                                       [("lrug",)]))
        if fused:
            stage(None, lambda _, l=l: halo_exchange(l, 0, apply_now=False))
            mixer_pass(l, True)
            stage(None, lambda _, l=l: (state_exchange(l), reset_states()))
        else:
            def pre(_, l=l):
                dma("sp", STIN[:], st_in[l], [], [("stin",)])
                halo_dump(l, 0)
                halo_inject(l, 0)
            stage(None, pre)
        mixer_pass(l, False)

        def mid(_, l=l):
            if dbg == ("hmid", l):
                dump_dbg(h[:], [hkey(k, ti) for k in range(8) for ti in range(5)])
            if not fused:
                dma("sp", st_out[l], STT[:], ALLST, [("st_out", l)])
            reset_states()
            if fused:
                halo_exchange(l, 1, apply_now=False)
            else:
                halo_dump(l, 1)
                halo_inject(l, 1)
            P.add("dve", lambda e: e.memset(uhalo[:], 0.0), [], [("uhalo",)])
        stage(None, mid)
        for su in range(2):
            if fused and su == 0:
                def ffn_norm0(_, l=l):
                    norm_phase(lambda k: ppc(l, 8 + k), 0, tiles=[1, 2])
                    halo_apply()
                    norm_phase(lambda k: ppc(l, 8 + k), 0, tiles=[0])
                stage(None, ffn_norm0)
            else:
                stage(None, lambda _, l=l, su=su: norm_phase(lambda k: ppc(l, 8 + k), su))
            ffn_phase(l, su)
        if dbg == ("h", l):
            stage(None, lambda _: dump_dbg(h[:], [hkey(k, ti) for k in range(8) for ti in range(5)]))
    run_stages()

    fg = lambda k: pp[:, NL * PL + k: NL * PL + k + 1]
    for ti in range(1, 5):
        t0, W = TILES[ti]
        pst, pk = ps_next()
        for k in range(8):
            act(sq[:, k % 4, 0:W], h[:, k, t0:t0 + W], AF.Square, [hkey(k, ti)], [("sq", k % 4)])
            mm(pst[:, 0:W], cba("ones1024"), sq[:, k % 4, 0:W], k == 0, k == 7, [("sq", k % 4), CONST], [pk])
        rstd, rk = f_next()
        act(rstd[:, 0:W], pst[:, 0:W], AF.Ln, [pk], [rk], bias=EPS)
        act(rstd[:, 0:W], rstd[:, 0:W], AF.Exp, [rk], [rk], scale=-0.5)
        for k in range(8):
            stt(h[:, k, t0:t0 + W], h[:, k, t0:t0 + W], fg(k), rstd[:, 0:W], ALU.mult, ALU.mult,
                [hkey(k, ti), rk, CONST], [hkey(k, ti)])
        for tb in range(4):
            tt0 = t0 + tb * 128
            for kg in range(2):
                pst2, pk2 = ps_next()
                for kk in range(4):
                    k = kg * 4 + kk
                    tr(pst2[:, kk * 128:(kk + 1) * 128], h[:, k, tt0:tt0 + 128], ident, [hkey(k, ti), CONST], [pk2])
                ob_, obk_ = f_next()
                cp(ob_[:, :], pst2[:, :], [pk2], [obk_], eng="act" if kg else "dve")
                dma("sp", outd[tt0 - PRE: tt0 - PRE + 128, kg * 512:(kg + 1) * 512], ob_[:, :], [obk_],
                    [("out", ti, tb, kg)])
    fin_reads = [("out", ti, tb, kg) for ti in range(1, 5) for tb in range(4) for kg in range(2)]
    if not fused:
        fin_reads += [("halo_out", i) for i in range(2 * nlayers)]
        fin_reads += [("st_out", l) for l in range(nlayers)]
    if dbg is not None:
        fin_reads.append(("dbg",))
    P.add("sp", None, fin_reads, [])
    nops, nwait = P.emit()
    _CACHE["last_ops"] = [(o["eng"], o["label"]) for o in P.ops if o["fn"] is not None]
    return nc, nops, nwait


FUSED = True


def _get_program(fused):
    key = ("nc", fused)
    if key not in _CACHE:
        _CACHE[key] = build_program(fused=fused)[0]
    return _CACHE[key]


def prepare_static(inp):
    cfd, cbd = make_consts()
    cstf = np.concatenate([cfd[n] for n, _ in CF_ORDER], axis=1).astype(np.float32)
    cstb = np.concatenate([cbd[n] for n, _ in CB_ORDER], axis=1).astype(np.float32)
    perm = winx_perm()
    st = {
        "pp": pack_params(inp),
        "cstf": np.ascontiguousarray(cstf),
        "cstb": np.ascontiguousarray(cstb),
        "w_in_x": np.ascontiguousarray(inp["w_in"][:, :, perm]),
        "w_out": np.ascontiguousarray(inp["w_out"]),
        "w_up": np.ascontiguousarray(inp["ffn_w_up"]),
        "w_down": np.ascontiguousarray(inp["ffn_w_down"]),
        "lrug": np.ascontiguousarray(pack_lrug(inp).reshape(NL * 4, 128, 128)),
    }
    return st


def seg_inputs(inp, st, b, s, states=None):
    x = inp["x"]
    xs = np.zeros((LT, D), np.float32)
    if s == 0:
        xs[:PRE] = inp["meta_tokens"]
    xs[PRE:] = x[b, s * NR:(s + 1) * NR]
    pos = np.arange(LT, dtype=np.float32) + s * NR
    m = np.zeros((128, 9), np.float32)
    m[:, 0] = 1.0 if s == 0 else 0.0
    m[:, 1] = 0.0 if s == 0 else 1.0
    if s > 0:
        m[:, 2 + (s - 1)] = 1.0
    for j in range(3):
        m[:, 6 + j] = 1.0 if j < s else 0.0
    d = dict(st)
    d["xs"] = xs
    d["rope"] = rope_tables(pos)
    d["wqseg"] = _WQSEG
    d["msk"] = m
    if states is not None:
        d.update(states)
    return d


def zero_states():
    return {
        "st_in": np.zeros((NL, 128, 584), np.float32),
        "halo_in": np.zeros((NL * 2, 128, 24), np.float32),
    }


def kernel(**inputs):
    inp = {k: np.asarray(v) for k, v in inputs.items()}
    st = prepare_static(inp)
    out = np.zeros((2, NSEG * NR, D), np.float32)
    if FUSED:
        nc = _get_program(True)
        in_maps = [seg_inputs(inp, st, r // NSEG, r % NSEG) for r in range(8)]
        res = run_bass_kernel_spmd(nc, in_maps, core_ids=list(range(8)))
        for r in range(8):
            b, s = r // NSEG, r % NSEG
            out[b, s * NR:(s + 1) * NR] = res.results[r]["out"]
        return out
    nc = _get_program(False)
    states = [zero_states(), zero_states()]
    for s in range(NSEG):
        in_maps = [seg_inputs(inp, st, b, s, states[b]) for b in range(2)]
        res = run_bass_kernel_spmd(nc, in_maps, core_ids=[0, 1])
        for b in range(2):
            r = res.results[b]
            out[b, s * NR:(s + 1) * NR] = r["out"]
            states[b] = {"st_in": r["st_out"], "halo_in": r["halo_out"]}
    return out
```
